# Optimizing a Trainium2 kernel written in Bass

```python
import functools
import jax, jax.numpy as jnp
from jax import lax
import numpy as np

D_MODEL = 2048
BATCH = 2
SEQ = 8192
DEPTH = 1
DEC_BATCH = 8
DEC_SEQ = 32
PAST_LEN = 2048

CHUNK = 64
BAND_CHUNKS = 8
BAND = BAND_CHUNKS * CHUNK
HEAD_DIM = 128
N_HEADS_A = 8
N_HEADS_B = 8
WIDTH_A = N_HEADS_A * HEAD_DIM
WIDTH_B = N_HEADS_B * HEAD_DIM
REL_CLIP = 128
SB_BLOCK = 128
D_FF = -(-8 * D_MODEL // (3 * 256)) * 256
IN_COLS = 3 * WIDTH_A + 3 * WIDTH_B + 2 * D_MODEL
EPS = 1e-6
ATTN_SCALE = HEAD_DIM ** -0.5

kernel_name = 'chunk_band_stickbreak_hybrid_step'


def rms_norm(x, g):
    xf = x.astype(jnp.float32)
    y = xf * lax.rsqrt(jnp.mean(xf * xf, axis=-1, keepdims=True) + EPS)
    return (y * g.astype(jnp.float32)).astype(x.dtype)


def ada_mod(c, w_ada, b_ada):
    m = jax.nn.silu(c) @ w_ada + b_ada
    return jnp.split(m[:, None, :], 6, axis=-1)


def split_heads(t, n_heads):
    return t.reshape(t.shape[:-1] + (n_heads, HEAD_DIM))


def mixer_proj(h, w_in):
    p = h @ w_in
    offs = [WIDTH_A, 2 * WIDTH_A, 3 * WIDTH_A, 3 * WIDTH_A + WIDTH_B,
            3 * WIDTH_A + 2 * WIDTH_B, 3 * WIDTH_A + 3 * WIDTH_B,
            3 * WIDTH_A + 3 * WIDTH_B + D_MODEL]
    qa, ka, va, qb, kb, vb, ga, gb = jnp.split(p, offs, axis=-1)
    return (split_heads(qa, N_HEADS_A), split_heads(ka, N_HEADS_A), split_heads(va, N_HEADS_A),
            split_heads(qb, N_HEADS_B), split_heads(kb, N_HEADS_B), split_heads(vb, N_HEADS_B),
            ga, gb)


def band_attn(q, k, v, q_pos, k_pos, rel_bias):
    s = jnp.einsum('bqhd,bkhd->bhqk', q, k, preferred_element_type=jnp.float32) * ATTN_SCALE
    rel = jnp.clip(q_pos[:, None] - k_pos[None, :], -REL_CLIP, REL_CLIP) + REL_CLIP
    s = s + rel_bias.astype(jnp.float32)[:, rel][None]
    s = jnp.where((k_pos >= 0)[None, None, None, :], s, -jnp.inf)
    p = jax.nn.softmax(s, axis=-1)
    return jnp.einsum('bhqk,bkhd->bqhd', p.astype(v.dtype), v)


def chunk_attn_prompt(q, k, v, rel_bias):
    b, s_len, h, d = q.shape
    kp = jnp.pad(k, ((0, 0), (BAND, 0), (0, 0), (0, 0)))
    vp = jnp.pad(v, ((0, 0), (BAND, 0), (0, 0), (0, 0)))

    def one_chunk(ci):
        start = ci * CHUNK
        qc = lax.dynamic_slice_in_dim(q, start, CHUNK, axis=1)
        kc = lax.dynamic_slice_in_dim(kp, start, BAND + CHUNK, axis=1)
        vc = lax.dynamic_slice_in_dim(vp, start, BAND + CHUNK, axis=1)
        q_pos = start + jnp.arange(CHUNK)
        k_pos = start - BAND + jnp.arange(BAND + CHUNK)
        return band_attn(qc, kc, vc, q_pos, k_pos, rel_bias)

    o = lax.map(one_chunk, jnp.arange(s_len // CHUNK))
    return jnp.moveaxis(o, 0, 1).reshape(b, s_len, h, d)


def chunk_attn_sample(q, k, v, cache_k, cache_v, rel_bias):
    n = q.shape[1]
    lc = cache_k.shape[1]
    kk = jnp.concatenate([cache_k.astype(k.dtype), k], axis=1)
    vv = jnp.concatenate([cache_v.astype(v.dtype), v], axis=1)
    q_pos = PAST_LEN + jnp.arange(n)
    k_pos = PAST_LEN - lc + jnp.arange(lc + n)
    return band_attn(q, kk, vv, q_pos, k_pos, rel_bias)


def stick_block(q, k, v, q_pos, k_pos):
    z = jnp.einsum('bqhd,bkhd->bhqk', q, k, preferred_element_type=jnp.float32) * ATTN_SCALE
    causal = (k_pos[None, :] < q_pos[:, None])[None, None]
    log_1m = jnp.where(causal, jax.nn.log_sigmoid(-z), 0.0)
    later = lax.cumsum(log_1m, axis=3, reverse=True) - log_1m
    a = jnp.where(causal, jnp.exp(jax.nn.log_sigmoid(z) + later), 0.0)
    return jnp.einsum('bhqk,bkhd->bqhd', a.astype(v.dtype), v)


def stick_prompt(q, k, v):
    b, s_len, h, d = q.shape
    k_pos = jnp.arange(s_len)

    def one_block(bi):
        start = bi * SB_BLOCK
        qb = lax.dynamic_slice_in_dim(q, start, SB_BLOCK, axis=1)
        return stick_block(qb, k, v, start + jnp.arange(SB_BLOCK), k_pos)

    o = lax.map(one_block, jnp.arange(s_len // SB_BLOCK))
    return jnp.moveaxis(o, 0, 1).reshape(b, s_len, h, d)


def stick_sample(q, k, v, cache_k, cache_v):
    n = q.shape[1]
    lc = cache_k.shape[1]
    kk = jnp.concatenate([cache_k.astype(k.dtype), k], axis=1)
    vv = jnp.concatenate([cache_v.astype(v.dtype), v], axis=1)
    return stick_block(q, kk, vv, PAST_LEN + jnp.arange(n), jnp.arange(lc + n))


def swiglu(h, w_gate_up, w_down):
    g, u = jnp.split(h @ w_gate_up, 2, axis=-1)
    return (jax.nn.silu(g) * u) @ w_down


def block(x, c, mix_a, mix_b, w_ada, b_ada, g_mix, w_in, w_a_out, w_b_out, w_o,
          g_ffn, w_gate_up, w_down):
    sh1, sc1, gt1, sh2, sc2, gt2 = ada_mod(c, w_ada, b_ada)
    h = rms_norm(x, g_mix) * (1.0 + sc1) + sh1
    qa, ka, va, qb, kb, vb, ga, gb = mixer_proj(h, w_in)
    oa = mix_a(qa, ka, va)
    ob = mix_b(qb, kb, vb)
    ya = oa.reshape(oa.shape[:2] + (WIDTH_A,)) @ w_a_out
    yb = ob.reshape(ob.shape[:2] + (WIDTH_B,)) @ w_b_out
    merged = (jax.nn.sigmoid(ga) * ya + jax.nn.sigmoid(gb) * yb) @ w_o
    x = x + gt1 * merged
    h = rms_norm(x, g_ffn) * (1.0 + sc2) + sh2
    x = x + gt2 * swiglu(h, w_gate_up, w_down)
    return x, ka, va, kb, vb


def setup_inputs(seed: int = 0) -> dict:
    key = jax.random.key(seed)
    ks = jax.random.split(key, 24)
    f32 = jnp.float32

    def nrm(k, shape, scale):
        return jax.random.normal(k, shape, f32) * scale

    la = min(BAND, PAST_LEN)
    return {
        'x_prompt': nrm(ks[0], (BATCH, SEQ, D_MODEL), 1.0),
        'x_sample': nrm(ks[1], (DEC_BATCH, DEC_SEQ, D_MODEL), 1.0),
        'c_prompt': nrm(ks[2], (BATCH, D_MODEL), 1.0),
        'c_sample': nrm(ks[3], (DEC_BATCH, D_MODEL), 1.0),
        'cache_a_k': nrm(ks[4], (DEPTH, DEC_BATCH, la, N_HEADS_A, HEAD_DIM), 1.0),
        'cache_a_v': nrm(ks[5], (DEPTH, DEC_BATCH, la, N_HEADS_A, HEAD_DIM), 1.0),
        'cache_b_k': nrm(ks[6], (DEPTH, DEC_BATCH, PAST_LEN, N_HEADS_B, HEAD_DIM), 1.0),
        'cache_b_v': nrm(ks[7], (DEPTH, DEC_BATCH, PAST_LEN, N_HEADS_B, HEAD_DIM), 1.0),
        'w_ada': nrm(ks[8], (DEPTH, D_MODEL, 6 * D_MODEL), 0.5 * D_MODEL ** -0.5),
        'b_ada': nrm(ks[9], (DEPTH, 6 * D_MODEL), 0.01),
        'g_mix': 1.0 + nrm(ks[10], (DEPTH, D_MODEL), 0.02),
        'w_in': nrm(ks[11], (DEPTH, D_MODEL, IN_COLS), D_MODEL ** -0.5),
        'rel_bias': nrm(ks[12], (DEPTH, N_HEADS_A, 2 * REL_CLIP + 1), 0.5),
        'w_a_out': nrm(ks[13], (DEPTH, WIDTH_A, D_MODEL), WIDTH_A ** -0.5),
        'w_b_out': nrm(ks[14], (DEPTH, WIDTH_B, D_MODEL), WIDTH_B ** -0.5),
        'w_o': nrm(ks[15], (DEPTH, D_MODEL, D_MODEL), D_MODEL ** -0.5),
        'g_ffn': 1.0 + nrm(ks[16], (DEPTH, D_MODEL), 0.02),
        'w_gate_up': nrm(ks[17], (DEPTH, D_MODEL, 2 * D_FF), D_MODEL ** -0.5),
        'w_down': nrm(ks[18], (DEPTH, D_FF, D_MODEL), D_FF ** -0.5),
        'g_final': 1.0 + nrm(ks[19], (D_MODEL,), 0.02),
    }


def reference(x_prompt, x_sample, c_prompt, c_sample, cache_a_k, cache_a_v, cache_b_k, cache_b_v,
              w_ada, b_ada, g_mix, w_in, rel_bias, w_a_out, w_b_out, w_o, g_ffn, w_gate_up,
              w_down, g_final):
    xp, xs = x_prompt, x_sample
    ak_p, av_p, bk_p, bv_p = [], [], [], []
    ak_s, av_s, bk_s, bv_s = [], [], [], []
    for l in range(DEPTH):
        lw = (w_ada[l], b_ada[l], g_mix[l], w_in[l], w_a_out[l], w_b_out[l], w_o[l],
              g_ffn[l], w_gate_up[l], w_down[l])
        xp, ka, va, kb, vb = block(
            xp, c_prompt, functools.partial(chunk_attn_prompt, rel_bias=rel_bias[l]),
            stick_prompt, *lw)
        keep = min(BAND, ka.shape[1])
        ak_p.append(ka[:, -keep:]); av_p.append(va[:, -keep:])
        bk_p.append(kb); bv_p.append(vb)
        xs, ka, va, kb, vb = block(
            xs, c_sample,
            functools.partial(chunk_attn_sample, cache_k=cache_a_k[l], cache_v=cache_a_v[l],
                              rel_bias=rel_bias[l]),
            functools.partial(stick_sample, cache_k=cache_b_k[l], cache_v=cache_b_v[l]),
            *lw)
        ak_s.append(ka); av_s.append(va); bk_s.append(kb); bv_s.append(vb)
    y_prompt = rms_norm(xp, g_final)
    y_sample = rms_norm(xs, g_final)
    return (y_prompt, y_sample,
            jnp.stack(ak_p), jnp.stack(av_p), jnp.stack(bk_p), jnp.stack(bv_p),
            jnp.stack(ak_s), jnp.stack(av_s), jnp.stack(bk_s), jnp.stack(bv_s))
```

```python
import numpy as np
from contextlib import ExitStack
import concourse.bass as bass
import concourse.mybir as mybir
from concourse.bass_utils import run_bass_kernel_spmd

F32 = mybir.dt.float32
BF16 = mybir.dt.bfloat16
AF = mybir.ActivationFunctionType
ALU = mybir.AluOpType
AX = mybir.AxisListType

D = 2048
KC = 16
NH = 8
HD = 128
DFF = 5632
FC = 44
NOWN = 2080
NS = 32
SCALE = HD ** -0.5
EPS = 1e-6
BIG = 30000.0
NBLK = 20
NAKEY = 3104
NBKEY = 9248
LT = 767


class Sem:
    def __init__(self, h):
        self.h = h
        self.count = 0


class Buf:
    __slots__ = ("name", "w", "r", "excl")

    def __init__(self, name="", excl=False):
        self.name = name
        self.w = None
        self.r = {}
        self.excl = excl


class Sched:
    ENG = ("pe", "act", "dve", "pool", "sp")

    def __init__(self, nc, es):
        self.nc = nc
        self.es = es
        self.q = {e: [] for e in self.ENG}
        self.esem = {e: Sem(es.enter_context(nc.semaphore("s_" + e))) for e in self.ENG}
        self.seen = {e: {} for e in self.ENG}
        self.dsems = []
        self.nsem = 0

    def newsem(self):
        self.nsem += 1
        s = Sem(self.es.enter_context(self.nc.semaphore("d%d" % self.nsem)))
        self.dsems.append(s)
        return s

    def _waits(self, eng, reads, writes):
        need = {}

        def add(s, n):
            if need.get(s, 0) < n:
                need[s] = n
        for b in reads:
            if b.w is not None:
                add(*b.w)
        for b in writes:
            if b.w is not None:
                add(*b.w)
            for s, n in b.r.items():
                add(s, n)
        out = []
        seen = self.seen[eng]
        for s, n in need.items():
            if eng == "pe" and s is self.esem["pe"]:
                continue
            if seen.get(s, 0) >= n:
                continue
            seen[s] = n
            out.append((s, n))
        return out

    def _commit(self, t, reads, writes):
        s, n = t
        for b in reads:
            if b.r.get(s, 0) < n:
                b.r[s] = n
        for b in writes:
            b.w = t
            b.r = {}

    def op(self, eng, fn, reads=(), writes=()):
        if any(b.excl for b in reads):
            writes = list(writes) + [b for b in reads if b.excl and b not in writes]
            reads = [b for b in reads if not b.excl]
        waits = self._waits(eng, reads, writes)
        s = self.esem[eng]
        s.count += 1
        t = (s, s.count)
        self.q[eng].append((waits, fn, s, 1))
        self._commit(t, reads, writes)
        return t

    def dma(self, eng, sem, pairs, reads=(), writes=(), transpose=False):
        waits = self._waits(eng, reads, writes)
        if sem.count > 0 and self.seen[eng].get(sem, 0) < sem.count:
            self.seen[eng][sem] = sem.count
            waits = waits + [(sem, sem.count)]
        sem.count += 16 * len(pairs)
        t = (sem, sem.count)
        for i, (o, i_) in enumerate(pairs):
            if transpose:
                self.q[eng].append((waits if i == 0 else [], (lambda e, o=o, i_=i_: e.dma_start_transpose(out=o, in_=i_)), sem, 16))
            else:
                self.q[eng].append((waits if i == 0 else [], (lambda e, o=o, i_=i_: e.dma_start(out=o, in_=i_)), sem, 16))
        self._commit(t, reads, writes)
        return t

    def barrier(self):
        allw = [(s, s.count) for s in list(self.esem.values()) + self.dsems if s.count > 0]
        for e in self.ENG:
            seen = self.seen[e]
            w = []
            for s, n in allw:
                if seen.get(s, 0) < n:
                    seen[s] = n
                    w.append((s, n))
            if w:
                self.q[e].append((w, None, None, 0))

    def replay(self, name, e):
        for waits, fn, s, inc in self.q[name]:
            for ws, n in waits:
                e.wait_ge(ws.h, n)
            if fn is not None:
                fn(e).then_inc(s.h, inc)


def build_program(STOP=99):
    nc = bass.Bass("TRN2", target_bir_lowering=False)

    def din(name, shape, dt=F32):
        return nc.dram_tensor(name, list(shape), dt, kind="ExternalInput").ap()

    def dout(name, shape):
        return nc.dram_tensor(name, list(shape), F32, kind="ExternalOutput").ap()

    def dscr(name, shape, dt):
        return nc.dram_tensor(name, list(shape), dt, kind="Internal").ap()

    xin = din("xin", [NBLK, 512, D])
    xs_in = din("xs", [NS, D])
    cvec = din("cvec", [128, KC, 2])
    gcols = din("gcols", [128, 2, KC])
    grow = din("grow", [2, D])
    gfin = din("gfin", [1, D])
    flags_in = din("flags", [128, 16])
    w_ada = din("w_ada", [D, 6 * D])
    b_ada = din("b_ada", [1, 6 * D])
    w_in = din("w_in", [D, 10240])
    relb = din("relb", [NH, 257])
    w_a_out = din("w_a_out", [1024, D])
    w_b_out = din("w_b_out", [1024, D])
    w_o = din("w_o", [D, D])
    w_gu = din("w_gu", [D, 2 * DFF])
    w_dn = din("w_dn", [DFF, D])
    ca_k = din("ca_k", [512, 1024])
    ca_v = din("ca_v", [512, 1024])
    cb_k = din("cb_k", [2048, 1024])
    cb_v = din("cb_v", [2048, 1024])

    y_o = dout("y", [NOWN, D])
    ka_o = dout("ka_o", [544, 1024])
    va_o = dout("va_o", [544, 1024])
    kb_o = dout("kb_o", [NOWN, 1024])
    vb_o = dout("vb_o", [NOWN, 1024])

    HT = dscr("HT", [NBLK + 1, 128, KC, 512], BF16)
    adaD = dscr("adaD", [2, 6 * D], F32)
    GT = dscr("GT", [NH, 128, LT], F32)
    gD = dscr("gD", [NH, LT], F32)
    QaT = dscr("QaT", [NH, 128, NOWN], BF16)
    QbT = dscr("QbT", [NH, 128, NOWN], BF16)
    KaT = dscr("KaT", [NH, 128, NAKEY], BF16)
    KbT = dscr("KbT", [NH, 128, NBKEY], BF16)
    VaS = dscr("VaS", [25, 128, 1024], BF16)
    VbS = dscr("VbS", [73, 128, 1024], BF16)
    GA = dscr("GA", [KC, 128, NOWN], F32)
    GB = dscr("GB", [KC, 128, NOWN], F32)
    OaT = dscr("OaT", [NH, 128, NOWN], BF16)
    ObT = dscr("ObT", [NH, 128, NOWN], BF16)
    MT = dscr("MT", [5, 128, KC, 512], BF16)
    X1 = dscr("X1", [17, 128, D], F32)
    H2T = dscr("H2T", [5, 128, KC, 512], BF16)
    ActT = dscr("ActT", [9, 128, FC, 256], BF16)
    X2 = dscr("X2", [17, 128, D], F32)

    es = ExitStack()
    with es:
        S = Sched(nc, es)
        ARW = 45056
        arena = es.enter_context(nc.sbuf_tensor("arena", [128, ARW], F32))
        identf = es.enter_context(nc.sbuf_tensor("identf", [128, 128], F32))
        identb = es.enter_context(nc.sbuf_tensor("identb", [128, 128], BF16))
        maskL = es.enter_context(nc.sbuf_tensor("maskL", [128, 128], F32))
        Tb = es.enter_context(nc.sbuf_tensor("Tb", [128, NH, 640], F32))
        modc = es.enter_context(nc.sbuf_tensor("modc", [128, 2, 4, KC], F32))
        flg = es.enter_context(nc.sbuf_tensor("flg", [128, 16], F32))
        small = es.enter_context(nc.sbuf_tensor("small", [128, 64], F32))
        PS = [es.enter_context(nc.psum_tensor("ps%d" % i, [128, 512], F32)) for i in range(8)]
        PSB = [Buf("ps%d" % i, excl=True) for i in range(8)]
        B_const = Buf("const")

        apos = [0]

        def areset():
            apos[0] = 0

        def af32(n):
            o = apos[0]
            apos[0] += n
            assert apos[0] <= ARW, apos[0]
            return arena[:, o:o + n]

        def abf(n):
            assert n % 2 == 0
            return af32(n // 2).bitcast(BF16)

        def wview(w, c0, nc_, kc=KC):
            return w.rearrange("(kc p) c -> p kc c", p=128)[:, :, c0:c0 + nc_]

        _ph = [0]
        for _once in (0,):
            sem_c = S.newsem()
            S.op("pool", lambda e: e.memset(identf[:], 0.0), writes=[B_const])
            S.op("pool", lambda e: e.affine_select(out=identf[:], in_=identf[:], pattern=[[-1, 128]],
                                                   compare_op=ALU.not_equal, fill=1.0, base=0,
                                                   channel_multiplier=1), writes=[B_const])
            S.op("pool", lambda e: e.tensor_copy(out=identb[:], in_=identf[:]), reads=[B_const], writes=[B_const])
            S.op("pool", lambda e: e.memset(maskL[:], 0.0), writes=[B_const])
            S.op("pool", lambda e: e.affine_select(out=maskL[:], in_=maskL[:], pattern=[[1, 128]], compare_op=ALU.is_gt,
                                                   fill=1.0, base=0, channel_multiplier=-1), writes=[B_const])
            S.op("pool", lambda e: e.memset(small[:, 63:64], EPS), writes=[B_const])
            S.dma("sp", sem_c, [(flg[:], flags_in[:, :])], writes=[B_const])

            B_g = Buf("g")
            B_GT = Buf("GT")
            gs = af32(LT)
            S.dma("sp", sem_c, [(gs[0:NH, 0:256], relb[:, 1:257])], writes=[B_g])
            S.op("dve", lambda e: e.tensor_copy(out=gs[0:NH, 256:LT], in_=gs[0:NH, 255:256].to_broadcast([NH, LT - 256])),
                 reads=[B_g], writes=[B_g])
            sem_g = S.newsem()
            B_gD = Buf("gD")
            S.dma("sp", sem_g, [(gD[:, :], gs[0:NH, :])], reads=[B_g], writes=[B_gD])
            sem_g2 = S.newsem()
            S.dma("sp", sem_g2, [(GT[h], bass.AP(gD.tensor, h * LT, [[0, 128], [1, LT]])) for h in range(NH)],
                  reads=[B_gD], writes=[B_GT])
            B_T = Buf("T")
            S.dma("sp", sem_c, [(Tb[:, h, :], bass.AP(GT.tensor, h * 128 * LT + 127, [[LT - 1, 128], [1, 640]]))
                                for h in range(NH)], reads=[B_GT], writes=[B_T])
            S.op("pool", lambda e: e.memset(Tb[0:64, :, 576:640], -BIG), reads=[B_T], writes=[B_T])
            S.op("pool", lambda e: e.memset(Tb[64:128, :, 0:64], -BIG), reads=[B_T], writes=[B_T])
            S.barrier()
            areset()
            _ph[0] += 1
            if _ph[0] >= STOP:
                break

            cv = af32(32).rearrange("p (k v) -> p k v", v=2)
            cs = af32(32).rearrange("p (k v) -> p k v", v=2)
            sc_b = abf(32).rearrange("p (k v) -> p k v", v=2)
            gc = af32(32).rearrange("p (g k) -> p g k", g=2)
            adaR = af32(6 * D)
            b2 = af32(6 * D)
            slabs = [abf(KC * 512).rearrange("p (k c) -> p k c", k=KC) for _ in range(2)]
            B_slab = [Buf("slab0"), Buf("slab1")]
            sem_slab = [S.newsem(), S.newsem()]
            B_cv = Buf("cv")
            B_adaR = Buf("adaR")
            B_b2 = Buf("b2")
            B_adaD = Buf("adaD")
            B_modc = Buf("modc")
            sem_m = S.newsem()
            S.dma("sp", sem_m, [(cv, cvec[:, :, :]), (gc, gcols[:, :, :])], writes=[B_cv])
            S.dma("sp", sem_m, [(b2[0:1, :], b_ada[0:1, :]), (b2[1:2, :], b_ada[0:1, :])], writes=[B_b2])
            S.op("act", lambda e: e.activation(out=cs, in_=cv, func=AF.Sigmoid), reads=[B_cv], writes=[B_cv])
            S.op("dve", lambda e: e.tensor_tensor(out=sc_b, in0=cv, in1=cs, op=ALU.mult), reads=[B_cv], writes=[B_cv])
            for gi in range(24):
                sl = slabs[gi % 2]
                bs = B_slab[gi % 2]
                S.dma("pool", sem_slab[gi % 2], [(sl, wview(w_ada, gi * 512, 512))], writes=[bs])
                pb = PSB[gi % 2]
                ps = PS[gi % 2]
                for kc in range(KC):
                    S.op("pe", lambda e, ps=ps, sl=sl, kc=kc: e.matmul(ps[0:2, :], lhsT=sc_b[:, kc, :], rhs=sl[:, kc, :],
                                                                         start=(kc == 0), stop=(kc == KC - 1)),
                         reads=[B_cv, bs], writes=[pb])
                S.op("dve", lambda e, ps=ps, gi=gi: e.tensor_tensor(out=adaR[0:2, gi * 512:(gi + 1) * 512], in0=ps[0:2, :],
                                                                   in1=b2[0:2, gi * 512:(gi + 1) * 512], op=ALU.add),
                     reads=[pb, B_b2], writes=[B_adaR])
            S.dma("sp", sem_m, [(adaD[:, :], adaR[0:2, :])], reads=[B_adaR], writes=[B_adaD])
            adaC = af32(192).rearrange("p (c v) -> p c v", v=2)
            B_adaC = Buf("adaC")
            for c in range(96):
                S.op("pe", lambda e, c=c: e.transpose(PS[2][:, 2 * c:2 * c + 2], adaR[0:2, c * 128:(c + 1) * 128], identf[0:2, 0:2]),
                     reads=[B_adaR, B_const], writes=[PSB[2]])
            S.op("dve", lambda e: e.tensor_copy(out=adaC, in_=PS[2][:, 0:192].rearrange("p (c v) -> p c v", v=2)),
                 reads=[PSB[2]], writes=[B_adaC])
            for v in range(2):
                for (j, jsc, jsh, g) in ((0, 1, 0, 0), (2, 4, 3, 1)):
                    S.op("dve", lambda e, v=v, j=j, jsc=jsc, g=g: e.scalar_tensor_tensor(
                        out=modc[:, v, j, :], in0=adaC[:, jsc * 16:(jsc + 1) * 16, v], scalar=1.0, in1=gc[:, g, :],
                        op0=ALU.add, op1=ALU.mult), reads=[B_adaC, B_cv], writes=[B_modc])
                    S.op("dve", lambda e, v=v, j=j, jsh=jsh: e.tensor_copy(out=modc[:, v, j + 1, :],
                                                                            in_=adaC[:, jsh * 16:(jsh + 1) * 16, v]),
                         reads=[B_adaC], writes=[B_modc])
            S.barrier()
            areset()
            _ph[0] += 1
            if _ph[0] >= STOP:
                break

            def norm_tile(xt, bx, ntok, v, jm, hTdst, bh, junk, xn, bxn, ssq, rstd, bsm, psbase):
                S.op("act", lambda e: e.activation(out=junk[0:ntok, :], in_=xt[0:ntok, :], func=AF.Square,
                                                   accum_out=ssq[0:ntok, :]), reads=[bx], writes=[bxn, bsm])
                S.op("act", lambda e: e.activation(out=rstd[0:ntok, :], in_=ssq[0:ntok, :], func=AF.Sqrt,
                                                   scale=1.0 / D, bias=small[0:ntok, 63:64]), reads=[bsm, B_const], writes=[bsm])
                S.op("dve", lambda e: e.reciprocal(out=rstd[0:ntok, :], in_=rstd[0:ntok, :]), reads=[bsm], writes=[bsm])
                S.op("act", lambda e: e.activation(out=xn[0:ntok, :], in_=xt[0:ntok, :], func=AF.Identity,
                                                   scale=rstd[0:ntok, :]), reads=[bx, bsm], writes=[bxn])
                for q4 in range(4):
                    pi = psbase + q4
                    for i in range(4):
                        kc = q4 * 4 + i
                        S.op("pe", lambda e, pi=pi, i=i, kc=kc: e.transpose(PS[pi][:, i * 128:i * 128 + ntok],
                                                                              xn[0:ntok, kc * 128:(kc + 1) * 128],
                                                                              identf[0:ntok, 0:ntok]),
                             reads=[bxn, B_const], writes=[PSB[pi]])
                    for i in range(4):
                        kc = q4 * 4 + i
                        if q4 % 2 == 0:
                            S.op("dve", lambda e, pi=pi, i=i, kc=kc: e.tensor_scalar(
                                out=hTdst[:, kc, 0:ntok], in0=PS[pi][:, i * 128:i * 128 + ntok],
                                scalar1=modc[:, v, jm, kc:kc + 1], scalar2=modc[:, v, jm + 1, kc:kc + 1],
                                op0=ALU.mult, op1=ALU.add), reads=[PSB[pi], B_modc], writes=[bh])
                        else:
                            S.op("act", lambda e, pi=pi, i=i, kc=kc: e.activation(
                                out=hTdst[:, kc, 0:ntok], in_=PS[pi][:, i * 128:i * 128 + ntok], func=AF.Identity,
                                scale=modc[:, v, jm, kc:kc + 1], bias=modc[:, v, jm + 1, kc:kc + 1]),
                                reads=[PSB[pi], B_modc], writes=[bh])

            def pipeline(units, stages):
                n = len(units)
                K = len(stages)
                for i, U in enumerate(units):
                    U.idx = i
                for t in range(n + K - 1):
                    for k in range(K):
                        i = t - k
                        if 0 <= i < n:
                            U = units[i]
                            if k == 0 and U.pre is not None:
                                U.pre()
                            stages[k](U)
                            if k == K - 1 and U.post is not None:
                                U.post()

            class NU:
                pass

            def norm_pass(tiles, sc_off, sh_off, gidx, store_blk, dep_bufs):
                xt = [af32(D) for _ in range(3)]
                Bx = [Buf("x0"), Buf("x1"), Buf("x2")]
                semx = [S.newsem(), S.newsem(), S.newsem()]
                tb = [af32(D) for _ in range(2)]
                Bt = [Buf("t0"), Buf("t1")]
                hb16 = [abf(D) for _ in range(2)]
                Bhb16 = [Buf("hb16_0"), Buf("hb16_1")]
                hst = [abf(KC * 512).rearrange("p (k c) -> p k c", k=KC) for _ in range(2)]
                Bh = [Buf("h0"), Buf("h1")]
                semh = [S.newsem(), S.newsem()]
                Bsm = [Buf("sm0"), Buf("sm1")]
                Ab = [af32(D) for _ in range(2)]
                Sb = [af32(D) for _ in range(2)]
                gtmp = af32(D)
                Bmodb = Buf("modb")
                semmb = S.newsem()
                S.dma("sp", semmb, [(gtmp, bass.AP(grow.tensor, gidx * D, [[0, 128], [1, D]]))] +
                      [(Ab[v], bass.AP(adaD.tensor, v * 6 * D + sc_off * D, [[0, 128], [1, D]])) for v in range(2)] +
                      [(Sb[v], bass.AP(adaD.tensor, v * 6 * D + sh_off * D, [[0, 128], [1, D]])) for v in range(2)],
                      reads=[B_adaD], writes=[Bmodb])
                for v in range(2):
                    S.op("dve", lambda e, v=v: e.scalar_tensor_tensor(out=Ab[v], in0=Ab[v], scalar=1.0, in1=gtmp, op0=ALU.add, op1=ALU.mult),
                         reads=[Bmodb], writes=[Bmodb])

                def s0(U):
                    xb, sb = U.idx % 3, U.idx % 2
                    ntok = U.ntok
                    S.dma("sp", semx[xb], [(xt[xb][0:ntok, :], U.src)], reads=U.srcbufs, writes=[Bx[xb]])
                    S.op("act", lambda e: e.activation(out=tb[sb][0:ntok, :], in_=xt[xb][0:ntok, :], func=AF.Square,
                                                       accum_out=small[0:ntok, 2 * sb:2 * sb + 1]), reads=[Bx[xb]], writes=[Bt[sb], Bsm[sb]])

                def s1(U):
                    xb, sb = U.idx % 3, U.idx % 2
                    ntok, v = U.ntok, U.v
                    rstd = small[:, 2 * sb + 1:2 * sb + 2]
                    S.op("act", lambda e: e.activation(out=rstd[0:ntok, :], in_=small[0:ntok, 2 * sb:2 * sb + 1], func=AF.Sqrt,
                                                       scale=1.0 / D, bias=small[0:ntok, 63:64]), reads=[Bsm[sb], B_const], writes=[Bsm[sb]])
                    S.op("dve", lambda e: e.reciprocal(out=rstd[0:ntok, :], in_=rstd[0:ntok, :]), reads=[Bsm[sb]], writes=[Bsm[sb]])
                    S.op("dve", lambda e: e.scalar_tensor_tensor(out=tb[sb][0:ntok, :], in0=xt[xb][0:ntok, :], scalar=rstd[0:ntok, :],
                                                                 in1=Ab[v][0:ntok, :], op0=ALU.mult, op1=ALU.mult),
                         reads=[Bx[xb], Bsm[sb], Bmodb], writes=[Bt[sb]])
                    S.op("pool", lambda e: e.tensor_tensor(out=hb16[sb][0:ntok, :], in0=tb[sb][0:ntok, :], in1=Sb[v][0:ntok, :], op=ALU.add),
                         reads=[Bt[sb], Bmodb], writes=[Bhb16[sb]])

                def s2(U):
                    sb = U.idx % 2
                    ntok = U.ntok
                    for q4 in range(4):
                        pi = 4 * sb + q4
                        for i in range(4):
                            kc = q4 * 4 + i
                            S.op("pe", lambda e, pi=pi, i=i, kc=kc: e.matmul(PS[pi][:, i * 128:i * 128 + ntok],
                                                                             lhsT=hb16[sb][0:ntok, kc * 128:(kc + 1) * 128],
                                                                             rhs=identb[0:ntok, 0:ntok], start=True, stop=True),
                                 reads=[Bhb16[sb], B_const], writes=[PSB[pi]])

                def s3(U):
                    sb = U.idx % 2
                    ntok = U.ntok
                    for q4 in range(4):
                        pi = 4 * sb + q4
                        src = PS[pi][:, :].rearrange("p (k c) -> p k c", k=4)[:, :, 0:ntok]
                        dst = U.hTdst[:, q4 * 4:(q4 + 1) * 4, 0:ntok]
                        if q4 % 2 == 0:
                            S.op("act", lambda e, src=src, dst=dst: e.activation(out=dst, in_=src, func=AF.Copy), reads=[PSB[pi]], writes=[U.bh])
                        else:
                            S.op("dve", lambda e, src=src, dst=dst: e.tensor_copy(out=dst, in_=src), reads=[PSB[pi]], writes=[U.bh])

                units = []
                for (src, srcbufs, ntok, v, blk, j, last) in tiles:
                    U = NU()
                    U.pre = None
                    U.post = None
                    U.src, U.srcbufs, U.ntok, U.v = src, srcbufs, ntok, v
                    hb = blk % 2
                    U.hTdst = hst[hb][:, :, j * 128:(j + 1) * 128]
                    U.bh = Bh[hb]
                    if last:
                        def post(blk=blk, hb=hb):
                            store_blk(blk, hst[hb], Bh[hb], semh[hb])
                        U.post = post
                    units.append(U)
                pipeline(units, [s0, s1, s2, s3])

            B_HT = [Buf("HT%d" % i) for i in range(NBLK + 1)]
            tiles1 = []
            for blk in range(NBLK + 1):
                nt = 4 if blk < NBLK else 1
                for j in range(nt):
                    src = xin[blk, j * 128:(j + 1) * 128, :] if blk < NBLK else xs_in[:, :]
                    tiles1.append((src, [], 128 if blk < NBLK else NS, 0 if blk < NBLK else 1, blk, j, j == nt - 1))

            def store1(blk, st, bst, sem):
                S.dma("sp", sem, [(HT[blk], st)], reads=[bst], writes=[B_HT[blk]])
            norm_pass(tiles1, 1, 0, 0, store1, None)
            S.barrier()
            areset()
            _ph[0] += 1
            if _ph[0] >= STOP:
                break

            wg = [abf(KC * 1024).rearrange("p (k c) -> p k c", k=KC) for _ in range(2)]
            Bwg = [Buf("wg0"), Buf("wg1")]
            semw = [S.newsem(), S.newsem()]
            hbk = [abf(KC * 512).rearrange("p (k c) -> p k c", k=KC) for _ in range(2)]
            Bhb = [Buf("hb0"), Buf("hb1")]
            semhb = [S.newsem(), S.newsem()]
            kvf = [af32(1024) for _ in range(2)]
            Bkvf = [Buf("kvf0"), Buf("kvf1")]
            semkvf = [S.newsem(), S.newsem()]
            kvb = [abf(1024) for _ in range(2)]
            Bkvb = [Buf("kvb0"), Buf("kvb1")]
            semkvb = [S.newsem(), S.newsem()]
            ktb = [abf(NH * 512).rearrange("p (h c) -> p h c", h=NH) for _ in range(2)]
            Bktb = [Buf("ktb0"), Buf("ktb1")]
            semktb = [S.newsem(), S.newsem()]
            qst = [abf(NH * 512).rearrange("p (h c) -> p h c", h=NH) for _ in range(2)]
            Bqst = [Buf("qst0"), Buf("qst1")]
            semqst = [S.newsem(), S.newsem()]
            gst = [af32(NH * 512).rearrange("p (h c) -> p h c", h=NH) for _ in range(2)]
            Bgst = [Buf("gst0"), Buf("gst1")]
            semgst = [S.newsem(), S.newsem()]
            B_scr = Buf("scr_w")

            own_blocks = [(0, 512, 0), (1, 512, 512), (2, 512, 1024), (3, 512, 1536), (NBLK, NS, 2048)]
            a_blocks = [(0, 512, 0), (1, 512, 512), (4, 512, 1024), (2, 512, 1536), (3, 512, 2048), (5, 512, 2560), (NBLK, NS, 3072)]
            b_blocks = [(0, 512, 0), (1, 512, 512), (2, 512, 1024), (3, 512, 1536)] + \
                       [(6 + i, 512, 2048 + 512 * i) for i in range(14)] + [(NBLK, NS, 9216)]
            own_row = {0: 0, 1: 512, 2: 1024, 3: 1536, NBLK: 2048}
            a_out_row = {2: 0, NBLK: 512}
            groups = [("q", 0, QaT), ("k", 1024, "a"), ("v", 2048, "a"), ("q", 3072, QbT), ("k", 4096, "b"), ("v", 5120, "b"),
                      ("g", 6144, (GA, 0)), ("g", 7168, (GA, 8)), ("g", 8192, (GB, 0)), ("g", 9216, (GB, 8))]
            cnt = {"hb": 0, "kv": 0, "kt": 0, "q": 0, "g": 0, "ps": 0}
            def blist_of(kind, dst):
                if kind in ("q", "g"):
                    return own_blocks
                return a_blocks if dst == "a" else b_blocks

            def load_w2(gi):
                c0_ = groups[gi][1]
                W_ = wg[gi % 2]
                S.dma("pool", semw[gi % 2], [(W_[:, :, 0:512], wview(w_in, c0_, 512)), (W_[:, :, 512:1024], wview(w_in, c0_ + 512, 512))],
                      writes=[Bwg[gi % 2]])
            seq2 = [blk for (kind, c0, dst) in groups for (blk, ntk, idx0) in blist_of(kind, dst)]

            def load_h2(n):
                S.dma("sp", semhb[n % 2], [(hbk[n % 2], HT[seq2[n]])], reads=[B_HT[seq2[n]]], writes=[Bhb[n % 2]])
            load_w2(0)
            load_h2(0)
            pendk = [None]
            for gi, (kind, c0, dst) in enumerate(groups):
                wb = gi % 2
                W = wg[wb]
                if gi + 1 < len(groups):
                    load_w2(gi + 1)
                blist = blist_of(kind, dst)
                for (blk, ntk, idx0) in blist:
                    hi_ = cnt["hb"] % 2
                    hb_ = hbk[hi_]
                    if cnt["hb"] + 1 < len(seq2):
                        load_h2(cnt["hb"] + 1)
                    cnt["hb"] += 1
                    if kind == "q":
                        qi = cnt["q"] % 2
                        cnt["q"] += 1
                        for cc in range(8):
                            pi = cnt["ps"] % 8
                            cnt["ps"] += 1
                            for kc in range(KC):
                                S.op("pe", lambda e, pi=pi, cc=cc, kc=kc, W=W, hb_=hb_, ntk=ntk: e.matmul(
                                    PS[pi][:, 0:ntk], lhsT=W[:, kc, cc * 128:(cc + 1) * 128], rhs=hb_[:, kc, 0:ntk],
                                    start=(kc == 0), stop=(kc == KC - 1)), reads=[Bwg[wb], Bhb[hi_]], writes=[PSB[pi]])
                            eng = "act" if cc % 2 == 0 else "dve"
                            if eng == "act":
                                S.op("act", lambda e, pi=pi, cc=cc, qi=qi, ntk=ntk: e.activation(
                                    out=qst[qi][:, cc, 0:ntk], in_=PS[pi][:, 0:ntk], func=AF.Copy), reads=[PSB[pi]], writes=[Bqst[qi]])
                            else:
                                S.op("dve", lambda e, pi=pi, cc=cc, qi=qi, ntk=ntk: e.tensor_copy(
                                    out=qst[qi][:, cc, 0:ntk], in_=PS[pi][:, 0:ntk]), reads=[PSB[pi]], writes=[Bqst[qi]])
                        r0 = own_row[blk]
                        S.dma("sp", semqst[qi], [(dst[:, :, r0:r0 + ntk].rearrange("h p c -> p h c"), qst[qi][:, :, 0:ntk])],
                              reads=[Bqst[qi]], writes=[B_scr])
                    elif kind == "g":
                        G, ch0 = dst
                        qi = cnt["g"] % 2
                        cnt["g"] += 1
                        for cc in range(8):
                            pi = cnt["ps"] % 8
                            cnt["ps"] += 1
                            for kc in range(KC):
                                S.op("pe", lambda e, pi=pi, cc=cc, kc=kc, W=W, hb_=hb_, ntk=ntk: e.matmul(
                                    PS[pi][:, 0:ntk], lhsT=W[:, kc, cc * 128:(cc + 1) * 128], rhs=hb_[:, kc, 0:ntk],
                                    start=(kc == 0), stop=(kc == KC - 1)), reads=[Bwg[wb], Bhb[hi_]], writes=[PSB[pi]])
                            S.op("act", lambda e, pi=pi, cc=cc, qi=qi, ntk=ntk: e.activation(
                                out=gst[qi][:, cc, 0:ntk], in_=PS[pi][:, 0:ntk], func=AF.Sigmoid), reads=[PSB[pi]], writes=[Bgst[qi]])
                        r0 = own_row[blk]
                        S.dma("sp", semgst[qi], [(G[ch0:ch0 + 8, :, r0:r0 + ntk].rearrange("h p c -> p h c"), gst[qi][:, :, 0:ntk])],
                              reads=[Bgst[qi]], writes=[B_scr])
                    else:
                        isA = (dst == "a")
                        KT_, VS_ = (KaT, VaS) if isA else (KbT, VbS)
                        nt = 4 if ntk == 512 else 1
                        ki = cnt["kt"] % 2
                        if kind == "k":
                            cnt["kt"] += 1
                        for j in range(nt):
                            ntok = 128 if ntk == 512 else NS
                            fi = cnt["kv"] % 2
                            cnt["kv"] += 1
                            for half in range(2):
                                pi = cnt["ps"] % 8
                                cnt["ps"] += 1
                                for kc in range(KC):
                                    S.op("pe", lambda e, pi=pi, half=half, kc=kc, W=W, hb_=hb_, j=j, ntok=ntok: e.matmul(
                                        PS[pi][0:ntok, :], lhsT=hb_[:, kc, j * 128:j * 128 + ntok], rhs=W[:, kc, half * 512:(half + 1) * 512],
                                        start=(kc == 0), stop=(kc == KC - 1)), reads=[Bwg[wb], Bhb[hi_]], writes=[PSB[pi]])
                                S.op("act", lambda e, pi=pi, half=half, fi=fi, ntok=ntok: e.activation(
                                    out=kvf[fi][0:ntok, half * 512:(half + 1) * 512], in_=PS[pi][0:ntok, :], func=AF.Copy),
                                    reads=[PSB[pi]], writes=[Bkvf[fi]])
                                S.op("pool", lambda e, half=half, fi=fi, ntok=ntok: e.tensor_copy(
                                    out=kvb[fi][0:ntok, half * 512:(half + 1) * 512],
                                    in_=kvf[fi][0:ntok, half * 512:(half + 1) * 512]),
                                    reads=[Bkvf[fi]], writes=[Bkvb[fi]])
                            outs = []
                            if isA and blk in a_out_row:
                                o_t = ka_o if kind == "k" else va_o
                                r0 = a_out_row[blk] + j * 128
                                outs.append((o_t[r0:r0 + ntok, :], kvf[fi][0:ntok, :]))
                            if (not isA) and blk in own_row:
                                o_t = kb_o if kind == "k" else vb_o
                                r0 = own_row[blk] + j * 128
                                outs.append((o_t[r0:r0 + ntok, :], kvf[fi][0:ntok, :]))
                            if outs:
                                S.dma("sp", semkvf[fi], outs, reads=[Bkvf[fi]], writes=[])
                            if kind == "v":
                                tix = (idx0 + j * 128) // 128
                                S.dma("sp", semkvb[fi], [(VS_[tix, 0:ntok, :], kvb[fi][0:ntok, :])], reads=[Bkvb[fi]], writes=[B_scr])
                            else:
                                def ktrans(fi=fi, ntok=ntok, ki=ki, j=j):
                                    pi = cnt["ps"] % 8
                                    cnt["ps"] += 1
                                    psb16 = PS[pi][:, :].bitcast(BF16)
                                    for h in range(NH):
                                        S.op("pe", lambda e, psb16=psb16, h=h, fi=fi, ntok=ntok: e.transpose(
                                            psb16[:, h * 128:h * 128 + ntok], kvb[fi][0:ntok, h * 128:(h + 1) * 128], identb[0:ntok, 0:ntok]),
                                            reads=[Bkvb[fi], B_const], writes=[PSB[pi]])
                                    S.op("dve", lambda e, psb16=psb16, ki=ki, j=j, ntok=ntok: e.tensor_copy(
                                        out=ktb[ki][:, :, j * 128:j * 128 + ntok],
                                        in_=psb16.rearrange("p (h c) -> p h c", h=NH)[:, :, 0:ntok]), reads=[PSB[pi]], writes=[Bktb[ki]])
                                if pendk[0] is not None:
                                    pendk[0]()
                                pendk[0] = ktrans
                        if kind == "k" and pendk[0] is not None:
                            pendk[0]()
                            pendk[0] = None
                        if kind == "k":
                            S.dma("sp", semktb[ki], [(KT_[:, :, idx0:idx0 + ntk].rearrange("h p c -> p h c"), ktb[ki][:, :, 0:ntk])],
                                  reads=[Bktb[ki]], writes=[B_scr])
            S.barrier()
            areset()
            _ph[0] += 1
            if _ph[0] >= STOP:
                break

            def sm(i):
                return small[:, i:i + 1]
            qtA = [abf(NH * 128).rearrange("p (h c) -> p h c", h=NH) for _ in range(2)]
            ktA = [abf(NH * 640).rearrange("p (h c) -> p h c", h=NH) for _ in range(2)]
            vtA = [abf(5 * 1024).rearrange("p (t c) -> p t c", t=5) for _ in range(2)]
            BldA = [Buf("ldA0"), Buf("ldA1")]
            semldA = [S.newsem(), S.newsem()]
            s_sb = [af32(640) for _ in range(2)]
            Bs = [Buf("s0"), Buf("s1")]
            p_sb = [abf(640) for _ in range(2)]
            Bp = [Buf("p0"), Buf("p1")]
            pT = [abf(5 * 128).rearrange("p (t c) -> p t c", t=5) for _ in range(2)]
            BpT = [Buf("pT0"), Buf("pT1")]
            oa = [abf(1024) for _ in range(3)]
            Boa = [Buf("oa0"), Buf("oa1"), Buf("oa2")]
            oaTs = [abf(NH * 512).rearrange("p (h c) -> p h c", h=NH) for _ in range(2)]
            BoaT = [Buf("oaT0"), Buf("oaT1")]
            semoaT = [S.newsem(), S.newsem()]
            Bst = [Buf("st%d" % i) for i in range(8)]
            B_OaT = Buf("OaT")
            ckA = abf(4 * 1024).rearrange("p (t c) -> p t c", t=4)
            cvA = abf(4 * 1024).rearrange("p (t c) -> p t c", t=4)
            ktS = abf(NH * 544).rearrange("p (h c) -> p h c", h=NH)
            qtS = abf(NH * NS).rearrange("p (h c) -> p h c", h=NH)
            vS0 = abf(1024)
            BckA = Buf("ckA")
            BcvA = Buf("cvA")
            BktS = Buf("ktS")
            semcA = S.newsem()
            semcA2 = S.newsem()

            class AU:
                pass

            def mkA(nq, hsel, q_ap, k_ap, nk, vblocks, bq, bk, bv, flag_c0, oa_t, boa):
                U = AU()
                U.nq, U.hsel, U.q_ap, U.k_ap, U.nk, U.vblocks = nq, hsel, q_ap, k_ap, nk, vblocks
                U.bq, U.bk, U.bv, U.flag_c0, U.oa_t, U.boa = bq, bk, bv, flag_c0, oa_t, boa
                U.pre = None
                U.post = None
                return U

            def a0(U):
                u = U.idx % 2
                nq, nk = U.nq, U.nk
                n1 = min(nk, 512)
                S.op("pe", lambda e: e.matmul(PS[2 * u][0:nq, 0:n1], lhsT=U.q_ap, rhs=U.k_ap[:, 0:n1], start=True, stop=True),
                     reads=[U.bq, U.bk], writes=[PSB[2 * u]])
                if nk > 512:
                    S.op("pe", lambda e: e.matmul(PS[2 * u + 1][0:nq, 0:nk - 512], lhsT=U.q_ap, rhs=U.k_ap[:, 512:nk], start=True, stop=True),
                         reads=[U.bq, U.bk], writes=[PSB[2 * u + 1]])

            def a1(U):
                u = U.idx % 2
                nq, nk, hsel = U.nq, U.nk, U.hsel
                n1 = min(nk, 512)
                s_ = s_sb[u]
                st = 8 + 4 * (U.idx % 8)
                bst = Bst[U.idx % 8]
                S.op("dve", lambda e: e.scalar_tensor_tensor(out=s_[0:nq, 0:n1], in0=PS[2 * u][0:nq, 0:n1], scalar=SCALE,
                                                             in1=Tb[0:nq, hsel, 0:n1], op0=ALU.mult, op1=ALU.add),
                     reads=[PSB[2 * u], B_T], writes=[Bs[u]])
                if nk > 512:
                    S.op("dve", lambda e: e.scalar_tensor_tensor(out=s_[0:nq, 512:nk], in0=PS[2 * u + 1][0:nq, 0:nk - 512], scalar=SCALE,
                                                                 in1=Tb[0:nq, hsel, 512:nk], op0=ALU.mult, op1=ALU.add),
                         reads=[PSB[2 * u + 1], B_T], writes=[Bs[u]])
                if U.flag_c0 is not None:
                    fc = U.flag_c0
                    S.op("dve", lambda e: e.tensor_scalar(out=s_[0:nq, fc:nk], in0=s_[0:nq, fc:nk], scalar1=flg[0:nq, 0:1],
                                                          scalar2=None, op0=ALU.add), reads=[Bs[u], B_const], writes=[Bs[u]])
                S.op("dve", lambda e: e.reduce_max(out=sm(st)[0:nq, :], in_=s_[0:nq, 0:nk], axis=AX.X), reads=[Bs[u]], writes=[bst])
                S.op("dve", lambda e: e.tensor_scalar(out=sm(st + 1)[0:nq, :], in0=sm(st)[0:nq, :], scalar1=-1.0, scalar2=None,
                                                      op0=ALU.mult), reads=[bst], writes=[bst])

            def a2(U):
                u = U.idx % 2
                nq, nk = U.nq, U.nk
                st = 8 + 4 * (U.idx % 8)
                bst = Bst[U.idx % 8]
                S.op("act", lambda e: e.activation(out=p_sb[u][0:nq, 0:nk], in_=s_sb[u][0:nq, 0:nk], func=AF.Exp, bias=sm(st + 1)[0:nq, :],
                                                   scale=1.0, accum_out=sm(st + 2)[0:nq, :]), reads=[Bs[u], bst], writes=[Bp[u], bst])

            def a3(U):
                u = U.idx % 2
                nq = U.nq
                st = 8 + 4 * (U.idx % 8)
                bst = Bst[U.idx % 8]
                S.op("dve", lambda e: e.reciprocal(out=sm(st + 3)[0:nq, :], in_=sm(st + 2)[0:nq, :]), reads=[bst], writes=[bst])
                psb16 = PS[4 + u][:, :].bitcast(BF16)
                for bi, (koff, nkb, v_ap) in enumerate(U.vblocks):
                    S.op("pe", lambda e, bi=bi, koff=koff, nkb=nkb: e.transpose(psb16[0:nkb, bi * 128:bi * 128 + nq],
                                                                                 p_sb[u][0:nq, koff:koff + nkb], identb[0:nq, 0:nq]),
                         reads=[Bp[u], B_const], writes=[PSB[4 + u]])

            def a4(U):
                u = U.idx % 2
                nq = U.nq
                nb = len(U.vblocks)
                psb16 = PS[4 + u][:, :].bitcast(BF16)
                S.op("act", lambda e: e.activation(out=pT[u][:, 0:nb, 0:nq],
                                                   in_=psb16[:, 0:nb * 128].rearrange("p (t c) -> p t c", t=nb)[:, :, 0:nq],
                                                   func=AF.Copy), reads=[PSB[4 + u]], writes=[BpT[u]])

            def a5(U):
                u = U.idx % 2
                nq = U.nq
                nb = len(U.vblocks)
                for bi, (koff, nkb, v_ap) in enumerate(U.vblocks):
                    S.op("pe", lambda e, bi=bi, nkb=nkb, v_ap=v_ap: e.matmul(PS[6 + u][0:nq, 0:128], lhsT=pT[u][0:nkb, bi, 0:nq], rhs=v_ap,
                                                                             start=(bi == 0), stop=(bi == nb - 1)),
                         reads=[BpT[u], U.bv], writes=[PSB[6 + u]])

            def a6(U):
                u = U.idx % 2
                nq, hsel = U.nq, U.hsel
                st = 8 + 4 * (U.idx % 8)
                bst = Bst[U.idx % 8]
                S.op("act", lambda e: e.activation(out=U.oa_t[0:nq, hsel * 128:(hsel + 1) * 128], in_=PS[6 + u][0:nq, 0:128],
                                                   func=AF.Identity, scale=sm(st + 3)[0:nq, :]), reads=[PSB[6 + u], bst], writes=[U.boa])

            afc = [0]

            def a_finish(nq, oa_t, boa, stage, bstage, col0):
                pi = 5
                psb16 = PS[pi][:, :].bitcast(BF16)
                for h in range(NH):
                    S.op("pe", lambda e, h=h: e.transpose(psb16[:, h * 128:h * 128 + nq], oa_t[0:nq, h * 128:(h + 1) * 128],
                                                           identb[0:nq, 0:nq]), reads=[boa, B_const], writes=[PSB[pi]])
                S.op("dve", lambda e: e.tensor_copy(out=stage[:, :, col0:col0 + nq],
                                                    in_=psb16.rearrange("p (h c) -> p h c", h=NH)[:, :, 0:nq]),
                     reads=[PSB[pi]], writes=[bstage])

            unitsA = []
            pc = 0
            for piece in range(2):
                for j in range(8):
                    li = pc % 2
                    oi = pc % 3
                    sti = (pc // 4) % 2
                    tok0 = piece * 1024 + 128 * j
                    k0 = piece * 1536 + 128 * j
                    t0_ = k0 // 128
                    flag_c0 = (1024 - 128 * j) if (piece == 0 and j >= 4) else None

                    def preA(li=li, tok0=tok0, k0=k0, t0_=t0_):
                        S.dma("sp", semldA[li], [
                            (qtA[li], QaT[:, :, tok0:tok0 + 128].rearrange("h p c -> p h c")),
                            (ktA[li], KaT[:, :, k0:k0 + 640].rearrange("h p c -> p h c")),
                            (vtA[li], VaS[t0_:t0_ + 5].rearrange("t p c -> p t c"))], reads=[B_scr], writes=[BldA[li]])

                    def postA(pc=pc, oi=oi, sti=sti):
                        a_finish(128, oa[oi], Boa[oi], oaTs[sti], BoaT[sti], (pc % 4) * 128)
                        if pc % 4 == 3:
                            r0 = (pc // 4) * 512
                            S.dma("sp", semoaT[sti], [(OaT[:, :, r0:r0 + 512].rearrange("h p c -> p h c"), oaTs[sti])],
                                  reads=[BoaT[sti]], writes=[B_OaT])
                    for h in range(NH):
                        vbl = [(128 * t, 128, vtA[li][:, t, h * 128:(h + 1) * 128]) for t in range(5)]
                        U = mkA(128, h, qtA[li][:, h, :], ktA[li][:, h, :], 640, vbl, BldA[li], BldA[li], BldA[li], flag_c0, oa[oi], Boa[oi])
                        if h == 0:
                            U.pre = preA
                        if h == NH - 1:
                            U.post = postA
                        unitsA.append(U)
                    pc += 1
            oiS = pc % 3

            def preS():
                S.dma("pool", semcA, [(ckA, ca_k.rearrange("(t p) c -> p t c", p=128)), (cvA, ca_v.rearrange("(t p) c -> p t c", p=128))],
                      writes=[BckA, BcvA])
                S.dma("sp", semcA2, [(qtS, QaT[:, :, 2048:2080].rearrange("h p c -> p h c")),
                                     (ktS[:, :, 0:NS], KaT[:, :, 3072:3104].rearrange("h p c -> p h c")),
                                     (vS0[0:NS, :], VaS[24, 0:NS, :])], reads=[B_scr], writes=[BktS])
                for h in range(NH):
                    pi = h % 2
                    psb16 = PS[pi][:, :].bitcast(BF16)
                    for t in range(4):
                        S.op("pe", lambda e, psb16=psb16, h=h, t=t: e.transpose(psb16[:, t * 128:(t + 1) * 128], ckA[:, t, h * 128:(h + 1) * 128],
                                                                                  identb[:, :]), reads=[BckA, B_const], writes=[PSB[pi]])
                    S.op("dve", lambda e, psb16=psb16, h=h: e.tensor_copy(out=ktS[:, h, NS:NS + 512], in_=psb16[:, 0:512]),
                         reads=[PSB[pi]], writes=[BktS])

            def postS():
                a_finish(NS, oa[oiS], Boa[oiS], oaTs[0], BoaT[0], 0)
                S.dma("sp", semoaT[0], [(OaT[:, :, 2048:2080].rearrange("h p c -> p h c"), oaTs[0][:, :, 0:NS])],
                      reads=[BoaT[0]], writes=[B_OaT])
            for h in range(NH):
                vbl = [(0, NS, vS0[0:NS, h * 128:(h + 1) * 128])] + \
                      [(NS + 128 * t, 128, cvA[:, t, h * 128:(h + 1) * 128]) for t in range(4)]
                U = mkA(NS, h, qtS[:, h, :], ktS[:, h, :], 544, vbl, BktS, BktS, BcvA, None, oa[oiS], Boa[oiS])
                if h == 0:
                    U.pre = preS
                if h == NH - 1:
                    U.post = postS
                unitsA.append(U)
            pipeline(unitsA, [a0, a1, a2, a3, a4, a5, a6])
            S.barrier()
            areset()
            _ph[0] += 1
            if _ph[0] >= STOP:
                break

            qb = abf(NH * 1024).rearrange("p (h c) -> p h c", h=NH)
            Bqb = Buf("qb")
            semqb = S.newsem()
            ktB = [abf(NH * 1024).rearrange("p (h c) -> p h c", h=NH) for _ in range(2)]
            vtB = [abf(8 * 1024).rearrange("p (t c) -> p t c", t=8) for _ in range(2)]
            BktB = [Buf("ktB0"), Buf("ktB1")]
            BvtB = [Buf("vtB0"), Buf("vtB1")]
            semkB = [S.newsem(), S.newsem()]
            semvB = [S.newsem(), S.newsem()]
            acc = af32(NH * 1024).rearrange("p (h c) -> p h c", h=NH)
            Bacc = Buf("acc")
            carry = af32(64)
            Bcar = Buf("carry")
            m_sb = [af32(1024) for _ in range(2)]
            Bm = [Buf("m0"), Buf("m1")]
            Pb = [af32(1026) for _ in range(3)]
            BPb = [Buf("P0"), Buf("P1"), Buf("P2")]
            A_sb = [abf(1024) for _ in range(2)]
            BA = [Buf("A0"), Buf("A1")]
            AT_sb = [abf(8 * 128).rearrange("p (t c) -> p t c", t=8) for _ in range(2)]
            BAT = [Buf("AT0"), Buf("AT1")]
            obst = abf(NH * 1024).rearrange("p (h c) -> p h c", h=NH)
            Bobst = Buf("obst")
            semob = S.newsem()
            B_ObT = Buf("ObT")

            class BU:
                pass

            def mk_unit(nq, q_ap, bq, k_ap, bk, c0, c1, blocks, bv, diag, bias_ap, car_ap, acc_ap, first):
                U = BU()
                U.nq, U.q_ap, U.bq, U.k_ap, U.bk, U.c0, U.c1 = nq, q_ap, bq, k_ap, bk, c0, c1
                U.blocks, U.bv, U.diag, U.bias_ap, U.car_ap, U.acc_ap, U.first = blocks, bv, diag, bias_ap, car_ap, acc_ap, first
                U.pre = None
                U.post = None
                ch = []
                c = c0
                while c < c1:
                    w = min(512, c1 - c)
                    ch.append((c, w))
                    c += w
                U.chunks = ch
                return U

            def st_z(U):
                u = U.idx % 2
                zb = [2 * u, 2 * u + 1]
                for ci, (cc, w) in enumerate(U.chunks):
                    S.op("pe", lambda e, ci=ci, cc=cc, w=w: e.matmul(PS[zb[ci]][0:U.nq, 0:w], lhsT=U.q_ap, rhs=U.k_ap[:, cc:cc + w],
                                                                     start=True, stop=True), reads=[U.bq, U.bk], writes=[PSB[zb[ci]]])

            def st_sig(U):
                u = U.idx % 2
                zb = [2 * u, 2 * u + 1]
                m_ = m_sb[u]
                nq = U.nq
                for ci, (cc, w) in enumerate(U.chunks):
                    if U.bias_ap is None:
                        S.op("act", lambda e, ci=ci, cc=cc, w=w: e.activation(out=m_[0:nq, cc:cc + w], in_=PS[zb[ci]][0:nq, 0:w],
                                                                              func=AF.Sigmoid, scale=-SCALE),
                             reads=[PSB[zb[ci]]], writes=[Bm[u]])
                    else:
                        S.op("act", lambda e, ci=ci, cc=cc, w=w: e.activation(out=m_[0:nq, cc:cc + w], in_=PS[zb[ci]][0:nq, 0:w],
                                                                              func=AF.Sigmoid, scale=-SCALE, bias=U.bias_ap[0:nq, :]),
                             reads=[PSB[zb[ci]], B_const], writes=[Bm[u]])

            def st_pinit(U):
                u3 = U.idx % 3
                P_ = Pb[u3]
                nq, c0 = U.nq, U.c0
                if U.first:
                    S.op("pool", lambda e: e.memset(P_[0:nq, c0:c0 + 1], 1.0), writes=[BPb[u3]])
                else:
                    S.op("pool", lambda e: e.tensor_copy(out=P_[0:nq, c0:c0 + 1], in_=U.car_ap[0:nq, :]), reads=[Bcar], writes=[BPb[u3]])

            def st_sig2(U):
                st_sig(U)
                st_pinit(U)

            def st_scan(U):
                u = U.idx % 2
                u3 = U.idx % 3
                m_, P_ = m_sb[u], Pb[u3]
                nq, c0, c1 = U.nq, U.c0, U.c1
                if U.diag:
                    nkb0 = U.blocks[0][1]
                    S.op("dve", lambda e: e.tensor_tensor(out=m_[0:nq, c0:c0 + nkb0], in0=m_[0:nq, c0:c0 + nkb0],
                                                          in1=maskL[0:nq, 0:nkb0], op=ALU.max),
                         reads=[Bm[u], B_const], writes=[Bm[u]])
                S.op("dve", lambda e: e.tensor_tensor_scan(out=P_[0:nq, c0 + 1:c1 + 1], data0=m_[0:nq, c0:c1], data1=m_[0:nq, c0:c1],
                                                           initial=P_[0:nq, c0:c0 + 1], op0=ALU.mult, op1=ALU.bypass),
                     reads=[Bm[u], BPb[u3]], writes=[BPb[u3]])

            def st_sub(U):
                u = U.idx % 2
                u3 = U.idx % 3
                P_, A_ = Pb[u3], A_sb[u]
                nq, c0, c1 = U.nq, U.c0, U.c1
                S.op("pool", lambda e: e.tensor_tensor(out=A_[0:nq, c0:c1], in0=P_[0:nq, c0:c1], in1=P_[0:nq, c0 + 1:c1 + 1],
                                                       op=ALU.subtract), reads=[BPb[u3]], writes=[BA[u]])
                S.op("pool", lambda e: e.tensor_copy(out=U.car_ap[0:nq, :], in_=P_[0:nq, c1:c1 + 1]), reads=[BPb[u3]], writes=[Bcar])

            def st_tr(U):
                u = U.idx % 2
                A_ = A_sb[u]
                nq = U.nq
                psb16 = PS[4 + u][:, :].bitcast(BF16)
                for bi, (off, nkb, v_ap) in enumerate(U.blocks):
                    S.op("pe", lambda e, bi=bi, off=off, nkb=nkb: e.transpose(psb16[0:nkb, bi * 128:bi * 128 + nq],
                                                                               A_[0:nq, off:off + nkb], identb[0:nq, 0:nq]),
                         reads=[BA[u], B_const], writes=[PSB[4 + u]])

            def st_cp(U):
                u = U.idx % 2
                nq = U.nq
                nb = len(U.blocks)
                psb16 = PS[4 + u][:, :].bitcast(BF16)
                S.op("act", lambda e: e.activation(out=AT_sb[u][:, 0:nb, 0:nq],
                                                   in_=psb16[:, 0:nb * 128].rearrange("p (t c) -> p t c", t=nb)[:, :, 0:nq],
                                                   func=AF.Copy), reads=[PSB[4 + u]], writes=[BAT[u]])

            def st_av(U):
                u = U.idx % 2
                nq = U.nq
                nb = len(U.blocks)
                for bi, (off, nkb, v_ap) in enumerate(U.blocks):
                    S.op("pe", lambda e, bi=bi, nkb=nkb, v_ap=v_ap: e.matmul(PS[6 + u][:, 0:nq], lhsT=v_ap, rhs=AT_sb[u][0:nkb, bi, 0:nq],
                                                                             start=(bi == 0), stop=(bi == nb - 1)),
                         reads=[BAT[u], U.bv], writes=[PSB[6 + u]])

            def st_acc(U):
                u = U.idx % 2
                nq = U.nq
                if U.first:
                    S.op("dve", lambda e: e.tensor_copy(out=U.acc_ap, in_=PS[6 + u][:, 0:nq]), reads=[PSB[6 + u]], writes=[Bacc])
                else:
                    S.op("dve", lambda e: e.tensor_tensor(out=U.acc_ap, in0=PS[6 + u][:, 0:nq], in1=U.acc_ap, op=ALU.add),
                         reads=[PSB[6 + u], Bacc], writes=[Bacc])

            def pipeline(units, stages):
                n = len(units)
                K = len(stages)
                for i, U in enumerate(units):
                    U.idx = i
                for t in range(n + K - 1):
                    for k in range(K):
                        i = t - k
                        if 0 <= i < n:
                            U = units[i]
                            if k == 0 and U.pre is not None:
                                U.pre()
                            stages[k](U)
                            if k == K - 1 and U.post is not None:
                                U.post()

            unitsB = []
            ldc = 0
            for piece in range(2):
                if piece == 0:
                    klist = [(0, None, True)] + [(2048 + 1024 * s_, 1 + i, False) for i, s_ in enumerate([2, 1, 0])]
                else:
                    klist = [(1024, None, True)] + [(2048 + 1024 * s_, 4 + i, False) for i, s_ in enumerate([6, 5, 4, 3, 2, 1, 0])]
                for ui, (kidx, fcol, diag) in enumerate(klist):
                    li = ldc % 2
                    ldc += 1

                    def pre(piece=piece, ui=ui, li=li, kidx=kidx):
                        if ui == 0:
                            S.dma("sp", semqb, [(qb, QbT[:, :, piece * 1024:(piece + 1) * 1024].rearrange("h p c -> p h c"))],
                                  reads=[B_scr], writes=[Bqb])
                        S.dma("sp", semkB[li], [(ktB[li], KbT[:, :, kidx:kidx + 1024].rearrange("h p c -> p h c"))],
                              reads=[B_scr], writes=[BktB[li]])
                        S.dma("sp", semvB[li], [(vtB[li], VbS[kidx // 128:kidx // 128 + 8].rearrange("t p c -> p t c"))],
                              reads=[B_scr], writes=[BvtB[li]])
                    firstU = True
                    for h in range(NH):
                        for qt_ in range(8):
                            kb0 = qt_ if diag else 0
                            blocks = [(128 * b_, 128, vtB[li][:, b_, h * 128:(h + 1) * 128]) for b_ in range(kb0, 8)]
                            U = mk_unit(128, qb[:, h, qt_ * 128:(qt_ + 1) * 128], Bqb, ktB[li][:, h, :], BktB[li], kb0 * 128, 1024, blocks,
                                        BvtB[li], diag, None if fcol is None else flg[:, fcol:fcol + 1],
                                        carry[:, h * 8 + qt_:h * 8 + qt_ + 1], acc[:, h, qt_ * 128:(qt_ + 1) * 128], ui == 0)
                            if firstU:
                                U.pre = pre
                                firstU = False
                            unitsB.append(U)

                def post(piece=piece):
                    S.op("act", lambda e: e.activation(out=obst, in_=acc, func=AF.Copy), reads=[Bacc], writes=[Bobst])
                    S.dma("sp", semob, [(ObT[:, :, piece * 1024:(piece + 1) * 1024].rearrange("h p c -> p h c"), obst)],
                          reads=[Bobst], writes=[B_ObT])
                unitsB[-1].post = post
            kS0 = abf(NH * NS).rearrange("p (h c) -> p h c", h=NH)
            vS0b = abf(1024)
            BkS0 = Buf("kS0")
            semkS0 = S.newsem()
            semck = S.newsem()
            semcvb = [S.newsem(), S.newsem()]
            obs2 = abf(NH * NS).rearrange("p (h c) -> p h c", h=NH)
            Bobs2 = Buf("obs2")

            def pre_s0():
                S.dma("sp", semqb, [(qb[:, :, 0:NS], QbT[:, :, 2048:2080].rearrange("h p c -> p h c"))], reads=[B_scr], writes=[Bqb])
                S.dma("sp", semkS0, [(kS0, KbT[:, :, 9216:9248].rearrange("h p c -> p h c")), (vS0b[0:NS, :], VbS[72, 0:NS, :])],
                      reads=[B_scr], writes=[BkS0])
            for h in range(NH):
                U = mk_unit(NS, qb[:, h, 0:NS], Bqb, kS0[:, h, :], BkS0, 0, NS, [(0, NS, vS0b[0:NS, h * 128:(h + 1) * 128])], BkS0, True, None,
                            carry[:, h:h + 1], acc[:, h, 0:NS], True)
                if h == 0:
                    U.pre = pre_s0
                unitsB.append(U)
            for half in range(2):
                li = half

                def pre_c(half=half, li=li):
                    S.dma("pool", semck, [(obst, cb_k[half * 1024:(half + 1) * 1024, :].rearrange("(t p) c -> p t c", p=128))],
                          writes=[Bobst])
                    S.dma("pool", semcvb[li], [(vtB[li], cb_v[half * 1024:(half + 1) * 1024, :].rearrange("(t p) c -> p t c", p=128))],
                          writes=[BvtB[li]])
                    for h in range(NH):
                        for g2 in range(2):
                            pi = (h * 2 + g2) % 2
                            psb16 = PS[pi][:, :].bitcast(BF16)
                            for t in range(4):
                                tt = g2 * 4 + t
                                S.op("pe", lambda e, psb16=psb16, h=h, t=t, tt=tt: e.transpose(
                                    psb16[:, t * 128:(t + 1) * 128], obst[:, tt, h * 128:(h + 1) * 128], identb[:, :]),
                                    reads=[Bobst, B_const], writes=[PSB[pi]])
                            S.op("dve", lambda e, psb16=psb16, h=h, g2=g2, li=li: e.tensor_copy(
                                out=ktB[li][:, h, g2 * 512:(g2 + 1) * 512], in_=psb16[:, 0:512]), reads=[PSB[pi]], writes=[BktB[li]])
                for h in range(NH):
                    blocks = [(128 * b_, 128, vtB[li][:, b_, h * 128:(h + 1) * 128]) for b_ in range(8)]
                    U = mk_unit(NS, qb[:, h, 0:NS], Bqb, ktB[li][:, h, :], BktB[li], 0, 1024, blocks, BvtB[li], False, None,
                                carry[:, h:h + 1], acc[:, h, 0:NS], False)
                    if h == 0:
                        U.pre = pre_c
                    unitsB.append(U)

            def post_s():
                S.op("act", lambda e: e.activation(out=obs2, in_=acc[:, :, 0:NS], func=AF.Copy), reads=[Bacc], writes=[Bobs2])
                S.dma("sp", semob, [(ObT[:, :, 2048:2080].rearrange("h p c -> p h c"), obs2)], reads=[Bobs2], writes=[B_ObT])
            unitsB[-1].post = post_s
            pipeline(unitsB, [st_z, st_sig2, st_scan, st_sub, st_tr, st_cp, st_av, st_acc])
            S.barrier()
            areset()
            _ph[0] += 1
            if _ph[0] >= STOP:
                break

            wa = abf(NH * D).rearrange("p (k c) -> p k c", k=NH)
            wbm = abf(NH * D).rearrange("p (k c) -> p k c", k=NH)
            Bwab = Buf("wab")
            semwab = S.newsem()
            S.dma("pool", semwab, [(wa[:, :, 0:1024], wview(w_a_out, 0, 1024)), (wa[:, :, 1024:2048], wview(w_a_out, 1024, 1024)),
                                   (wbm[:, :, 0:1024], wview(w_b_out, 0, 1024)), (wbm[:, :, 1024:2048], wview(w_b_out, 1024, 1024))],
                  writes=[Bwab])
            oat = [abf(NH * 512).rearrange("p (h c) -> p h c", h=NH) for _ in range(2)]
            obt = [abf(NH * 512).rearrange("p (h c) -> p h c", h=NH) for _ in range(2)]
            Boab = [Buf("oab0"), Buf("oab1")]
            semoab = [S.newsem(), S.newsem()]
            ga8 = af32(8 * 512).rearrange("p (h c) -> p h c", h=8)
            gb8 = af32(8 * 512).rearrange("p (h c) -> p h c", h=8)
            Bg8 = Buf("g8")
            semg8 = S.newsem()
            t1 = [af32(512) for _ in range(2)]
            t2 = [af32(512) for _ in range(2)]
            Bt12 = [Buf("t12_0"), Buf("t12_1")]
            mst = abf(KC * 512).rearrange("p (k c) -> p k c", k=KC)
            Bmst = Buf("mst")
            semmst = S.newsem()
            B_MT = [Buf("MT%d" % i) for i in range(5)]
            cc5 = 0
            for bi5, (blk, ntk, r0) in enumerate(own_blocks):
                li = bi5 % 2
                S.dma("sp", semoab[li], [(oat[li][:, :, 0:ntk], OaT[:, :, r0:r0 + ntk].rearrange("h p c -> p h c")),
                                         (obt[li][:, :, 0:ntk], ObT[:, :, r0:r0 + ntk].rearrange("h p c -> p h c"))],
                      reads=[B_OaT, B_ObT], writes=[Boab[li]])
                for half in range(2):
                    S.dma("sp", semg8, [(ga8[:, :, 0:ntk], GA[8 * half:8 * half + 8, :, r0:r0 + ntk].rearrange("h p c -> p h c")),
                                        (gb8[:, :, 0:ntk], GB[8 * half:8 * half + 8, :, r0:r0 + ntk].rearrange("h p c -> p h c"))],
                          reads=[B_scr], writes=[Bg8])
                    for cc in range(8):
                        c = 8 * half + cc
                        u = cc5 % 2
                        cc5 += 1
                        pa, pb_ = 2 * u, 2 * u + 1
                        for kc in range(NH):
                            S.op("pe", lambda e, pa=pa, kc=kc, c=c, li=li, ntk=ntk: e.matmul(
                                PS[pa][:, 0:ntk], lhsT=wa[:, kc, c * 128:(c + 1) * 128], rhs=oat[li][:, kc, 0:ntk],
                                start=(kc == 0), stop=(kc == NH - 1)), reads=[Bwab, Boab[li]], writes=[PSB[pa]])
                        for kc in range(NH):
                            S.op("pe", lambda e, pb_=pb_, kc=kc, c=c, li=li, ntk=ntk: e.matmul(
                                PS[pb_][:, 0:ntk], lhsT=wbm[:, kc, c * 128:(c + 1) * 128], rhs=obt[li][:, kc, 0:ntk],
                                start=(kc == 0), stop=(kc == NH - 1)), reads=[Bwab, Boab[li]], writes=[PSB[pb_]])
                        S.op("dve", lambda e, pa=pa, u=u, cc=cc, ntk=ntk: e.tensor_tensor(
                            out=t1[u][:, 0:ntk], in0=PS[pa][:, 0:ntk], in1=ga8[:, cc, 0:ntk], op=ALU.mult),
                            reads=[PSB[pa], Bg8], writes=[Bt12[u]])
                        S.op("dve", lambda e, pb_=pb_, u=u, cc=cc, ntk=ntk: e.tensor_tensor(
                            out=t2[u][:, 0:ntk], in0=PS[pb_][:, 0:ntk], in1=gb8[:, cc, 0:ntk], op=ALU.mult),
                            reads=[PSB[pb_], Bg8], writes=[Bt12[u]])
                        S.op("pool", lambda e, u=u, c=c, ntk=ntk: e.tensor_tensor(
                            out=mst[:, c, 0:ntk], in0=t1[u][:, 0:ntk], in1=t2[u][:, 0:ntk], op=ALU.add),
                            reads=[Bt12[u]], writes=[Bmst])
                S.dma("sp", semmst, [(MT[bi5], mst)], reads=[Bmst], writes=[B_MT[bi5]])
            S.barrier()
            areset()
            _ph[0] += 1
            if _ph[0] >= STOP:
                break

            wos = [abf(KC * 512).rearrange("p (k c) -> p k c", k=KC) for _ in range(2)]
            Bwos = [Buf("wos0"), Buf("wos1")]
            semwos = [S.newsem(), S.newsem()]
            mtb = [abf(KC * 512).rearrange("p (k c) -> p k c", k=KC) for _ in range(2)]
            Bmtb = [Buf("mtb0"), Buf("mtb1")]
            semmtb = [S.newsem(), S.newsem()]
            xc5 = [af32(512) for _ in range(2)]
            Bxc5 = [Buf("xc5_0"), Buf("xc5_1")]
            semxc5 = [S.newsem(), S.newsem()]
            semx1 = [S.newsem(), S.newsem()]
            t5 = [af32(512) for _ in range(2)]
            Bt5 = [Buf("t5_0"), Buf("t5_1")]
            gtb = [af32(D) for _ in range(2)]
            Bgtb = Buf("gtb")
            semgtb = S.newsem()
            S.dma("sp", semgtb, [(gtb[v], bass.AP(adaD.tensor, v * 6 * D + 2 * D, [[0, 128], [1, D]])) for v in range(2)],
                  reads=[B_adaD], writes=[Bgtb])
            B_X1 = [Buf("X1_%d" % i) for i in range(17)]

            def load_w5(s_):
                S.dma("pool", semwos[s_ % 2], [(wos[s_ % 2], wview(w_o, s_ * 512, 512))], writes=[Bwos[s_ % 2]])

            def load_m5(n_):
                S.dma("sp", semmtb[n_ % 2], [(mtb[n_ % 2], MT[n_ % 5])], reads=[B_MT[n_ % 5]], writes=[Bmtb[n_ % 2]])
            load_w5(0)
            load_m5(0)
            l5 = 0
            c5 = 0
            for s5 in range(4):
                wi = s5 % 2
                if s5 + 1 < 4:
                    load_w5(s5 + 1)
                tix = 0
                for bi5, (blk, ntk, r0) in enumerate(own_blocks):
                    li = l5 % 2
                    if l5 + 1 < 20:
                        load_m5(l5 + 1)
                    l5 += 1
                    v = 0 if blk < NBLK else 1
                    nt = 4 if ntk == 512 else 1
                    for j in range(nt):
                        ntok = 128 if ntk == 512 else NS
                        u = c5 % 2
                        pi = c5 % 4
                        c5 += 1
                        src = xin[blk, j * 128:(j + 1) * 128, s5 * 512:(s5 + 1) * 512] if blk < NBLK else xs_in[:, s5 * 512:(s5 + 1) * 512]
                        S.dma("sp", semxc5[u], [(xc5[u][0:ntok, :], src)], writes=[Bxc5[u]])
                        for kc in range(KC):
                            S.op("pe", lambda e, pi=pi, kc=kc, li=li, j=j, ntok=ntok, wi=wi: e.matmul(
                                PS[pi][0:ntok, :], lhsT=mtb[li][:, kc, j * 128:j * 128 + ntok], rhs=wos[wi][:, kc, :],
                                start=(kc == 0), stop=(kc == KC - 1)), reads=[Bwos[wi], Bmtb[li]], writes=[PSB[pi]])
                        S.op("dve", lambda e, pi=pi, u=u, v=v, ntok=ntok, s5=s5: e.tensor_tensor(
                            out=t5[u][0:ntok, :], in0=PS[pi][0:ntok, :], in1=gtb[v][0:ntok, s5 * 512:(s5 + 1) * 512], op=ALU.mult),
                            reads=[PSB[pi], Bgtb], writes=[Bt5[u]])
                        S.op("pool", lambda e, u=u, ntok=ntok: e.tensor_tensor(
                            out=xc5[u][0:ntok, :], in0=xc5[u][0:ntok, :], in1=t5[u][0:ntok, :], op=ALU.add),
                            reads=[Bt5[u], Bxc5[u]], writes=[Bxc5[u]])
                        S.dma("sp", semx1[u], [(X1[tix, 0:ntok, s5 * 512:(s5 + 1) * 512], xc5[u][0:ntok, :])],
                              reads=[Bxc5[u]], writes=[B_X1[tix]])
                        tix += 1
            S.barrier()
            areset()
            B_H2T = [Buf("H2T%d" % i) for i in range(5)]
            tiles5 = []
            tix = 0
            for bi5, (blk, ntk, r0) in enumerate(own_blocks):
                nt = 4 if ntk == 512 else 1
                for j in range(nt):
                    ntok = 128 if ntk == 512 else NS
                    tiles5.append((X1[tix, 0:ntok, :], [B_X1[tix]], ntok, 0 if blk < NBLK else 1, bi5, j, j == nt - 1))
                    tix += 1

            def store5(bi5, st, bst, sem):
                S.dma("sp", sem, [(H2T[bi5], st)], reads=[bst], writes=[B_H2T[bi5]])
            norm_pass(tiles5, 4, 3, 1, store5, None)
            S.barrier()
            areset()
            _ph[0] += 1
            if _ph[0] >= STOP:
                break

            wgu = [abf(KC * 1024).rearrange("p (k c) -> p k c", k=KC) for _ in range(2)]
            Bwgu = [Buf("wgu0"), Buf("wgu1")]
            semwgu = [S.newsem(), S.newsem()]
            h2b = [abf(KC * 512).rearrange("p (k c) -> p k c", k=KC) for _ in range(2)]
            Bh2b = [Buf("h2b0"), Buf("h2b1")]
            semh2b = [S.newsem(), S.newsem()]
            sg = [af32(512) for _ in range(2)]
            Bsg = [Buf("sg0"), Buf("sg1")]
            ast = [abf(4 * 512).rearrange("p (k c) -> p k c", k=4) for _ in range(2)]
            Bast = [Buf("ast0"), Buf("ast1")]
            semast = [S.newsem(), S.newsem()]
            B_ActT = Buf("ActT")
            c6 = 0
            l6 = 0
            a6 = 0
            def load_w6(g_):
                S.dma("pool", semwgu[g_ % 2], [(wgu[g_ % 2][:, :, 0:512], wview(w_gu, g_ * 512, 512)),
                                               (wgu[g_ % 2][:, :, 512:1024], wview(w_gu, DFF + g_ * 512, 512))], writes=[Bwgu[g_ % 2]])

            def load_h6(n_):
                S.dma("sp", semh2b[n_ % 2], [(h2b[n_ % 2], H2T[n_ % 5])], reads=[B_H2T[n_ % 5]], writes=[Bh2b[n_ % 2]])
            load_w6(0)
            load_h6(0)
            for g in range(11):
                wi = g % 2
                if g + 1 < 11:
                    load_w6(g + 1)
                for bi5, (blk, ntk, r0) in enumerate(own_blocks):
                    li = l6 % 2
                    if l6 + 1 < 55:
                        load_h6(l6 + 1)
                    l6 += 1
                    ai = a6 % 2
                    a6 += 1
                    for cc in range(4):
                        u = c6 % 2
                        c6 += 1
                        pg, pu = 2 * u, 2 * u + 1
                        for kc in range(KC):
                            S.op("pe", lambda e, pg=pg, kc=kc, cc=cc, wi=wi, li=li, ntk=ntk: e.matmul(
                                PS[pg][:, 0:ntk], lhsT=wgu[wi][:, kc, cc * 128:(cc + 1) * 128], rhs=h2b[li][:, kc, 0:ntk],
                                start=(kc == 0), stop=(kc == KC - 1)), reads=[Bwgu[wi], Bh2b[li]], writes=[PSB[pg]])
                        for kc in range(KC):
                            S.op("pe", lambda e, pu=pu, kc=kc, cc=cc, wi=wi, li=li, ntk=ntk: e.matmul(
                                PS[pu][:, 0:ntk], lhsT=wgu[wi][:, kc, 512 + cc * 128:512 + (cc + 1) * 128], rhs=h2b[li][:, kc, 0:ntk],
                                start=(kc == 0), stop=(kc == KC - 1)), reads=[Bwgu[wi], Bh2b[li]], writes=[PSB[pu]])
                        S.op("act", lambda e, pg=pg, u=u, ntk=ntk: e.activation(out=sg[u][:, 0:ntk], in_=PS[pg][:, 0:ntk], func=AF.Silu),
                             reads=[PSB[pg]], writes=[Bsg[u]])
                        S.op("dve", lambda e, pu=pu, u=u, ai=ai, cc=cc, ntk=ntk: e.tensor_tensor(
                            out=ast[ai][:, cc, 0:ntk], in0=PS[pu][:, 0:ntk], in1=sg[u][:, 0:ntk], op=ALU.mult),
                            reads=[PSB[pu], Bsg[u]], writes=[Bast[ai]])
                    S.dma("sp", semast[ai], [(ActT[2 * bi5 + hf_, :, 4 * g:4 * g + 4, 0:min(256, ntk - 256 * hf_)],
                                              ast[ai][:, :, 256 * hf_:256 * hf_ + min(256, ntk - 256 * hf_)])
                                             for hf_ in range(2) if ntk > 256 * hf_],
                          reads=[Bast[ai]], writes=[B_ActT])
            S.barrier()
            areset()
            _ph[0] += 1
            if _ph[0] >= STOP:
                break

            NSL = 4
            SW = D // NSL
            wd = [abf(FC * SW).rearrange("p (k c) -> p k c", k=FC) for _ in range(2)]
            Bwd = [Buf("wd0"), Buf("wd1")]
            semwd = [S.newsem(), S.newsem()]
            actb = [abf(FC * 256).rearrange("p (k c) -> p k c", k=FC) for _ in range(2)]
            Bactb = [Buf("actb0"), Buf("actb1")]
            semactb = [S.newsem(), S.newsem()]
            x1c = [af32(SW) for _ in range(2)]
            Bx1c = [Buf("x1c0"), Buf("x1c1")]
            semx1c = [S.newsem(), S.newsem()]
            semx2 = [S.newsem(), S.newsem()]
            t7 = [af32(SW) for _ in range(2)]
            Bt7 = [Buf("t7_0"), Buf("t7_1")]
            gt2b = [af32(D) for _ in range(2)]
            Bgt2 = Buf("gt2")
            semgt2 = S.newsem()
            S.dma("sp", semgt2, [(gt2b[v], bass.AP(adaD.tensor, v * 6 * D + 5 * D, [[0, 128], [1, D]])) for v in range(2)],
                  reads=[B_adaD], writes=[Bgt2])
            B_X2 = [Buf("X2_%d" % i) for i in range(17)]
            hbl = []
            for bi5 in range(4):
                for half in range(2):
                    hbl.append((2 * bi5 + half, 256, bi5 * 4 + half * 2, 2, 128, 0))
            hbl.append((8, NS, 16, 1, NS, 1))
            NHB = len(hbl)

            def load_w7(s_):
                S.dma("pool", semwd[s_ % 2], [(wd[s_ % 2][:, 0:22, :], wview(w_dn, s_ * SW, SW)[:, 0:22, :]),
                                              (wd[s_ % 2][:, 22:44, :], wview(w_dn, s_ * SW, SW)[:, 22:44, :])], writes=[Bwd[s_ % 2]])

            def load_a7(n_):
                ai_, w_, _, _, _, _ = hbl[n_ % NHB]
                S.dma("sp", semactb[n_ % 2], [(actb[n_ % 2][:, :, 0:w_], ActT[ai_, :, :, 0:w_])], reads=[B_ActT], writes=[Bactb[n_ % 2]])
            load_w7(0)
            load_a7(0)
            l7 = 0
            c7 = 0
            for s7 in range(NSL):
                wi = s7 % 2
                if s7 + 1 < NSL:
                    load_w7(s7 + 1)
                for (ai_, w_, tix0, ntl, ntok, v) in hbl:
                    li = l7 % 2
                    if l7 + 1 < NSL * NHB:
                        load_a7(l7 + 1)
                    l7 += 1
                    for j in range(ntl):
                        tix = tix0 + j
                        u = c7 % 2
                        pi = c7 % 4
                        c7 += 1
                        S.dma("sp", semx1c[u], [(x1c[u][0:ntok, :], X1[tix, 0:ntok, s7 * SW:(s7 + 1) * SW])],
                              reads=[B_X1[tix]], writes=[Bx1c[u]])
                        for kc in range(FC):
                            S.op("pe", lambda e, pi=pi, kc=kc, li=li, j=j, ntok=ntok, wi=wi: e.matmul(
                                PS[pi][0:ntok, 0:SW], lhsT=actb[li][:, kc, j * 128:j * 128 + ntok], rhs=wd[wi][:, kc, :],
                                start=(kc == 0), stop=(kc == FC - 1)), reads=[Bwd[wi], Bactb[li]], writes=[PSB[pi]])
                        S.op("dve", lambda e, pi=pi, u=u, v=v, ntok=ntok, s7=s7: e.tensor_tensor(
                            out=t7[u][0:ntok, :], in0=PS[pi][0:ntok, 0:SW], in1=gt2b[v][0:ntok, s7 * SW:(s7 + 1) * SW], op=ALU.mult),
                            reads=[PSB[pi], Bgt2], writes=[Bt7[u]])
                        S.op("pool", lambda e, u=u, ntok=ntok: e.tensor_tensor(
                            out=x1c[u][0:ntok, :], in0=x1c[u][0:ntok, :], in1=t7[u][0:ntok, :], op=ALU.add),
                            reads=[Bt7[u], Bx1c[u]], writes=[Bx1c[u]])
                        S.dma("sp", semx2[u], [(X2[tix, 0:ntok, s7 * SW:(s7 + 1) * SW], x1c[u][0:ntok, :])],
                              reads=[Bx1c[u]], writes=[B_X2[tix]])
            S.barrier()
            areset()
            _ph[0] += 1
            if _ph[0] >= STOP:
                break

            gfb = af32(D)
            Bgfb = Buf("gfb")
            semgfb = S.newsem()
            S.dma("sp", semgfb, [(gfb, bass.AP(gfin.tensor, 0, [[0, 128], [1, D]]))], writes=[Bgfb])
            x8 = [af32(D) for _ in range(2)]
            Bx8 = [Buf("x8_0"), Buf("x8_1")]
            semx8 = [S.newsem(), S.newsem()]
            y8 = [af32(D) for _ in range(2)]
            By8 = [Buf("y8_0"), Buf("y8_1")]
            semy8 = [S.newsem(), S.newsem()]
            Bsm8 = [Buf("sm8_0"), Buf("sm8_1")]
            for tix in range(17):
                u = tix % 2
                ntok = 128 if tix < 16 else NS
                S.dma("sp", semx8[u], [(x8[u][0:ntok, :], X2[tix, 0:ntok, :])], reads=[B_X2[tix]], writes=[Bx8[u]])
                ssq = small[:, 48 + 2 * u:49 + 2 * u]
                rstd = small[:, 49 + 2 * u:50 + 2 * u]
                S.op("act", lambda e, u=u, ntok=ntok, ssq=ssq: e.activation(out=y8[u][0:ntok, :], in_=x8[u][0:ntok, :], func=AF.Square,
                                                                            accum_out=ssq[0:ntok, :]), reads=[Bx8[u]], writes=[By8[u], Bsm8[u]])
                S.op("act", lambda e, ntok=ntok, ssq=ssq, rstd=rstd: e.activation(out=rstd[0:ntok, :], in_=ssq[0:ntok, :], func=AF.Sqrt,
                                                                                  scale=1.0 / D, bias=small[0:ntok, 63:64]),
                     reads=[Bsm8[u], B_const], writes=[Bsm8[u]])
                S.op("dve", lambda e, ntok=ntok, rstd=rstd: e.reciprocal(out=rstd[0:ntok, :], in_=rstd[0:ntok, :]),
                     reads=[Bsm8[u]], writes=[Bsm8[u]])
                S.op("act", lambda e, u=u, ntok=ntok, rstd=rstd: e.activation(out=y8[u][0:ntok, :], in_=x8[u][0:ntok, :], func=AF.Identity,
                                                                              scale=rstd[0:ntok, :]), reads=[Bx8[u], Bsm8[u]], writes=[By8[u]])
                S.op("dve", lambda e, u=u, ntok=ntok: e.tensor_tensor(out=y8[u][0:ntok, :], in0=y8[u][0:ntok, :], in1=gfb[0:ntok, :],
                                                                      op=ALU.mult), reads=[By8[u], Bgfb], writes=[By8[u]])
                S.dma("sp", semy8[u], [(y_o[tix * 128:tix * 128 + ntok, :], y8[u][0:ntok, :])], reads=[By8[u]], writes=[])
            S.barrier()

        with nc.Block() as block:
            @block.tensor
            def _(e):
                S.replay("pe", e)

            @block.scalar
            def _(e):
                S.replay("act", e)

            @block.vector
            def _(e):
                S.replay("dve", e)

            @block.gpsimd
            def _(e):
                S.replay("pool", e)

            @block.sync
            def _(e):
                S.replay("sp", e)
    return nc


_NC_CACHE = {}


def _prep_inputs(x_prompt, x_sample, c_prompt, c_sample, cache_a_k, cache_a_v, cache_b_k, cache_b_v,
                 w_ada, b_ada, g_mix, w_in, rel_bias, w_a_out, w_b_out, w_o, g_ffn, w_gate_up, w_down, g_final):
    f = lambda a: np.ascontiguousarray(np.asarray(a, dtype=np.float32))
    shared = {
        "w_ada": f(w_ada[0]), "b_ada": f(b_ada[0]).reshape(1, -1), "w_in": f(w_in[0]), "relb": f(rel_bias[0]),
        "w_a_out": f(w_a_out[0]), "w_b_out": f(w_b_out[0]), "w_o": f(w_o[0]), "w_gu": f(w_gate_up[0]),
        "w_dn": f(w_down[0]), "gfin": f(g_final).reshape(1, -1),
    }
    gcols = np.stack([f(g_mix[0]).reshape(KC, 128).T, f(g_ffn[0]).reshape(KC, 128).T], axis=1)
    shared["gcols"] = np.ascontiguousarray(gcols)
    shared["grow"] = np.ascontiguousarray(np.stack([f(g_mix[0]), f(g_ffn[0])], axis=0))
    xp = np.asarray(x_prompt, dtype=np.float32)
    xs = np.asarray(x_sample, dtype=np.float32)
    maps = []
    for c in range(8):
        b, r = c // 4, c % 4
        xr = xp[b, ::-1]
        def piece(p):
            return xr[1024 * (7 - p):1024 * (8 - p)]
        def halo(p):
            if p == 0:
                return xr[0:512]
            return xr[1024 * (8 - p):1024 * (8 - p) + 512]
        lo, hi = r, 7 - r
        blocks = [piece(lo)[:512], piece(lo)[512:], piece(hi)[:512], piece(hi)[512:], halo(lo), halo(hi)]
        for s in range(7):
            p = s if s <= 6 - r else 0
            blocks += [piece(p)[:512], piece(p)[512:]]
        xin = np.ascontiguousarray(np.stack(blocks, axis=0))
        cv = np.stack([np.asarray(c_prompt[b], np.float32).reshape(KC, 128).T,
                       np.asarray(c_sample[c], np.float32).reshape(KC, 128).T], axis=2)
        flags = np.zeros((128, 16), np.float32)
        flags[:, 0] = -BIG if r == 0 else 0.0
        for i, s in enumerate([2, 1, 0]):
            flags[:, 1 + i] = 0.0 if s <= r - 1 else BIG
        for i, s in enumerate([6, 5, 4, 3, 2, 1, 0]):
            flags[:, 4 + i] = 0.0 if s <= 6 - r else BIG
        m = dict(shared)
        m.update({
            "xin": xin, "xs": np.ascontiguousarray(xs[c, ::-1]), "cvec": np.ascontiguousarray(cv), "flags": flags,
            "ca_k": np.ascontiguousarray(np.asarray(cache_a_k[0, c], np.float32)[::-1].reshape(512, 1024)),
            "ca_v": np.ascontiguousarray(np.asarray(cache_a_v[0, c], np.float32)[::-1].reshape(512, 1024)),
            "cb_k": np.ascontiguousarray(np.asarray(cache_b_k[0, c], np.float32)[::-1].reshape(2048, 1024)),
            "cb_v": np.ascontiguousarray(np.asarray(cache_b_v[0, c], np.float32)[::-1].reshape(2048, 1024)),
        })
        maps.append(m)
    return maps


def kernel(**inputs):
    maps = _prep_inputs(**inputs)
    if "nc" not in _NC_CACHE:
        _NC_CACHE["nc"] = build_program()
    nc = _NC_CACHE["nc"]
    res = run_bass_kernel_spmd(nc, maps, core_ids=list(range(8)))
    R = res.results
    y_p = np.zeros((2, 8192, D), np.float32)
    y_s = np.zeros((8, NS, D), np.float32)
    ak_p = np.zeros((1, 2, 512, NH, HD), np.float32)
    av_p = np.zeros((1, 2, 512, NH, HD), np.float32)
    bk_p = np.zeros((1, 2, 8192, NH, HD), np.float32)
    bv_p = np.zeros((1, 2, 8192, NH, HD), np.float32)
    ak_s = np.zeros((1, 8, NS, NH, HD), np.float32)
    av_s = np.zeros((1, 8, NS, NH, HD), np.float32)
    bk_s = np.zeros((1, 8, NS, NH, HD), np.float32)
    bv_s = np.zeros((1, 8, NS, NH, HD), np.float32)
    for c in range(8):
        b, r = c // 4, c % 4
        o = R[c]
        for pi, p in enumerate((r, 7 - r)):
            sl = slice(1024 * p, 1024 * (p + 1))
            rows = slice(1024 * pi, 1024 * (pi + 1))
            y_p[b, sl] = o["y"][rows][::-1]
            bk_p[0, b, sl] = o["kb_o"][rows][::-1].reshape(1024, NH, HD)
            bv_p[0, b, sl] = o["vb_o"][rows][::-1].reshape(1024, NH, HD)
        if r == 0:
            ak_p[0, b] = o["ka_o"][0:512][::-1].reshape(512, NH, HD)
            av_p[0, b] = o["va_o"][0:512][::-1].reshape(512, NH, HD)
        y_s[c] = o["y"][2048:2080][::-1]
        ak_s[0, c] = o["ka_o"][512:544][::-1].reshape(NS, NH, HD)
        av_s[0, c] = o["va_o"][512:544][::-1].reshape(NS, NH, HD)
        bk_s[0, c] = o["kb_o"][2048:2080][::-1].reshape(NS, NH, HD)
        bv_s[0, c] = o["vb_o"][2048:2080][::-1].reshape(NS, NH, HD)
    return (y_p, y_s, ak_p, av_p, bk_p, bv_p, ak_s, av_s, bk_s, bv_s)
```

```python
import numpy as np
from contextlib import ExitStack
import concourse.bass as bass
import concourse.mybir as mybir
from concourse.bass_utils import run_bass_kernel_spmd

F32 = mybir.dt.float32
BF16 = mybir.dt.bfloat16
AF = mybir.ActivationFunctionType
ALU = mybir.AluOpType
AX = mybir.AxisListType

D = 2048
KC = 16
NH = 8
HD = 128
DFF = 5632
FC = 44
NOWN = 2080
NS = 32
SCALE = HD ** -0.5
EPS = 1e-6
BIG = 30000.0
NBLK = 20
NAKEY = 3104
NBKEY = 9248
LT = 767


class Sem:
    def __init__(self, h):
        self.h = h
        self.count = 0


class Buf:
    __slots__ = ("name", "w", "r", "excl")

    def __init__(self, name="", excl=False):
        self.name = name
        self.w = None
        self.r = {}
        self.excl = excl


class Sched:
    ENG = ("pe", "act", "dve", "pool", "sp")

    def __init__(self, nc, es):
        self.nc = nc
        self.es = es
        self.q = {e: [] for e in self.ENG}
        self.esem = {e: Sem(es.enter_context(nc.semaphore("s_" + e))) for e in self.ENG}
        self.seen = {e: {} for e in self.ENG}
        self.dsems = []
        self.nsem = 0

    def newsem(self):
        self.nsem += 1
        s = Sem(self.es.enter_context(self.nc.semaphore("d%d" % self.nsem)))
        self.dsems.append(s)
        return s

    def _waits(self, eng, reads, writes):
        need = {}

        def add(s, n):
            if need.get(s, 0) < n:
                need[s] = n
        for b in reads:
            if b.w is not None:
                add(*b.w)
        for b in writes:
            if b.w is not None:
                add(*b.w)
            for s, n in b.r.items():
                add(s, n)
        out = []
        seen = self.seen[eng]
        for s, n in need.items():
            if eng == "pe" and s is self.esem["pe"]:
                continue
            if seen.get(s, 0) >= n:
                continue
            seen[s] = n
            out.append((s, n))
        return out

    def _commit(self, t, reads, writes):
        s, n = t
        for b in reads:
            if b.r.get(s, 0) < n:
                b.r[s] = n
        for b in writes:
            b.w = t
            b.r = {}

    def op(self, eng, fn, reads=(), writes=()):
        if any(b.excl for b in reads):
            writes = list(writes) + [b for b in reads if b.excl and b not in writes]
            reads = [b for b in reads if not b.excl]
        waits = self._waits(eng, reads, writes)
        s = self.esem[eng]
        s.count += 1
        t = (s, s.count)
        self.q[eng].append((waits, fn, s, 1))
        self._commit(t, reads, writes)
        return t

    def dma(self, eng, sem, pairs, reads=(), writes=(), transpose=False):
        waits = self._waits(eng, reads, writes)
        if sem.count > 0 and self.seen[eng].get(sem, 0) < sem.count:
            self.seen[eng][sem] = sem.count
            waits = waits + [(sem, sem.count)]
        sem.count += 16 * len(pairs)
        t = (sem, sem.count)
        for i, (o, i_) in enumerate(pairs):
            if transpose:
                self.q[eng].append((waits if i == 0 else [], (lambda e, o=o, i_=i_: e.dma_start_transpose(out=o, in_=i_)), sem, 16))
            else:
                self.q[eng].append((waits if i == 0 else [], (lambda e, o=o, i_=i_: e.dma_start(out=o, in_=i_)), sem, 16))
        self._commit(t, reads, writes)
        return t

    def barrier(self):
        allw = [(s, s.count) for s in list(self.esem.values()) + self.dsems if s.count > 0]
        for e in self.ENG:
            seen = self.seen[e]
            w = []
            for s, n in allw:
                if seen.get(s, 0) < n:
                    seen[s] = n
                    w.append((s, n))
            if w:
                self.q[e].append((w, None, None, 0))

    def replay(self, name, e):
        for waits, fn, s, inc in self.q[name]:
            for ws, n in waits:
                e.wait_ge(ws.h, n)
            if fn is not None:
                fn(e).then_inc(s.h, inc)


def build_program(STOP=99):
    nc = bass.Bass("TRN2", target_bir_lowering=False)

    def din(name, shape, dt=F32):
        return nc.dram_tensor(name, list(shape), dt, kind="ExternalInput").ap()

    def dout(name, shape):
        return nc.dram_tensor(name, list(shape), F32, kind="ExternalOutput").ap()

    def dscr(name, shape, dt):
        return nc.dram_tensor(name, list(shape), dt, kind="Internal").ap()

    xin = din("xin", [NBLK, 512, D])
    xs_in = din("xs", [NS, D])
    cvec = din("cvec", [128, KC, 2])
    gcols = din("gcols", [128, 2, KC])
    grow = din("grow", [2, D])
    gfin = din("gfin", [1, D])
    flags_in = din("flags", [128, 16])
    w_ada = din("w_ada", [D, 6 * D])
    b_ada = din("b_ada", [1, 6 * D])
    w_in = din("w_in", [D, 10240])
    relb = din("relb", [NH, 257])
    w_a_out = din("w_a_out", [1024, D])
    w_b_out = din("w_b_out", [1024, D])
    w_o = din("w_o", [D, D])
    w_gu = din("w_gu", [D, 2 * DFF])
    w_dn = din("w_dn", [DFF, D])
    ca_k = din("ca_k", [512, 1024])
    ca_v = din("ca_v", [512, 1024])
    cb_k = din("cb_k", [2048, 1024])
    cb_v = din("cb_v", [2048, 1024])

    y_o = dout("y", [NOWN, D])
    ka_o = dout("ka_o", [544, 1024])
    va_o = dout("va_o", [544, 1024])
    kb_o = dout("kb_o", [NOWN, 1024])
    vb_o = dout("vb_o", [NOWN, 1024])

    HT = dscr("HT", [NBLK + 1, 128, KC, 512], BF16)
    adaD = dscr("adaD", [2, 6 * D], F32)
    GT = dscr("GT", [NH, 128, LT], F32)
    gD = dscr("gD", [NH, LT], F32)
    QaT = dscr("QaT", [NH, 128, NOWN], BF16)
    QbT = dscr("QbT", [NH, 128, NOWN], BF16)
    KaT = dscr("KaT", [NH, 128, NAKEY], BF16)
    KbT = dscr("KbT", [NH, 128, NBKEY], BF16)
    VaS = dscr("VaS", [25, 128, 1024], BF16)
    VbS = dscr("VbS", [73, 128, 1024], BF16)
    GA = dscr("GA", [KC, 128, NOWN], F32)
    GB = dscr("GB", [KC, 128, NOWN], F32)
    OaT = dscr("OaT", [NH, 128, NOWN], BF16)
    ObT = dscr("ObT", [NH, 128, NOWN], BF16)
    MT = dscr("MT", [5, 128, KC, 512], BF16)
    X1 = dscr("X1", [17, 128, D], F32)
    H2T = dscr("H2T", [5, 128, KC, 512], BF16)
    ActT = dscr("ActT", [9, 128, FC, 256], BF16)
    X2 = dscr("X2", [17, 128, D], F32)

    es = ExitStack()
    with es:
        S = Sched(nc, es)
        ARW = 45056
        arena = es.enter_context(nc.sbuf_tensor("arena", [128, ARW], F32))
        identf = es.enter_context(nc.sbuf_tensor("identf", [128, 128], F32))
        identb = es.enter_context(nc.sbuf_tensor("identb", [128, 128], BF16))
        maskL = es.enter_context(nc.sbuf_tensor("maskL", [128, 128], F32))
        Tb = es.enter_context(nc.sbuf_tensor("Tb", [128, NH, 640], F32))
        modc = es.enter_context(nc.sbuf_tensor("modc", [128, 2, 4, KC], F32))
        flg = es.enter_context(nc.sbuf_tensor("flg", [128, 16], F32))
        small = es.enter_context(nc.sbuf_tensor("small", [128, 64], F32))
        PS = [es.enter_context(nc.psum_tensor("ps%d" % i, [128, 512], F32)) for i in range(8)]
        PSB = [Buf("ps%d" % i, excl=True) for i in range(8)]
        B_const = Buf("const")

        apos = [0]

        def areset():
            apos[0] = 0

        def af32(n):
            o = apos[0]
            apos[0] += n
            assert apos[0] <= ARW, apos[0]
            return arena[:, o:o + n]

        def abf(n):
            assert n % 2 == 0
            return af32(n // 2).bitcast(BF16)

        def wview(w, c0, nc_, kc=KC):
            return w.rearrange("(kc p) c -> p kc c", p=128)[:, :, c0:c0 + nc_]

        _ph = [0]
        for _once in (0,):
            sem_c = S.newsem()
            S.op("pool", lambda e: e.memset(identf[:], 0.0), writes=[B_const])
            S.op("pool", lambda e: e.affine_select(out=identf[:], in_=identf[:], pattern=[[-1, 128]],
                                                   compare_op=ALU.not_equal, fill=1.0, base=0,
                                                   channel_multiplier=1), writes=[B_const])
            S.op("pool", lambda e: e.tensor_copy(out=identb[:], in_=identf[:]), reads=[B_const], writes=[B_const])
            S.op("pool", lambda e: e.memset(maskL[:], 0.0), writes=[B_const])
            S.op("pool", lambda e: e.affine_select(out=maskL[:], in_=maskL[:], pattern=[[1, 128]], compare_op=ALU.is_gt,
                                                   fill=1.0, base=0, channel_multiplier=-1), writes=[B_const])
            S.op("pool", lambda e: e.memset(small[:, 63:64], EPS), writes=[B_const])
            S.dma("sp", sem_c, [(flg[:], flags_in[:, :])], writes=[B_const])

            B_g = Buf("g")
            B_GT = Buf("GT")
            gs = af32(LT)
            S.dma("sp", sem_c, [(gs[0:NH, 0:256], relb[:, 1:257])], writes=[B_g])
            S.op("dve", lambda e: e.tensor_copy(out=gs[0:NH, 256:LT], in_=gs[0:NH, 255:256].to_broadcast([NH, LT - 256])),
                 reads=[B_g], writes=[B_g])
            sem_g = S.newsem()
            B_gD = Buf("gD")
            S.dma("sp", sem_g, [(gD[:, :], gs[0:NH, :])], reads=[B_g], writes=[B_gD])
            sem_g2 = S.newsem()
            S.dma("sp", sem_g2, [(GT[h], bass.AP(gD.tensor, h * LT, [[0, 128], [1, LT]])) for h in range(NH)],
                  reads=[B_gD], writes=[B_GT])
            B_T = Buf("T")
            S.dma("sp", sem_c, [(Tb[:, h, :], bass.AP(GT.tensor, h * 128 * LT + 127, [[LT - 1, 128], [1, 640]]))
                                for h in range(NH)], reads=[B_GT], writes=[B_T])
            S.op("pool", lambda e: e.memset(Tb[0:64, :, 576:640], -BIG), reads=[B_T], writes=[B_T])
            S.op("pool", lambda e: e.memset(Tb[64:128, :, 0:64], -BIG), reads=[B_T], writes=[B_T])
            S.barrier()
            areset()
            _ph[0] += 1
            if _ph[0] >= STOP:
                break

            cv = af32(32).rearrange("p (k v) -> p k v", v=2)
            cs = af32(32).rearrange("p (k v) -> p k v", v=2)
            sc_b = abf(32).rearrange("p (k v) -> p k v", v=2)
            gc = af32(32).rearrange("p (g k) -> p g k", g=2)
            adaR = af32(6 * D)
            b2 = af32(6 * D)
            slabs = [abf(KC * 512).rearrange("p (k c) -> p k c", k=KC) for _ in range(2)]
            B_slab = [Buf("slab0"), Buf("slab1")]
            sem_slab = [S.newsem(), S.newsem()]
            B_cv = Buf("cv")
            B_adaR = Buf("adaR")
            B_b2 = Buf("b2")
            B_adaD = Buf("adaD")
            B_modc = Buf("modc")
            sem_m = S.newsem()
            S.dma("sp", sem_m, [(cv, cvec[:, :, :]), (gc, gcols[:, :, :])], writes=[B_cv])
            S.dma("sp", sem_m, [(b2[0:1, :], b_ada[0:1, :]), (b2[1:2, :], b_ada[0:1, :])], writes=[B_b2])
            S.op("act", lambda e: e.activation(out=cs, in_=cv, func=AF.Sigmoid), reads=[B_cv], writes=[B_cv])
            S.op("dve", lambda e: e.tensor_tensor(out=sc_b, in0=cv, in1=cs, op=ALU.mult), reads=[B_cv], writes=[B_cv])
            for gi in range(24):
                sl = slabs[gi % 2]
                bs = B_slab[gi % 2]
                S.dma("pool", sem_slab[gi % 2], [(sl, wview(w_ada, gi * 512, 512))], writes=[bs])
                pb = PSB[gi % 2]
                ps = PS[gi % 2]
                for kc in range(KC):
                    S.op("pe", lambda e, ps=ps, sl=sl, kc=kc: e.matmul(ps[0:2, :], lhsT=sc_b[:, kc, :], rhs=sl[:, kc, :],
                                                                         start=(kc == 0), stop=(kc == KC - 1)),
                         reads=[B_cv, bs], writes=[pb])
                S.op("dve", lambda e, ps=ps, gi=gi: e.tensor_tensor(out=adaR[0:2, gi * 512:(gi + 1) * 512], in0=ps[0:2, :],
                                                                   in1=b2[0:2, gi * 512:(gi + 1) * 512], op=ALU.add),
                     reads=[pb, B_b2], writes=[B_adaR])
            S.dma("sp", sem_m, [(adaD[:, :], adaR[0:2, :])], reads=[B_adaR], writes=[B_adaD])
            adaC = af32(192).rearrange("p (c v) -> p c v", v=2)
            B_adaC = Buf("adaC")
            for c in range(96):
                S.op("pe", lambda e, c=c: e.transpose(PS[2][:, 2 * c:2 * c + 2], adaR[0:2, c * 128:(c + 1) * 128], identf[0:2, 0:2]),
                     reads=[B_adaR, B_const], writes=[PSB[2]])
            S.op("dve", lambda e: e.tensor_copy(out=adaC, in_=PS[2][:, 0:192].rearrange("p (c v) -> p c v", v=2)),
                 reads=[PSB[2]], writes=[B_adaC])
            for v in range(2):
                for (j, jsc, jsh, g) in ((0, 1, 0, 0), (2, 4, 3, 1)):
                    S.op("dve", lambda e, v=v, j=j, jsc=jsc, g=g: e.scalar_tensor_tensor(
                        out=modc[:, v, j, :], in0=adaC[:, jsc * 16:(jsc + 1) * 16, v], scalar=1.0, in1=gc[:, g, :],
                        op0=ALU.add, op1=ALU.mult), reads=[B_adaC, B_cv], writes=[B_modc])
                    S.op("dve", lambda e, v=v, j=j, jsh=jsh: e.tensor_copy(out=modc[:, v, j + 1, :],
                                                                            in_=adaC[:, jsh * 16:(jsh + 1) * 16, v]),
                         reads=[B_adaC], writes=[B_modc])
            S.barrier()
            areset()
            _ph[0] += 1
            if _ph[0] >= STOP:
                break

            def norm_tile(xt, bx, ntok, v, jm, hTdst, bh, junk, xn, bxn, ssq, rstd, bsm, psbase):
                S.op("act", lambda e: e.activation(out=junk[0:ntok, :], in_=xt[0:ntok, :], func=AF.Square,
                                                   accum_out=ssq[0:ntok, :]), reads=[bx], writes=[bxn, bsm])
                S.op("act", lambda e: e.activation(out=rstd[0:ntok, :], in_=ssq[0:ntok, :], func=AF.Sqrt,
                                                   scale=1.0 / D, bias=small[0:ntok, 63:64]), reads=[bsm, B_const], writes=[bsm])
                S.op("dve", lambda e: e.reciprocal(out=rstd[0:ntok, :], in_=rstd[0:ntok, :]), reads=[bsm], writes=[bsm])
                S.op("act", lambda e: e.activation(out=xn[0:ntok, :], in_=xt[0:ntok, :], func=AF.Identity,
                                                   scale=rstd[0:ntok, :]), reads=[bx, bsm], writes=[bxn])
                for q4 in range(4):
                    pi = psbase + q4
                    for i in range(4):
                        kc = q4 * 4 + i
                        S.op("pe", lambda e, pi=pi, i=i, kc=kc: e.transpose(PS[pi][:, i * 128:i * 128 + ntok],
                                                                              xn[0:ntok, kc * 128:(kc + 1) * 128],
                                                                              identf[0:ntok, 0:ntok]),
                             reads=[bxn, B_const], writes=[PSB[pi]])
                    for i in range(4):
                        kc = q4 * 4 + i
                        if q4 % 2 == 0:
                            S.op("dve", lambda e, pi=pi, i=i, kc=kc: e.tensor_scalar(
                                out=hTdst[:, kc, 0:ntok], in0=PS[pi][:, i * 128:i * 128 + ntok],
                                scalar1=modc[:, v, jm, kc:kc + 1], scalar2=modc[:, v, jm + 1, kc:kc + 1],
                                op0=ALU.mult, op1=ALU.add), reads=[PSB[pi], B_modc], writes=[bh])
                        else:
                            S.op("act", lambda e, pi=pi, i=i, kc=kc: e.activation(
                                out=hTdst[:, kc, 0:ntok], in_=PS[pi][:, i * 128:i * 128 + ntok], func=AF.Identity,
                                scale=modc[:, v, jm, kc:kc + 1], bias=modc[:, v, jm + 1, kc:kc + 1]),
                                reads=[PSB[pi], B_modc], writes=[bh])

            def pipeline(units, stages):
                n = len(units)
                K = len(stages)
                for i, U in enumerate(units):
                    U.idx = i
                for t in range(n + K - 1):
                    for k in range(K):
                        i = t - k
                        if 0 <= i < n:
                            U = units[i]
                            if k == 0 and U.pre is not None:
                                U.pre()
                            stages[k](U)
                            if k == K - 1 and U.post is not None:
                                U.post()

            class NU:
                pass

            def norm_pass(tiles, sc_off, sh_off, gidx, store_blk, dep_bufs):
                xt = [af32(D) for _ in range(3)]
                Bx = [Buf("x0"), Buf("x1"), Buf("x2")]
                semx = [S.newsem(), S.newsem(), S.newsem()]
                tb = [af32(D) for _ in range(2)]
                Bt = [Buf("t0"), Buf("t1")]
                junk = af32(D)
                Bjunk = Buf("junk")
                hb16 = [abf(D) for _ in range(2)]
                Bhb16 = [Buf("hb16_0"), Buf("hb16_1")]
                hst = [abf(KC * 512).rearrange("p (k c) -> p k c", k=KC) for _ in range(2)]
                Bh = [Buf("h0"), Buf("h1")]
                semh = [S.newsem(), S.newsem()]
                Bsm = [Buf("sm0"), Buf("sm1"), Buf("sm2"), Buf("sm3")]
                Ab = [af32(D) for _ in range(2)]
                Sb = [af32(D) for _ in range(2)]
                gtmp = af32(D)
                Bmodb = Buf("modb")
                semmb = S.newsem()
                S.dma("sp", semmb, [(gtmp, bass.AP(grow.tensor, gidx * D, [[0, 128], [1, D]]))] +
                      [(Ab[v], bass.AP(adaD.tensor, v * 6 * D + sc_off * D, [[0, 128], [1, D]])) for v in range(2)] +
                      [(Sb[v], bass.AP(adaD.tensor, v * 6 * D + sh_off * D, [[0, 128], [1, D]])) for v in range(2)],
                      reads=[B_adaD], writes=[Bmodb])
                for v in range(2):
                    S.op("dve", lambda e, v=v: e.scalar_tensor_tensor(out=Ab[v], in0=Ab[v], scalar=1.0, in1=gtmp, op0=ALU.add, op1=ALU.mult),
                         reads=[Bmodb], writes=[Bmodb])

                def s0(U):
                    xb, sb = U.idx % 3, U.idx % 4
                    ntok = U.ntok
                    S.dma("sp", semx[xb], [(xt[xb][0:ntok, :], U.src)], reads=U.srcbufs, writes=[Bx[xb]])
                    S.op("act", lambda e: e.activation(out=junk[0:ntok, :], in_=xt[xb][0:ntok, :], func=AF.Square,
                                                       accum_out=small[0:ntok, 2 * sb:2 * sb + 1]), reads=[Bx[xb]], writes=[Bjunk, Bsm[sb]])

                def s1(U):
                    xb, sb, s4 = U.idx % 3, U.idx % 2, U.idx % 4
                    ntok, v = U.ntok, U.v
                    rstd = small[:, 2 * s4 + 1:2 * s4 + 2]
                    S.op("act", lambda e: e.activation(out=rstd[0:ntok, :], in_=small[0:ntok, 2 * s4:2 * s4 + 1], func=AF.Sqrt,
                                                       scale=1.0 / D, bias=small[0:ntok, 63:64]), reads=[Bsm[s4], B_const], writes=[Bsm[s4]])
                    S.op("dve", lambda e: e.reciprocal(out=rstd[0:ntok, :], in_=rstd[0:ntok, :]), reads=[Bsm[s4]], writes=[Bsm[s4]])
                    S.op("dve", lambda e: e.scalar_tensor_tensor(out=tb[sb][0:ntok, :], in0=xt[xb][0:ntok, :], scalar=rstd[0:ntok, :],
                                                                 in1=Ab[v][0:ntok, :], op0=ALU.mult, op1=ALU.mult),
                         reads=[Bx[xb], Bsm[s4], Bmodb], writes=[Bt[sb]])
                    S.op("pool", lambda e: e.tensor_tensor(out=hb16[sb][0:ntok, :], in0=tb[sb][0:ntok, :], in1=Sb[v][0:ntok, :], op=ALU.add),
                         reads=[Bt[sb], Bmodb], writes=[Bhb16[sb]])

                def s2(U):
                    sb = U.idx % 2
                    ntok = U.ntok
                    for q4 in range(4):
                        pi = 4 * sb + q4
                        for i in range(4):
                            kc = q4 * 4 + i
                            S.op("pe", lambda e, pi=pi, i=i, kc=kc: e.matmul(PS[pi][:, i * 128:i * 128 + ntok],
                                                                             lhsT=hb16[sb][0:ntok, kc * 128:(kc + 1) * 128],
                                                                             rhs=identb[0:ntok, 0:ntok], start=True, stop=True),
                                 reads=[Bhb16[sb], B_const], writes=[PSB[pi]])

                def s3(U):
                    sb = U.idx % 2
                    ntok = U.ntok
                    for q4 in range(4):
                        pi = 4 * sb + q4
                        src = PS[pi][:, :].rearrange("p (k c) -> p k c", k=4)[:, :, 0:ntok]
                        dst = U.hTdst[:, q4 * 4:(q4 + 1) * 4, 0:ntok]
                        if q4 % 2 == 0:
                            S.op("act", lambda e, src=src, dst=dst: e.activation(out=dst, in_=src, func=AF.Copy), reads=[PSB[pi]], writes=[U.bh])
                        else:
                            S.op("dve", lambda e, src=src, dst=dst: e.tensor_copy(out=dst, in_=src), reads=[PSB[pi]], writes=[U.bh])

                units = []
                for (src, srcbufs, ntok, v, blk, j, last) in tiles:
                    U = NU()
                    U.pre = None
                    U.post = None
                    U.src, U.srcbufs, U.ntok, U.v = src, srcbufs, ntok, v
                    hb = blk % 2
                    U.hTdst = hst[hb][:, :, j * 128:(j + 1) * 128]
                    U.bh = Bh[hb]
                    if last:
                        def post(blk=blk, hb=hb):
                            store_blk(blk, hst[hb], Bh[hb], semh[hb])
                        U.post = post
                    units.append(U)
                pipeline(units, [s0, s1, s2, s3])

            B_HT = [Buf("HT%d" % i) for i in range(NBLK + 1)]
            tiles1 = []
            for blk in range(NBLK + 1):
                nt = 4 if blk < NBLK else 1
                for j in range(nt):
                    src = xin[blk, j * 128:(j + 1) * 128, :] if blk < NBLK else xs_in[:, :]
                    tiles1.append((src, [], 128 if blk < NBLK else NS, 0 if blk < NBLK else 1, blk, j, j == nt - 1))

            def store1(blk, st, bst, sem):
                S.dma("sp", sem, [(HT[blk], st)], reads=[bst], writes=[B_HT[blk]])
            norm_pass(tiles1, 1, 0, 0, store1, None)
            S.barrier()
            areset()
            _ph[0] += 1
            if _ph[0] >= STOP:
                break

            wg = [abf(KC * 1024).rearrange("p (k c) -> p k c", k=KC) for _ in range(2)]
            Bwg = [Buf("wg0"), Buf("wg1")]
            semw = [S.newsem(), S.newsem()]
            hbk = [abf(KC * 512).rearrange("p (k c) -> p k c", k=KC) for _ in range(2)]
            Bhb = [Buf("hb0"), Buf("hb1")]
            semhb = [S.newsem(), S.newsem()]
            kvf = [af32(1024) for _ in range(2)]
            Bkvf = [Buf("kvf0"), Buf("kvf1")]
            semkvf = [S.newsem(), S.newsem()]
            kvb = [abf(1024) for _ in range(2)]
            Bkvb = [Buf("kvb0"), Buf("kvb1")]
            semkvb = [S.newsem(), S.newsem()]
            ktb = [abf(NH * 512).rearrange("p (h c) -> p h c", h=NH) for _ in range(2)]
            Bktb = [Buf("ktb0"), Buf("ktb1")]
            semktb = [S.newsem(), S.newsem()]
            qst = [abf(NH * 512).rearrange("p (h c) -> p h c", h=NH) for _ in range(2)]
            Bqst = [Buf("qst0"), Buf("qst1")]
            semqst = [S.newsem(), S.newsem()]
            gst = [af32(NH * 512).rearrange("p (h c) -> p h c", h=NH) for _ in range(2)]
            Bgst = [Buf("gst0"), Buf("gst1")]
            semgst = [S.newsem(), S.newsem()]
            B_scr = Buf("scr_w")

            own_blocks = [(0, 512, 0), (1, 512, 512), (2, 512, 1024), (3, 512, 1536), (NBLK, NS, 2048)]
            a_blocks = [(0, 512, 0), (1, 512, 512), (4, 512, 1024), (2, 512, 1536), (3, 512, 2048), (5, 512, 2560), (NBLK, NS, 3072)]
            b_blocks = [(0, 512, 0), (1, 512, 512), (2, 512, 1024), (3, 512, 1536)] + \
                       [(6 + i, 512, 2048 + 512 * i) for i in range(14)] + [(NBLK, NS, 9216)]
            own_row = {0: 0, 1: 512, 2: 1024, 3: 1536, NBLK: 2048}
            a_out_row = {2: 0, NBLK: 512}
            groups = [("q", 0, QaT), ("k", 1024, "a"), ("v", 2048, "a"), ("q", 3072, QbT), ("k", 4096, "b"), ("v", 5120, "b"),
                      ("g", 6144, (GA, 0)), ("g", 7168, (GA, 8)), ("g", 8192, (GB, 0)), ("g", 9216, (GB, 8))]
            cnt = {"hb": 0, "kv": 0, "kt": 0, "q": 0, "g": 0, "ps": 0}
            def blist_of(kind, dst):
                if kind in ("q", "g"):
                    return own_blocks
                return a_blocks if dst == "a" else b_blocks

            def load_w2(gi):
                c0_ = groups[gi][1]
                W_ = wg[gi % 2]
                S.dma("pool", semw[gi % 2], [(W_[:, :, 0:512], wview(w_in, c0_, 512)), (W_[:, :, 512:1024], wview(w_in, c0_ + 512, 512))],
                      writes=[Bwg[gi % 2]])
            seq2 = [blk for (kind, c0, dst) in groups for (blk, ntk, idx0) in blist_of(kind, dst)]

            def load_h2(n):
                S.dma("sp", semhb[n % 2], [(hbk[n % 2], HT[seq2[n]])], reads=[B_HT[seq2[n]]], writes=[Bhb[n % 2]])
            load_w2(0)
            load_h2(0)
            pendk = [None]
            for gi, (kind, c0, dst) in enumerate(groups):
                wb = gi % 2
                W = wg[wb]
                if gi + 1 < len(groups):
                    load_w2(gi + 1)
                blist = blist_of(kind, dst)
                for (blk, ntk, idx0) in blist:
                    hi_ = cnt["hb"] % 2
                    hb_ = hbk[hi_]
                    if cnt["hb"] + 1 < len(seq2):
                        load_h2(cnt["hb"] + 1)
                    cnt["hb"] += 1
                    if kind == "q":
                        qi = cnt["q"] % 2
                        cnt["q"] += 1
                        for cc in range(8):
                            pi = cnt["ps"] % 8
                            cnt["ps"] += 1
                            for kc in range(KC):
                                S.op("pe", lambda e, pi=pi, cc=cc, kc=kc, W=W, hb_=hb_, ntk=ntk: e.matmul(
                                    PS[pi][:, 0:ntk], lhsT=W[:, kc, cc * 128:(cc + 1) * 128], rhs=hb_[:, kc, 0:ntk],
                                    start=(kc == 0), stop=(kc == KC - 1)), reads=[Bwg[wb], Bhb[hi_]], writes=[PSB[pi]])
                            eng = "act" if cc % 2 == 0 else "dve"
                            if eng == "act":
                                S.op("act", lambda e, pi=pi, cc=cc, qi=qi, ntk=ntk: e.activation(
                                    out=qst[qi][:, cc, 0:ntk], in_=PS[pi][:, 0:ntk], func=AF.Copy), reads=[PSB[pi]], writes=[Bqst[qi]])
                            else:
                                S.op("dve", lambda e, pi=pi, cc=cc, qi=qi, ntk=ntk: e.tensor_copy(
                                    out=qst[qi][:, cc, 0:ntk], in_=PS[pi][:, 0:ntk]), reads=[PSB[pi]], writes=[Bqst[qi]])
                        r0 = own_row[blk]
                        S.dma("sp", semqst[qi], [(dst[:, :, r0:r0 + ntk].rearrange("h p c -> p h c"), qst[qi][:, :, 0:ntk])],
                              reads=[Bqst[qi]], writes=[B_scr])
                    elif kind == "g":
                        G, ch0 = dst
                        qi = cnt["g"] % 2
                        cnt["g"] += 1
                        for cc in range(8):
                            pi = cnt["ps"] % 8
                            cnt["ps"] += 1
                            for kc in range(KC):
                                S.op("pe", lambda e, pi=pi, cc=cc, kc=kc, W=W, hb_=hb_, ntk=ntk: e.matmul(
                                    PS[pi][:, 0:ntk], lhsT=W[:, kc, cc * 128:(cc + 1) * 128], rhs=hb_[:, kc, 0:ntk],
                                    start=(kc == 0), stop=(kc == KC - 1)), reads=[Bwg[wb], Bhb[hi_]], writes=[PSB[pi]])
                            S.op("act", lambda e, pi=pi, cc=cc, qi=qi, ntk=ntk: e.activation(
                                out=gst[qi][:, cc, 0:ntk], in_=PS[pi][:, 0:ntk], func=AF.Sigmoid), reads=[PSB[pi]], writes=[Bgst[qi]])
                        r0 = own_row[blk]
                        S.dma("sp", semgst[qi], [(G[ch0:ch0 + 8, :, r0:r0 + ntk].rearrange("h p c -> p h c"), gst[qi][:, :, 0:ntk])],
                              reads=[Bgst[qi]], writes=[B_scr])
                    else:
                        isA = (dst == "a")
                        KT_, VS_ = (KaT, VaS) if isA else (KbT, VbS)
                        nt = 4 if ntk == 512 else 1
                        ki = cnt["kt"] % 2
                        if kind == "k":
                            cnt["kt"] += 1
                        for j in range(nt):
                            ntok = 128 if ntk == 512 else NS
                            fi = cnt["kv"] % 2
                            cnt["kv"] += 1
                            for half in range(2):
                                pi = cnt["ps"] % 8
                                cnt["ps"] += 1
                                for kc in range(KC):
                                    S.op("pe", lambda e, pi=pi, half=half, kc=kc, W=W, hb_=hb_, j=j, ntok=ntok: e.matmul(
                                        PS[pi][0:ntok, :], lhsT=hb_[:, kc, j * 128:j * 128 + ntok], rhs=W[:, kc, half * 512:(half + 1) * 512],
                                        start=(kc == 0), stop=(kc == KC - 1)), reads=[Bwg[wb], Bhb[hi_]], writes=[PSB[pi]])
                                S.op("act", lambda e, pi=pi, half=half, fi=fi, ntok=ntok: e.activation(
                                    out=kvf[fi][0:ntok, half * 512:(half + 1) * 512], in_=PS[pi][0:ntok, :], func=AF.Copy),
                                    reads=[PSB[pi]], writes=[Bkvf[fi]])
                                S.op("pool", lambda e, half=half, fi=fi, ntok=ntok: e.tensor_copy(
                                    out=kvb[fi][0:ntok, half * 512:(half + 1) * 512],
                                    in_=kvf[fi][0:ntok, half * 512:(half + 1) * 512]),
                                    reads=[Bkvf[fi]], writes=[Bkvb[fi]])
                            outs = []
                            if isA and blk in a_out_row:
                                o_t = ka_o if kind == "k" else va_o
                                r0 = a_out_row[blk] + j * 128
                                outs.append((o_t[r0:r0 + ntok, :], kvf[fi][0:ntok, :]))
                            if (not isA) and blk in own_row:
                                o_t = kb_o if kind == "k" else vb_o
                                r0 = own_row[blk] + j * 128
                                outs.append((o_t[r0:r0 + ntok, :], kvf[fi][0:ntok, :]))
                            if outs:
                                S.dma("sp", semkvf[fi], outs, reads=[Bkvf[fi]], writes=[])
                            if kind == "v":
                                tix = (idx0 + j * 128) // 128
                                S.dma("sp", semkvb[fi], [(VS_[tix, 0:ntok, :], kvb[fi][0:ntok, :])], reads=[Bkvb[fi]], writes=[B_scr])
                            else:
                                def ktrans(fi=fi, ntok=ntok, ki=ki, j=j):
                                    pi = cnt["ps"] % 8
                                    cnt["ps"] += 1
                                    psb16 = PS[pi][:, :].bitcast(BF16)
                                    for h in range(NH):
                                        S.op("pe", lambda e, psb16=psb16, h=h, fi=fi, ntok=ntok: e.transpose(
                                            psb16[:, h * 128:h * 128 + ntok], kvb[fi][0:ntok, h * 128:(h + 1) * 128], identb[0:ntok, 0:ntok]),
                                            reads=[Bkvb[fi], B_const], writes=[PSB[pi]])
                                    S.op("dve", lambda e, psb16=psb16, ki=ki, j=j, ntok=ntok: e.tensor_copy(
                                        out=ktb[ki][:, :, j * 128:j * 128 + ntok],
                                        in_=psb16.rearrange("p (h c) -> p h c", h=NH)[:, :, 0:ntok]), reads=[PSB[pi]], writes=[Bktb[ki]])
                                if pendk[0] is not None:
                                    pendk[0]()
                                pendk[0] = ktrans
                        if kind == "k" and pendk[0] is not None:
                            pendk[0]()
                            pendk[0] = None
                        if kind == "k":
                            S.dma("sp", semktb[ki], [(KT_[:, :, idx0:idx0 + ntk].rearrange("h p c -> p h c"), ktb[ki][:, :, 0:ntk])],
                                  reads=[Bktb[ki]], writes=[B_scr])
            S.barrier()
            areset()
            _ph[0] += 1
            if _ph[0] >= STOP:
                break

            def sm(i):
                return small[:, i:i + 1]
            qtA = [abf(NH * 128).rearrange("p (h c) -> p h c", h=NH) for _ in range(2)]
            ktA = [abf(NH * 640).rearrange("p (h c) -> p h c", h=NH) for _ in range(2)]
            vtA = [abf(5 * 1024).rearrange("p (t c) -> p t c", t=5) for _ in range(2)]
            BldA = [Buf("ldA0"), Buf("ldA1")]
            semldA = [S.newsem(), S.newsem()]
            s_sb = [af32(640) for _ in range(2)]
            Bs = [Buf("s0"), Buf("s1")]
            p_sb = [abf(640) for _ in range(2)]
            Bp = [Buf("p0"), Buf("p1")]
            pT = [abf(5 * 128).rearrange("p (t c) -> p t c", t=5) for _ in range(2)]
            BpT = [Buf("pT0"), Buf("pT1")]
            oa = [abf(1024) for _ in range(3)]
            Boa = [Buf("oa0"), Buf("oa1"), Buf("oa2")]
            oaTs = [abf(NH * 512).rearrange("p (h c) -> p h c", h=NH) for _ in range(2)]
            BoaT = [Buf("oaT0"), Buf("oaT1")]
            semoaT = [S.newsem(), S.newsem()]
            Bst = [Buf("st%d" % i) for i in range(8)]
            B_OaT = Buf("OaT")
            ckA = abf(4 * 1024).rearrange("p (t c) -> p t c", t=4)
            cvA = abf(4 * 1024).rearrange("p (t c) -> p t c", t=4)
            ktS = abf(NH * 544).rearrange("p (h c) -> p h c", h=NH)
            qtS = abf(NH * NS).rearrange("p (h c) -> p h c", h=NH)
            vS0 = abf(1024)
            BckA = Buf("ckA")
            BcvA = Buf("cvA")
            BktS = Buf("ktS")
            semcA = S.newsem()
            semcA2 = S.newsem()

            class AU:
                pass

            def mkA(nq, hsel, q_ap, k_ap, nk, vblocks, bq, bk, bv, flag_c0, oa_t, boa):
                U = AU()
                U.nq, U.hsel, U.q_ap, U.k_ap, U.nk, U.vblocks = nq, hsel, q_ap, k_ap, nk, vblocks
                U.bq, U.bk, U.bv, U.flag_c0, U.oa_t, U.boa = bq, bk, bv, flag_c0, oa_t, boa
                U.pre = None
                U.post = None
                return U

            def a0(U):
                u = U.idx % 2
                nq, nk = U.nq, U.nk
                n1 = min(nk, 512)
                S.op("pe", lambda e: e.matmul(PS[2 * u][0:nq, 0:n1], lhsT=U.q_ap, rhs=U.k_ap[:, 0:n1], start=True, stop=True),
                     reads=[U.bq, U.bk], writes=[PSB[2 * u]])
                if nk > 512:
                    S.op("pe", lambda e: e.matmul(PS[2 * u + 1][0:nq, 0:nk - 512], lhsT=U.q_ap, rhs=U.k_ap[:, 512:nk], start=True, stop=True),
                         reads=[U.bq, U.bk], writes=[PSB[2 * u + 1]])

            def a1(U):
                u = U.idx % 2
                nq, nk, hsel = U.nq, U.nk, U.hsel
                n1 = min(nk, 512)
                s_ = s_sb[u]
                st = 8 + 4 * (U.idx % 8)
                bst = Bst[U.idx % 8]
                S.op("dve", lambda e: e.scalar_tensor_tensor(out=s_[0:nq, 0:n1], in0=PS[2 * u][0:nq, 0:n1], scalar=SCALE,
                                                             in1=Tb[0:nq, hsel, 0:n1], op0=ALU.mult, op1=ALU.add),
                     reads=[PSB[2 * u], B_T], writes=[Bs[u]])
                if nk > 512:
                    S.op("dve", lambda e: e.scalar_tensor_tensor(out=s_[0:nq, 512:nk], in0=PS[2 * u + 1][0:nq, 0:nk - 512], scalar=SCALE,
                                                                 in1=Tb[0:nq, hsel, 512:nk], op0=ALU.mult, op1=ALU.add),
                         reads=[PSB[2 * u + 1], B_T], writes=[Bs[u]])
                if U.flag_c0 is not None:
                    fc = U.flag_c0
                    S.op("dve", lambda e: e.tensor_scalar(out=s_[0:nq, fc:nk], in0=s_[0:nq, fc:nk], scalar1=flg[0:nq, 0:1],
                                                          scalar2=None, op0=ALU.add), reads=[Bs[u], B_const], writes=[Bs[u]])
                S.op("dve", lambda e: e.reduce_max(out=sm(st)[0:nq, :], in_=s_[0:nq, 0:nk], axis=AX.X), reads=[Bs[u]], writes=[bst])
                S.op("dve", lambda e: e.tensor_scalar(out=sm(st + 1)[0:nq, :], in0=sm(st)[0:nq, :], scalar1=-1.0, scalar2=None,
                                                      op0=ALU.mult), reads=[bst], writes=[bst])

            def a2(U):
                u = U.idx % 2
                nq, nk = U.nq, U.nk
                st = 8 + 4 * (U.idx % 8)
                bst = Bst[U.idx % 8]
                S.op("act", lambda e: e.activation(out=p_sb[u][0:nq, 0:nk], in_=s_sb[u][0:nq, 0:nk], func=AF.Exp, bias=sm(st + 1)[0:nq, :],
                                                   scale=1.0, accum_out=sm(st + 2)[0:nq, :]), reads=[Bs[u], bst], writes=[Bp[u], bst])

            def a3(U):
                u = U.idx % 2
                nq = U.nq
                st = 8 + 4 * (U.idx % 8)
                bst = Bst[U.idx % 8]
                S.op("dve", lambda e: e.reciprocal(out=sm(st + 3)[0:nq, :], in_=sm(st + 2)[0:nq, :]), reads=[bst], writes=[bst])
                psb16 = PS[4 + u][:, :].bitcast(BF16)
                for bi, (koff, nkb, v_ap) in enumerate(U.vblocks):
                    S.op("pe", lambda e, bi=bi, koff=koff, nkb=nkb: e.transpose(psb16[0:nkb, bi * 128:bi * 128 + nq],
                                                                                 p_sb[u][0:nq, koff:koff + nkb], identb[0:nq, 0:nq]),
                         reads=[Bp[u], B_const], writes=[PSB[4 + u]])

            def a4(U):
                u = U.idx % 2
                nq = U.nq
                nb = len(U.vblocks)
                psb16 = PS[4 + u][:, :].bitcast(BF16)
                S.op("act", lambda e: e.activation(out=pT[u][:, 0:nb, 0:nq],
                                                   in_=psb16[:, 0:nb * 128].rearrange("p (t c) -> p t c", t=nb)[:, :, 0:nq],
                                                   func=AF.Copy), reads=[PSB[4 + u]], writes=[BpT[u]])

            def a5(U):
                u = U.idx % 2
                nq = U.nq
                nb = len(U.vblocks)
                for bi, (koff, nkb, v_ap) in enumerate(U.vblocks):
                    S.op("pe", lambda e, bi=bi, nkb=nkb, v_ap=v_ap: e.matmul(PS[6 + u][0:nq, 0:128], lhsT=pT[u][0:nkb, bi, 0:nq], rhs=v_ap,
                                                                             start=(bi == 0), stop=(bi == nb - 1)),
                         reads=[BpT[u], U.bv], writes=[PSB[6 + u]])

            def a6(U):
                u = U.idx % 2
                nq, hsel = U.nq, U.hsel
                st = 8 + 4 * (U.idx % 8)
                bst = Bst[U.idx % 8]
                S.op("act", lambda e: e.activation(out=U.oa_t[0:nq, hsel * 128:(hsel + 1) * 128], in_=PS[6 + u][0:nq, 0:128],
                                                   func=AF.Identity, scale=sm(st + 3)[0:nq, :]), reads=[PSB[6 + u], bst], writes=[U.boa])

            afc = [0]

            def a_finish(nq, oa_t, boa, stage, bstage, col0):
                pi = 5
                psb16 = PS[pi][:, :].bitcast(BF16)
                for h in range(NH):
                    S.op("pe", lambda e, h=h: e.transpose(psb16[:, h * 128:h * 128 + nq], oa_t[0:nq, h * 128:(h + 1) * 128],
                                                           identb[0:nq, 0:nq]), reads=[boa, B_const], writes=[PSB[pi]])
                S.op("dve", lambda e: e.tensor_copy(out=stage[:, :, col0:col0 + nq],
                                                    in_=psb16.rearrange("p (h c) -> p h c", h=NH)[:, :, 0:nq]),
                     reads=[PSB[pi]], writes=[bstage])

            unitsA = []
            pc = 0
            for piece in range(2):
                for j in range(8):
                    li = pc % 2
                    oi = pc % 3
                    sti = (pc // 4) % 2
                    tok0 = piece * 1024 + 128 * j
                    k0 = piece * 1536 + 128 * j
                    t0_ = k0 // 128
                    flag_c0 = (1024 - 128 * j) if (piece == 0 and j >= 4) else None

                    def preA(li=li, tok0=tok0, k0=k0, t0_=t0_):
                        S.dma("sp", semldA[li], [
                            (qtA[li], QaT[:, :, tok0:tok0 + 128].rearrange("h p c -> p h c")),
                            (ktA[li], KaT[:, :, k0:k0 + 640].rearrange("h p c -> p h c")),
                            (vtA[li], VaS[t0_:t0_ + 5].rearrange("t p c -> p t c"))], reads=[B_scr], writes=[BldA[li]])

                    def postA(pc=pc, oi=oi, sti=sti):
                        a_finish(128, oa[oi], Boa[oi], oaTs[sti], BoaT[sti], (pc % 4) * 128)
                        if pc % 4 == 3:
                            r0 = (pc // 4) * 512
                            S.dma("sp", semoaT[sti], [(OaT[:, :, r0:r0 + 512].rearrange("h p c -> p h c"), oaTs[sti])],
                                  reads=[BoaT[sti]], writes=[B_OaT])
                    for h in range(NH):
                        vbl = [(128 * t, 128, vtA[li][:, t, h * 128:(h + 1) * 128]) for t in range(5)]
                        U = mkA(128, h, qtA[li][:, h, :], ktA[li][:, h, :], 640, vbl, BldA[li], BldA[li], BldA[li], flag_c0, oa[oi], Boa[oi])
                        if h == 0:
                            U.pre = preA
                        if h == NH - 1:
                            U.post = postA
                        unitsA.append(U)
                    pc += 1
            oiS = pc % 3

            def preS():
                S.dma("pool", semcA, [(ckA, ca_k.rearrange("(t p) c -> p t c", p=128)), (cvA, ca_v.rearrange("(t p) c -> p t c", p=128))],
                      writes=[BckA, BcvA])
                S.dma("sp", semcA2, [(qtS, QaT[:, :, 2048:2080].rearrange("h p c -> p h c")),
                                     (ktS[:, :, 0:NS], KaT[:, :, 3072:3104].rearrange("h p c -> p h c")),
                                     (vS0[0:NS, :], VaS[24, 0:NS, :])], reads=[B_scr], writes=[BktS])
                for h in range(NH):
                    pi = h % 2
                    psb16 = PS[pi][:, :].bitcast(BF16)
                    for t in range(4):
                        S.op("pe", lambda e, psb16=psb16, h=h, t=t: e.transpose(psb16[:, t * 128:(t + 1) * 128], ckA[:, t, h * 128:(h + 1) * 128],
                                                                                  identb[:, :]), reads=[BckA, B_const], writes=[PSB[pi]])
                    S.op("dve", lambda e, psb16=psb16, h=h: e.tensor_copy(out=ktS[:, h, NS:NS + 512], in_=psb16[:, 0:512]),
                         reads=[PSB[pi]], writes=[BktS])

            def postS():
                a_finish(NS, oa[oiS], Boa[oiS], oaTs[0], BoaT[0], 0)
                S.dma("sp", semoaT[0], [(OaT[:, :, 2048:2080].rearrange("h p c -> p h c"), oaTs[0][:, :, 0:NS])],
                      reads=[BoaT[0]], writes=[B_OaT])
            for h in range(NH):
                vbl = [(0, NS, vS0[0:NS, h * 128:(h + 1) * 128])] + \
                      [(NS + 128 * t, 128, cvA[:, t, h * 128:(h + 1) * 128]) for t in range(4)]
                U = mkA(NS, h, qtS[:, h, :], ktS[:, h, :], 544, vbl, BktS, BktS, BcvA, None, oa[oiS], Boa[oiS])
                if h == 0:
                    U.pre = preS
                if h == NH - 1:
                    U.post = postS
                unitsA.append(U)
            pipeline(unitsA, [a0, a1, a2, a3, a4, a5, a6])
            S.barrier()
            areset()
            _ph[0] += 1
            if _ph[0] >= STOP:
                break

            qb = abf(NH * 1024).rearrange("p (h c) -> p h c", h=NH)
            Bqb = Buf("qb")
            semqb = S.newsem()
            ktB = [abf(NH * 1024).rearrange("p (h c) -> p h c", h=NH) for _ in range(2)]
            vtB = [abf(8 * 1024).rearrange("p (t c) -> p t c", t=8) for _ in range(2)]
            BktB = [Buf("ktB0"), Buf("ktB1")]
            BvtB = [Buf("vtB0"), Buf("vtB1")]
            semkB = [S.newsem(), S.newsem()]
            semvB = [S.newsem(), S.newsem()]
            acc = af32(NH * 1024).rearrange("p (h c) -> p h c", h=NH)
            Bacc = Buf("acc")
            carry = af32(64)
            Bcar = Buf("carry")
            m_sb = [af32(1024) for _ in range(2)]
            Bm = [Buf("m0"), Buf("m1")]
            Pb = [af32(1026) for _ in range(3)]
            BPb = [Buf("P0"), Buf("P1"), Buf("P2")]
            A_sb = [abf(1024) for _ in range(2)]
            BA = [Buf("A0"), Buf("A1")]
            AT_sb = [abf(8 * 128).rearrange("p (t c) -> p t c", t=8) for _ in range(2)]
            BAT = [Buf("AT0"), Buf("AT1")]
            obst = abf(NH * 1024).rearrange("p (h c) -> p h c", h=NH)
            Bobst = Buf("obst")
            semob = S.newsem()
            B_ObT = Buf("ObT")

            class BU:
                pass

            def mk_unit(nq, q_ap, bq, k_ap, bk, c0, c1, blocks, bv, diag, bias_ap, car_ap, acc_ap, first):
                U = BU()
                U.nq, U.q_ap, U.bq, U.k_ap, U.bk, U.c0, U.c1 = nq, q_ap, bq, k_ap, bk, c0, c1
                U.blocks, U.bv, U.diag, U.bias_ap, U.car_ap, U.acc_ap, U.first = blocks, bv, diag, bias_ap, car_ap, acc_ap, first
                U.pre = None
                U.post = None
                ch = []
                c = c0
                while c < c1:
                    w = min(512, c1 - c)
                    ch.append((c, w))
                    c += w
                U.chunks = ch
                return U

            def st_z(U):
                u = U.idx % 2
                zb = [2 * u, 2 * u + 1]
                for ci, (cc, w) in enumerate(U.chunks):
                    S.op("pe", lambda e, ci=ci, cc=cc, w=w: e.matmul(PS[zb[ci]][0:U.nq, 0:w], lhsT=U.q_ap, rhs=U.k_ap[:, cc:cc + w],
                                                                     start=True, stop=True), reads=[U.bq, U.bk], writes=[PSB[zb[ci]]])

            def st_sig(U):
                u = U.idx % 2
                zb = [2 * u, 2 * u + 1]
                m_ = m_sb[u]
                nq = U.nq
                for ci, (cc, w) in enumerate(U.chunks):
                    if U.bias_ap is None:
                        S.op("act", lambda e, ci=ci, cc=cc, w=w: e.activation(out=m_[0:nq, cc:cc + w], in_=PS[zb[ci]][0:nq, 0:w],
                                                                              func=AF.Sigmoid, scale=-SCALE),
                             reads=[PSB[zb[ci]]], writes=[Bm[u]])
                    else:
                        S.op("act", lambda e, ci=ci, cc=cc, w=w: e.activation(out=m_[0:nq, cc:cc + w], in_=PS[zb[ci]][0:nq, 0:w],
                                                                              func=AF.Sigmoid, scale=-SCALE, bias=U.bias_ap[0:nq, :]),
                             reads=[PSB[zb[ci]], B_const], writes=[Bm[u]])

            def st_pinit(U):
                u3 = U.idx % 3
                P_ = Pb[u3]
                nq, c0 = U.nq, U.c0
                if U.first:
                    S.op("pool", lambda e: e.memset(P_[0:nq, c0:c0 + 1], 1.0), writes=[BPb[u3]])
                else:
                    S.op("pool", lambda e: e.tensor_copy(out=P_[0:nq, c0:c0 + 1], in_=U.car_ap[0:nq, :]), reads=[Bcar], writes=[BPb[u3]])

            def st_sig2(U):
                st_sig(U)
                st_pinit(U)

            def st_scan(U):
                u = U.idx % 2
                u3 = U.idx % 3
                m_, P_ = m_sb[u], Pb[u3]
                nq, c0, c1 = U.nq, U.c0, U.c1
                if U.diag:
                    nkb0 = U.blocks[0][1]
                    S.op("dve", lambda e: e.tensor_tensor(out=m_[0:nq, c0:c0 + nkb0], in0=m_[0:nq, c0:c0 + nkb0],
                                                          in1=maskL[0:nq, 0:nkb0], op=ALU.max),
                         reads=[Bm[u], B_const], writes=[Bm[u]])
                S.op("dve", lambda e: e.tensor_tensor_scan(out=P_[0:nq, c0 + 1:c1 + 1], data0=m_[0:nq, c0:c1], data1=m_[0:nq, c0:c1],
                                                           initial=P_[0:nq, c0:c0 + 1], op0=ALU.mult, op1=ALU.bypass),
                     reads=[Bm[u], BPb[u3]], writes=[BPb[u3]])

            def st_sub(U):
                u = U.idx % 2
                u3 = U.idx % 3
                P_, A_ = Pb[u3], A_sb[u]
                nq, c0, c1 = U.nq, U.c0, U.c1
                S.op("pool", lambda e: e.tensor_tensor(out=A_[0:nq, c0:c1], in0=P_[0:nq, c0:c1], in1=P_[0:nq, c0 + 1:c1 + 1],
                                                       op=ALU.subtract), reads=[BPb[u3]], writes=[BA[u]])
                S.op("pool", lambda e: e.tensor_copy(out=U.car_ap[0:nq, :], in_=P_[0:nq, c1:c1 + 1]), reads=[BPb[u3]], writes=[Bcar])

            def st_tr(U):
                u = U.idx % 2
                A_ = A_sb[u]
                nq = U.nq
                psb16 = PS[4 + u][:, :].bitcast(BF16)
                for bi, (off, nkb, v_ap) in enumerate(U.blocks):
                    S.op("pe", lambda e, bi=bi, off=off, nkb=nkb: e.transpose(psb16[0:nkb, bi * 128:bi * 128 + nq],
                                                                               A_[0:nq, off:off + nkb], identb[0:nq, 0:nq]),
                         reads=[BA[u], B_const], writes=[PSB[4 + u]])

            def st_cp(U):
                u = U.idx % 2
                nq = U.nq
                nb = len(U.blocks)
                psb16 = PS[4 + u][:, :].bitcast(BF16)
                S.op("act", lambda e: e.activation(out=AT_sb[u][:, 0:nb, 0:nq],
                                                   in_=psb16[:, 0:nb * 128].rearrange("p (t c) -> p t c", t=nb)[:, :, 0:nq],
                                                   func=AF.Copy), reads=[PSB[4 + u]], writes=[BAT[u]])

            def st_av(U):
                u = U.idx % 2
                nq = U.nq
                nb = len(U.blocks)
                for bi, (off, nkb, v_ap) in enumerate(U.blocks):
                    S.op("pe", lambda e, bi=bi, nkb=nkb, v_ap=v_ap: e.matmul(PS[6 + u][:, 0:nq], lhsT=v_ap, rhs=AT_sb[u][0:nkb, bi, 0:nq],
                                                                             start=(bi == 0), stop=(bi == nb - 1)),
                         reads=[BAT[u], U.bv], writes=[PSB[6 + u]])

            def st_acc(U):
                u = U.idx % 2
                nq = U.nq
                if U.first:
                    S.op("dve", lambda e: e.tensor_copy(out=U.acc_ap, in_=PS[6 + u][:, 0:nq]), reads=[PSB[6 + u]], writes=[Bacc])
                else:
                    S.op("dve", lambda e: e.tensor_tensor(out=U.acc_ap, in0=PS[6 + u][:, 0:nq], in1=U.acc_ap, op=ALU.add),
                         reads=[PSB[6 + u], Bacc], writes=[Bacc])

            def pipeline(units, stages):
                n = len(units)
                K = len(stages)
                for i, U in enumerate(units):
                    U.idx = i
                for t in range(n + K - 1):
                    for k in range(K):
                        i = t - k
                        if 0 <= i < n:
                            U = units[i]
                            if k == 0 and U.pre is not None:
                                U.pre()
                            stages[k](U)
                            if k == K - 1 and U.post is not None:
                                U.post()

            unitsB = []
            ldc = 0
            for piece in range(2):
                if piece == 0:
                    klist = [(0, None, True)] + [(2048 + 1024 * s_, 1 + i, False) for i, s_ in enumerate([2, 1, 0])]
                else:
                    klist = [(1024, None, True)] + [(2048 + 1024 * s_, 4 + i, False) for i, s_ in enumerate([6, 5, 4, 3, 2, 1, 0])]
                for ui, (kidx, fcol, diag) in enumerate(klist):
                    li = ldc % 2
                    ldc += 1

                    def pre(piece=piece, ui=ui, li=li, kidx=kidx):
                        if ui == 0:
                            S.dma("sp", semqb, [(qb, QbT[:, :, piece * 1024:(piece + 1) * 1024].rearrange("h p c -> p h c"))],
                                  reads=[B_scr], writes=[Bqb])
                        S.dma("sp", semkB[li], [(ktB[li], KbT[:, :, kidx:kidx + 1024].rearrange("h p c -> p h c"))],
                              reads=[B_scr], writes=[BktB[li]])
                        S.dma("sp", semvB[li], [(vtB[li], VbS[kidx // 128:kidx // 128 + 8].rearrange("t p c -> p t c"))],
                              reads=[B_scr], writes=[BvtB[li]])
                    firstU = True
                    for h in range(NH):
                        for qt_ in range(8):
                            kb0 = qt_ if diag else 0
                            blocks = [(128 * b_, 128, vtB[li][:, b_, h * 128:(h + 1) * 128]) for b_ in range(kb0, 8)]
                            U = mk_unit(128, qb[:, h, qt_ * 128:(qt_ + 1) * 128], Bqb, ktB[li][:, h, :], BktB[li], kb0 * 128, 1024, blocks,
                                        BvtB[li], diag, None if fcol is None else flg[:, fcol:fcol + 1],
                                        carry[:, h * 8 + qt_:h * 8 + qt_ + 1], acc[:, h, qt_ * 128:(qt_ + 1) * 128], ui == 0)
                            if firstU:
                                U.pre = pre
                                firstU = False
                            unitsB.append(U)

                def post(piece=piece):
                    S.op("act", lambda e: e.activation(out=obst, in_=acc, func=AF.Copy), reads=[Bacc], writes=[Bobst])
                    S.dma("sp", semob, [(ObT[:, :, piece * 1024:(piece + 1) * 1024].rearrange("h p c -> p h c"), obst)],
                          reads=[Bobst], writes=[B_ObT])
                unitsB[-1].post = post
            kS0 = abf(NH * NS).rearrange("p (h c) -> p h c", h=NH)
            vS0b = abf(1024)
            BkS0 = Buf("kS0")
            semkS0 = S.newsem()
            semck = S.newsem()
            semcvb = [S.newsem(), S.newsem()]
            obs2 = abf(NH * NS).rearrange("p (h c) -> p h c", h=NH)
            Bobs2 = Buf("obs2")

            def pre_s0():
                S.dma("sp", semqb, [(qb[:, :, 0:NS], QbT[:, :, 2048:2080].rearrange("h p c -> p h c"))], reads=[B_scr], writes=[Bqb])
                S.dma("sp", semkS0, [(kS0, KbT[:, :, 9216:9248].rearrange("h p c -> p h c")), (vS0b[0:NS, :], VbS[72, 0:NS, :])],
                      reads=[B_scr], writes=[BkS0])
            for h in range(NH):
                U = mk_unit(NS, qb[:, h, 0:NS], Bqb, kS0[:, h, :], BkS0, 0, NS, [(0, NS, vS0b[0:NS, h * 128:(h + 1) * 128])], BkS0, True, None,
                            carry[:, h:h + 1], acc[:, h, 0:NS], True)
                if h == 0:
                    U.pre = pre_s0
                unitsB.append(U)
            for half in range(2):
                li = half

                def pre_c(half=half, li=li):
                    S.dma("pool", semck, [(obst, cb_k[half * 1024:(half + 1) * 1024, :].rearrange("(t p) c -> p t c", p=128))],
                          writes=[Bobst])
                    S.dma("pool", semcvb[li], [(vtB[li], cb_v[half * 1024:(half + 1) * 1024, :].rearrange("(t p) c -> p t c", p=128))],
                          writes=[BvtB[li]])
                    for h in range(NH):
                        for g2 in range(2):
                            pi = (h * 2 + g2) % 2
                            psb16 = PS[pi][:, :].bitcast(BF16)
                            for t in range(4):
                                tt = g2 * 4 + t
                                S.op("pe", lambda e, psb16=psb16, h=h, t=t, tt=tt: e.transpose(
                                    psb16[:, t * 128:(t + 1) * 128], obst[:, tt, h * 128:(h + 1) * 128], identb[:, :]),
                                    reads=[Bobst, B_const], writes=[PSB[pi]])
                            S.op("dve", lambda e, psb16=psb16, h=h, g2=g2, li=li: e.tensor_copy(
                                out=ktB[li][:, h, g2 * 512:(g2 + 1) * 512], in_=psb16[:, 0:512]), reads=[PSB[pi]], writes=[BktB[li]])
                for h in range(NH):
                    blocks = [(128 * b_, 128, vtB[li][:, b_, h * 128:(h + 1) * 128]) for b_ in range(8)]
                    U = mk_unit(NS, qb[:, h, 0:NS], Bqb, ktB[li][:, h, :], BktB[li], 0, 1024, blocks, BvtB[li], False, None,
                                carry[:, h:h + 1], acc[:, h, 0:NS], False)
                    if h == 0:
                        U.pre = pre_c
                    unitsB.append(U)

            def post_s():
                S.op("act", lambda e: e.activation(out=obs2, in_=acc[:, :, 0:NS], func=AF.Copy), reads=[Bacc], writes=[Bobs2])
                S.dma("sp", semob, [(ObT[:, :, 2048:2080].rearrange("h p c -> p h c"), obs2)], reads=[Bobs2], writes=[B_ObT])
            unitsB[-1].post = post_s
            pipeline(unitsB, [st_z, st_sig2, st_scan, st_sub, st_tr, st_cp, st_av, st_acc])
            S.barrier()
            areset()
            _ph[0] += 1
            if _ph[0] >= STOP:
                break

            wa = abf(NH * D).rearrange("p (k c) -> p k c", k=NH)
            wbm = abf(NH * D).rearrange("p (k c) -> p k c", k=NH)
            Bwab = Buf("wab")
            semwab = S.newsem()
            S.dma("pool", semwab, [(wa[:, :, 0:1024], wview(w_a_out, 0, 1024)), (wa[:, :, 1024:2048], wview(w_a_out, 1024, 1024)),
                                   (wbm[:, :, 0:1024], wview(w_b_out, 0, 1024)), (wbm[:, :, 1024:2048], wview(w_b_out, 1024, 1024))],
                  writes=[Bwab])
            oat = [abf(NH * 512).rearrange("p (h c) -> p h c", h=NH) for _ in range(2)]
            obt = [abf(NH * 512).rearrange("p (h c) -> p h c", h=NH) for _ in range(2)]
            Boab = [Buf("oab0"), Buf("oab1")]
            semoab = [S.newsem(), S.newsem()]
            ga8 = af32(8 * 512).rearrange("p (h c) -> p h c", h=8)
            gb8 = af32(8 * 512).rearrange("p (h c) -> p h c", h=8)
            Bg8 = Buf("g8")
            semg8 = S.newsem()
            t1 = [af32(512) for _ in range(2)]
            t2 = [af32(512) for _ in range(2)]
            Bt12 = [Buf("t12_0"), Buf("t12_1")]
            mst = abf(KC * 512).rearrange("p (k c) -> p k c", k=KC)
            Bmst = Buf("mst")
            semmst = S.newsem()
            B_MT = [Buf("MT%d" % i) for i in range(5)]
            cc5 = 0
            for bi5, (blk, ntk, r0) in enumerate(own_blocks):
                li = bi5 % 2
                S.dma("sp", semoab[li], [(oat[li][:, :, 0:ntk], OaT[:, :, r0:r0 + ntk].rearrange("h p c -> p h c")),
                                         (obt[li][:, :, 0:ntk], ObT[:, :, r0:r0 + ntk].rearrange("h p c -> p h c"))],
                      reads=[B_OaT, B_ObT], writes=[Boab[li]])
                for half in range(2):
                    S.dma("sp", semg8, [(ga8[:, :, 0:ntk], GA[8 * half:8 * half + 8, :, r0:r0 + ntk].rearrange("h p c -> p h c")),
                                        (gb8[:, :, 0:ntk], GB[8 * half:8 * half + 8, :, r0:r0 + ntk].rearrange("h p c -> p h c"))],
                          reads=[B_scr], writes=[Bg8])
                    for cc in range(8):
                        c = 8 * half + cc
                        u = cc5 % 2
                        cc5 += 1
                        pa, pb_ = 2 * u, 2 * u + 1
                        for kc in range(NH):
                            S.op("pe", lambda e, pa=pa, kc=kc, c=c, li=li, ntk=ntk: e.matmul(
                                PS[pa][:, 0:ntk], lhsT=wa[:, kc, c * 128:(c + 1) * 128], rhs=oat[li][:, kc, 0:ntk],
                                start=(kc == 0), stop=(kc == NH - 1)), reads=[Bwab, Boab[li]], writes=[PSB[pa]])
                        for kc in range(NH):
                            S.op("pe", lambda e, pb_=pb_, kc=kc, c=c, li=li, ntk=ntk: e.matmul(
                                PS[pb_][:, 0:ntk], lhsT=wbm[:, kc, c * 128:(c + 1) * 128], rhs=obt[li][:, kc, 0:ntk],
                                start=(kc == 0), stop=(kc == NH - 1)), reads=[Bwab, Boab[li]], writes=[PSB[pb_]])
                        S.op("dve", lambda e, pa=pa, u=u, cc=cc, ntk=ntk: e.tensor_tensor(
                            out=t1[u][:, 0:ntk], in0=PS[pa][:, 0:ntk], in1=ga8[:, cc, 0:ntk], op=ALU.mult),
                            reads=[PSB[pa], Bg8], writes=[Bt12[u]])
                        S.op("dve", lambda e, pb_=pb_, u=u, cc=cc, ntk=ntk: e.tensor_tensor(
                            out=t2[u][:, 0:ntk], in0=PS[pb_][:, 0:ntk], in1=gb8[:, cc, 0:ntk], op=ALU.mult),
                            reads=[PSB[pb_], Bg8], writes=[Bt12[u]])
                        S.op("pool", lambda e, u=u, c=c, ntk=ntk: e.tensor_tensor(
                            out=mst[:, c, 0:ntk], in0=t1[u][:, 0:ntk], in1=t2[u][:, 0:ntk], op=ALU.add),
                            reads=[Bt12[u]], writes=[Bmst])
                S.dma("sp", semmst, [(MT[bi5], mst)], reads=[Bmst], writes=[B_MT[bi5]])
            S.barrier()
            areset()
            _ph[0] += 1
            if _ph[0] >= STOP:
                break

            wos = [abf(KC * 512).rearrange("p (k c) -> p k c", k=KC) for _ in range(2)]
            Bwos = [Buf("wos0"), Buf("wos1")]
            semwos = [S.newsem(), S.newsem()]
            mtb = [abf(KC * 512).rearrange("p (k c) -> p k c", k=KC) for _ in range(2)]
            Bmtb = [Buf("mtb0"), Buf("mtb1")]
            semmtb = [S.newsem(), S.newsem()]
            xc5 = [af32(512) for _ in range(2)]
            Bxc5 = [Buf("xc5_0"), Buf("xc5_1")]
            semxc5 = [S.newsem(), S.newsem()]
            semx1 = [S.newsem(), S.newsem()]
            t5 = [af32(512) for _ in range(2)]
            Bt5 = [Buf("t5_0"), Buf("t5_1")]
            gtb = [af32(D) for _ in range(2)]
            Bgtb = Buf("gtb")
            semgtb = S.newsem()
            S.dma("sp", semgtb, [(gtb[v], bass.AP(adaD.tensor, v * 6 * D + 2 * D, [[0, 128], [1, D]])) for v in range(2)],
                  reads=[B_adaD], writes=[Bgtb])
            B_X1 = [Buf("X1_%d" % i) for i in range(17)]

            def load_w5(s_):
                S.dma("pool", semwos[s_ % 2], [(wos[s_ % 2], wview(w_o, s_ * 512, 512))], writes=[Bwos[s_ % 2]])

            def load_m5(n_):
                S.dma("sp", semmtb[n_ % 2], [(mtb[n_ % 2], MT[n_ % 5])], reads=[B_MT[n_ % 5]], writes=[Bmtb[n_ % 2]])
            load_w5(0)
            load_m5(0)
            l5 = 0
            c5 = 0
            for s5 in range(4):
                wi = s5 % 2
                if s5 + 1 < 4:
                    load_w5(s5 + 1)
                tix = 0
                for bi5, (blk, ntk, r0) in enumerate(own_blocks):
                    li = l5 % 2
                    if l5 + 1 < 20:
                        load_m5(l5 + 1)
                    l5 += 1
                    v = 0 if blk < NBLK else 1
                    nt = 4 if ntk == 512 else 1
                    for j in range(nt):
                        ntok = 128 if ntk == 512 else NS
                        u = c5 % 2
                        pi = c5 % 4
                        c5 += 1
                        src = xin[blk, j * 128:(j + 1) * 128, s5 * 512:(s5 + 1) * 512] if blk < NBLK else xs_in[:, s5 * 512:(s5 + 1) * 512]
                        S.dma("sp", semxc5[u], [(xc5[u][0:ntok, :], src)], writes=[Bxc5[u]])
                        for kc in range(KC):
                            S.op("pe", lambda e, pi=pi, kc=kc, li=li, j=j, ntok=ntok, wi=wi: e.matmul(
                                PS[pi][0:ntok, :], lhsT=mtb[li][:, kc, j * 128:j * 128 + ntok], rhs=wos[wi][:, kc, :],
                                start=(kc == 0), stop=(kc == KC - 1)), reads=[Bwos[wi], Bmtb[li]], writes=[PSB[pi]])
                        S.op("dve", lambda e, pi=pi, u=u, v=v, ntok=ntok, s5=s5: e.tensor_tensor(
                            out=t5[u][0:ntok, :], in0=PS[pi][0:ntok, :], in1=gtb[v][0:ntok, s5 * 512:(s5 + 1) * 512], op=ALU.mult),
                            reads=[PSB[pi], Bgtb], writes=[Bt5[u]])
                        S.op("pool", lambda e, u=u, ntok=ntok: e.tensor_tensor(
                            out=xc5[u][0:ntok, :], in0=xc5[u][0:ntok, :], in1=t5[u][0:ntok, :], op=ALU.add),
                            reads=[Bt5[u], Bxc5[u]], writes=[Bxc5[u]])
                        S.dma("sp", semx1[u], [(X1[tix, 0:ntok, s5 * 512:(s5 + 1) * 512], xc5[u][0:ntok, :])],
                              reads=[Bxc5[u]], writes=[B_X1[tix]])
                        tix += 1
            S.barrier()
            areset()
            B_H2T = [Buf("H2T%d" % i) for i in range(5)]
            tiles5 = []
            tix = 0
            for bi5, (blk, ntk, r0) in enumerate(own_blocks):
                nt = 4 if ntk == 512 else 1
                for j in range(nt):
                    ntok = 128 if ntk == 512 else NS
                    tiles5.append((X1[tix, 0:ntok, :], [B_X1[tix]], ntok, 0 if blk < NBLK else 1, bi5, j, j == nt - 1))
                    tix += 1

            def store5(bi5, st, bst, sem):
                S.dma("sp", sem, [(H2T[bi5], st)], reads=[bst], writes=[B_H2T[bi5]])
            norm_pass(tiles5, 4, 3, 1, store5, None)
            S.barrier()
            areset()
            _ph[0] += 1
            if _ph[0] >= STOP:
                break

            wgu = [abf(KC * 1024).rearrange("p (k c) -> p k c", k=KC) for _ in range(2)]
            Bwgu = [Buf("wgu0"), Buf("wgu1")]
            semwgu = [S.newsem(), S.newsem()]
            h2b = [abf(KC * 512).rearrange("p (k c) -> p k c", k=KC) for _ in range(2)]
            Bh2b = [Buf("h2b0"), Buf("h2b1")]
            semh2b = [S.newsem(), S.newsem()]
            sg = [af32(512) for _ in range(2)]
            Bsg = [Buf("sg0"), Buf("sg1")]
            ast = [abf(4 * 512).rearrange("p (k c) -> p k c", k=4) for _ in range(2)]
            Bast = [Buf("ast0"), Buf("ast1")]
            semast = [S.newsem(), S.newsem()]
            B_ActT = Buf("ActT")
            c6 = 0
            l6 = 0
            a6 = 0
            def load_w6(g_):
                S.dma("pool", semwgu[g_ % 2], [(wgu[g_ % 2][:, :, 0:512], wview(w_gu, g_ * 512, 512)),
                                               (wgu[g_ % 2][:, :, 512:1024], wview(w_gu, DFF + g_ * 512, 512))], writes=[Bwgu[g_ % 2]])

            def load_h6(n_):
                S.dma("sp", semh2b[n_ % 2], [(h2b[n_ % 2], H2T[n_ % 5])], reads=[B_H2T[n_ % 5]], writes=[Bh2b[n_ % 2]])
            load_w6(0)
            load_h6(0)
            for g in range(11):
                wi = g % 2
                if g + 1 < 11:
                    load_w6(g + 1)
                for bi5, (blk, ntk, r0) in enumerate(own_blocks):
                    li = l6 % 2
                    if l6 + 1 < 55:
                        load_h6(l6 + 1)
                    l6 += 1
                    ai = a6 % 2
                    a6 += 1
                    for cc in range(4):
                        u = c6 % 2
                        c6 += 1
                        pg, pu = 2 * u, 2 * u + 1
                        for kc in range(KC):
                            S.op("pe", lambda e, pg=pg, kc=kc, cc=cc, wi=wi, li=li, ntk=ntk: e.matmul(
                                PS[pg][:, 0:ntk], lhsT=wgu[wi][:, kc, cc * 128:(cc + 1) * 128], rhs=h2b[li][:, kc, 0:ntk],
                                start=(kc == 0), stop=(kc == KC - 1)), reads=[Bwgu[wi], Bh2b[li]], writes=[PSB[pg]])
                        for kc in range(KC):
                            S.op("pe", lambda e, pu=pu, kc=kc, cc=cc, wi=wi, li=li, ntk=ntk: e.matmul(
                                PS[pu][:, 0:ntk], lhsT=wgu[wi][:, kc, 512 + cc * 128:512 + (cc + 1) * 128], rhs=h2b[li][:, kc, 0:ntk],
                                start=(kc == 0), stop=(kc == KC - 1)), reads=[Bwgu[wi], Bh2b[li]], writes=[PSB[pu]])
                        S.op("act", lambda e, pg=pg, u=u, ntk=ntk: e.activation(out=sg[u][:, 0:ntk], in_=PS[pg][:, 0:ntk], func=AF.Silu),
                             reads=[PSB[pg]], writes=[Bsg[u]])
                        S.op("dve", lambda e, pu=pu, u=u, ai=ai, cc=cc, ntk=ntk: e.tensor_tensor(
                            out=ast[ai][:, cc, 0:ntk], in0=PS[pu][:, 0:ntk], in1=sg[u][:, 0:ntk], op=ALU.mult),
                            reads=[PSB[pu], Bsg[u]], writes=[Bast[ai]])
                    S.dma("sp", semast[ai], [(ActT[2 * bi5 + hf_, :, 4 * g:4 * g + 4, 0:min(256, ntk - 256 * hf_)],
                                              ast[ai][:, :, 256 * hf_:256 * hf_ + min(256, ntk - 256 * hf_)])
                                             for hf_ in range(2) if ntk > 256 * hf_],
                          reads=[Bast[ai]], writes=[B_ActT])
            S.barrier()
            areset()
            _ph[0] += 1
            if _ph[0] >= STOP:
                break

            NSL = 4
            SW = D // NSL
            wd = [abf(FC * SW).rearrange("p (k c) -> p k c", k=FC) for _ in range(2)]
            Bwd = [Buf("wd0"), Buf("wd1")]
            semwd = [S.newsem(), S.newsem()]
            actb = [abf(FC * 256).rearrange("p (k c) -> p k c", k=FC) for _ in range(2)]
            Bactb = [Buf("actb0"), Buf("actb1")]
            semactb = [S.newsem(), S.newsem()]
            x1c = [af32(SW) for _ in range(2)]
            Bx1c = [Buf("x1c0"), Buf("x1c1")]
            semx1c = [S.newsem(), S.newsem()]
            semx2 = [S.newsem(), S.newsem()]
            t7 = [af32(SW) for _ in range(2)]
            Bt7 = [Buf("t7_0"), Buf("t7_1")]
            gt2b = [af32(D) for _ in range(2)]
            Bgt2 = Buf("gt2")
            semgt2 = S.newsem()
            S.dma("sp", semgt2, [(gt2b[v], bass.AP(adaD.tensor, v * 6 * D + 5 * D, [[0, 128], [1, D]])) for v in range(2)],
                  reads=[B_adaD], writes=[Bgt2])
            B_X2 = [Buf("X2_%d" % i) for i in range(17)]
            hbl = []
            for bi5 in range(4):
                for half in range(2):
                    hbl.append((2 * bi5 + half, 256, bi5 * 4 + half * 2, 2, 128, 0))
            hbl.append((8, NS, 16, 1, NS, 1))
            NHB = len(hbl)

            def load_w7(s_):
                S.dma("pool", semwd[s_ % 2], [(wd[s_ % 2][:, 0:22, :], wview(w_dn, s_ * SW, SW)[:, 0:22, :]),
                                              (wd[s_ % 2][:, 22:44, :], wview(w_dn, s_ * SW, SW)[:, 22:44, :])], writes=[Bwd[s_ % 2]])

            def load_a7(n_):
                ai_, w_, _, _, _, _ = hbl[n_ % NHB]
                S.dma("sp", semactb[n_ % 2], [(actb[n_ % 2][:, :, 0:w_], ActT[ai_, :, :, 0:w_])], reads=[B_ActT], writes=[Bactb[n_ % 2]])
            load_w7(0)
            load_a7(0)
            l7 = 0
            c7 = 0
            for s7 in range(NSL):
                wi = s7 % 2
                if s7 + 1 < NSL:
                    load_w7(s7 + 1)
                for (ai_, w_, tix0, ntl, ntok, v) in hbl:
                    li = l7 % 2
                    if l7 + 1 < NSL * NHB:
                        load_a7(l7 + 1)
                    l7 += 1
                    for j in range(ntl):
                        tix = tix0 + j
                        u = c7 % 2
                        pi = c7 % 4
                        c7 += 1
                        S.dma("sp", semx1c[u], [(x1c[u][0:ntok, :], X1[tix, 0:ntok, s7 * SW:(s7 + 1) * SW])],
                              reads=[B_X1[tix]], writes=[Bx1c[u]])
                        for kc in range(FC):
                            S.op("pe", lambda e, pi=pi, kc=kc, li=li, j=j, ntok=ntok, wi=wi: e.matmul(
                                PS[pi][0:ntok, 0:SW], lhsT=actb[li][:, kc, j * 128:j * 128 + ntok], rhs=wd[wi][:, kc, :],
                                start=(kc == 0), stop=(kc == FC - 1)), reads=[Bwd[wi], Bactb[li]], writes=[PSB[pi]])
                        S.op("dve", lambda e, pi=pi, u=u, v=v, ntok=ntok, s7=s7: e.tensor_tensor(
                            out=t7[u][0:ntok, :], in0=PS[pi][0:ntok, 0:SW], in1=gt2b[v][0:ntok, s7 * SW:(s7 + 1) * SW], op=ALU.mult),
                            reads=[PSB[pi], Bgt2], writes=[Bt7[u]])
                        S.op("pool", lambda e, u=u, ntok=ntok: e.tensor_tensor(
                            out=x1c[u][0:ntok, :], in0=x1c[u][0:ntok, :], in1=t7[u][0:ntok, :], op=ALU.add),
                            reads=[Bt7[u], Bx1c[u]], writes=[Bx1c[u]])
                        S.dma("sp", semx2[u], [(X2[tix, 0:ntok, s7 * SW:(s7 + 1) * SW], x1c[u][0:ntok, :])],
                              reads=[Bx1c[u]], writes=[B_X2[tix]])
            S.barrier()
            areset()
            _ph[0] += 1
            if _ph[0] >= STOP:
                break

            gfb = af32(D)
            Bgfb = Buf("gfb")
            semgfb = S.newsem()
            S.dma("sp", semgfb, [(gfb, bass.AP(gfin.tensor, 0, [[0, 128], [1, D]]))], writes=[Bgfb])
            x8 = [af32(D) for _ in range(2)]
            Bx8 = [Buf("x8_0"), Buf("x8_1")]
            semx8 = [S.newsem(), S.newsem()]
            y8 = [af32(D) for _ in range(2)]
            By8 = [Buf("y8_0"), Buf("y8_1")]
            semy8 = [S.newsem(), S.newsem()]
            Bsm8 = [Buf("sm8_0"), Buf("sm8_1")]
            for tix in range(17):
                u = tix % 2
                ntok = 128 if tix < 16 else NS
                S.dma("sp", semx8[u], [(x8[u][0:ntok, :], X2[tix, 0:ntok, :])], reads=[B_X2[tix]], writes=[Bx8[u]])
                ssq = small[:, 48 + 2 * u:49 + 2 * u]
                rstd = small[:, 49 + 2 * u:50 + 2 * u]
                S.op("act", lambda e, u=u, ntok=ntok, ssq=ssq: e.activation(out=y8[u][0:ntok, :], in_=x8[u][0:ntok, :], func=AF.Square,
                                                                            accum_out=ssq[0:ntok, :]), reads=[Bx8[u]], writes=[By8[u], Bsm8[u]])
                S.op("act", lambda e, ntok=ntok, ssq=ssq, rstd=rstd: e.activation(out=rstd[0:ntok, :], in_=ssq[0:ntok, :], func=AF.Sqrt,
                                                                                  scale=1.0 / D, bias=small[0:ntok, 63:64]),
                     reads=[Bsm8[u], B_const], writes=[Bsm8[u]])
                S.op("dve", lambda e, ntok=ntok, rstd=rstd: e.reciprocal(out=rstd[0:ntok, :], in_=rstd[0:ntok, :]),
                     reads=[Bsm8[u]], writes=[Bsm8[u]])
                S.op("act", lambda e, u=u, ntok=ntok, rstd=rstd: e.activation(out=y8[u][0:ntok, :], in_=x8[u][0:ntok, :], func=AF.Identity,
                                                                              scale=rstd[0:ntok, :]), reads=[Bx8[u], Bsm8[u]], writes=[By8[u]])
                S.op("dve", lambda e, u=u, ntok=ntok: e.tensor_tensor(out=y8[u][0:ntok, :], in0=y8[u][0:ntok, :], in1=gfb[0:ntok, :],
                                                                      op=ALU.mult), reads=[By8[u], Bgfb], writes=[By8[u]])
                S.dma("sp", semy8[u], [(y_o[tix * 128:tix * 128 + ntok, :], y8[u][0:ntok, :])], reads=[By8[u]], writes=[])
            S.barrier()

        with nc.Block() as block:
            @block.tensor
            def _(e):
                S.replay("pe", e)

            @block.scalar
            def _(e):
                S.replay("act", e)

            @block.vector
            def _(e):
                S.replay("dve", e)

            @block.gpsimd
            def _(e):
                S.replay("pool", e)

            @block.sync
            def _(e):
                S.replay("sp", e)
    return nc


_NC_CACHE = {}


def _prep_inputs(x_prompt, x_sample, c_prompt, c_sample, cache_a_k, cache_a_v, cache_b_k, cache_b_v,
                 w_ada, b_ada, g_mix, w_in, rel_bias, w_a_out, w_b_out, w_o, g_ffn, w_gate_up, w_down, g_final):
    f = lambda a: np.ascontiguousarray(np.asarray(a, dtype=np.float32))
    shared = {
        "w_ada": f(w_ada[0]), "b_ada": f(b_ada[0]).reshape(1, -1), "w_in": f(w_in[0]), "relb": f(rel_bias[0]),
        "w_a_out": f(w_a_out[0]), "w_b_out": f(w_b_out[0]), "w_o": f(w_o[0]), "w_gu": f(w_gate_up[0]),
        "w_dn": f(w_down[0]), "gfin": f(g_final).reshape(1, -1),
    }
    gcols = np.stack([f(g_mix[0]).reshape(KC, 128).T, f(g_ffn[0]).reshape(KC, 128).T], axis=1)
    shared["gcols"] = np.ascontiguousarray(gcols)
    shared["grow"] = np.ascontiguousarray(np.stack([f(g_mix[0]), f(g_ffn[0])], axis=0))
    xp = np.asarray(x_prompt, dtype=np.float32)
    xs = np.asarray(x_sample, dtype=np.float32)
    maps = []
    for c in range(8):
        b, r = c // 4, c % 4
        xr = xp[b, ::-1]
        def piece(p):
            return xr[1024 * (7 - p):1024 * (8 - p)]
        def halo(p):
            if p == 0:
                return xr[0:512]
            return xr[1024 * (8 - p):1024 * (8 - p) + 512]
        lo, hi = r, 7 - r
        blocks = [piece(lo)[:512], piece(lo)[512:], piece(hi)[:512], piece(hi)[512:], halo(lo), halo(hi)]
        for s in range(7):
            p = s if s <= 6 - r else 0
            blocks += [piece(p)[:512], piece(p)[512:]]
        xin = np.ascontiguousarray(np.stack(blocks, axis=0))
        cv = np.stack([np.asarray(c_prompt[b], np.float32).reshape(KC, 128).T,
                       np.asarray(c_sample[c], np.float32).reshape(KC, 128).T], axis=2)
        flags = np.zeros((128, 16), np.float32)
        flags[:, 0] = -BIG if r == 0 else 0.0
        for i, s in enumerate([2, 1, 0]):
            flags[:, 1 + i] = 0.0 if s <= r - 1 else BIG
        for i, s in enumerate([6, 5, 4, 3, 2, 1, 0]):
            flags[:, 4 + i] = 0.0 if s <= 6 - r else BIG
        m = dict(shared)
        m.update({
            "xin": xin, "xs": np.ascontiguousarray(xs[c, ::-1]), "cvec": np.ascontiguousarray(cv), "flags": flags,
            "ca_k": np.ascontiguousarray(np.asarray(cache_a_k[0, c], np.float32)[::-1].reshape(512, 1024)),
            "ca_v": np.ascontiguousarray(np.asarray(cache_a_v[0, c], np.float32)[::-1].reshape(512, 1024)),
            "cb_k": np.ascontiguousarray(np.asarray(cache_b_k[0, c], np.float32)[::-1].reshape(2048, 1024)),
            "cb_v": np.ascontiguousarray(np.asarray(cache_b_v[0, c], np.float32)[::-1].reshape(2048, 1024)),
        })
        maps.append(m)
    return maps


def kernel(**inputs):
    maps = _prep_inputs(**inputs)
    if "nc" not in _NC_CACHE:
        _NC_CACHE["nc"] = build_program()
    nc = _NC_CACHE["nc"]
    res = run_bass_kernel_spmd(nc, maps, core_ids=list(range(8)))
    R = res.results
    y_p = np.zeros((2, 8192, D), np.float32)
    y_s = np.zeros((8, NS, D), np.float32)
    ak_p = np.zeros((1, 2, 512, NH, HD), np.float32)
    av_p = np.zeros((1, 2, 512, NH, HD), np.float32)
    bk_p = np.zeros((1, 2, 8192, NH, HD), np.float32)
    bv_p = np.zeros((1, 2, 8192, NH, HD), np.float32)
    ak_s = np.zeros((1, 8, NS, NH, HD), np.float32)
    av_s = np.zeros((1, 8, NS, NH, HD), np.float32)
    bk_s = np.zeros((1, 8, NS, NH, HD), np.float32)
    bv_s = np.zeros((1, 8, NS, NH, HD), np.float32)
    for c in range(8):
        b, r = c // 4, c % 4
        o = R[c]
        for pi, p in enumerate((r, 7 - r)):
            sl = slice(1024 * p, 1024 * (p + 1))
            rows = slice(1024 * pi, 1024 * (pi + 1))
            y_p[b, sl] = o["y"][rows][::-1]
            bk_p[0, b, sl] = o["kb_o"][rows][::-1].reshape(1024, NH, HD)
            bv_p[0, b, sl] = o["vb_o"][rows][::-1].reshape(1024, NH, HD)
        if r == 0:
            ak_p[0, b] = o["ka_o"][0:512][::-1].reshape(512, NH, HD)
            av_p[0, b] = o["va_o"][0:512][::-1].reshape(512, NH, HD)
        y_s[c] = o["y"][2048:2080][::-1]
        ak_s[0, c] = o["ka_o"][512:544][::-1].reshape(NS, NH, HD)
        av_s[0, c] = o["va_o"][512:544][::-1].reshape(NS, NH, HD)
        bk_s[0, c] = o["kb_o"][2048:2080][::-1].reshape(NS, NH, HD)
        bv_s[0, c] = o["vb_o"][2048:2080][::-1].reshape(NS, NH, HD)
    return (y_p, y_s, ak_p, av_p, bk_p, bv_p, ak_s, av_s, bk_s, bv_s)
```

```python
import numpy as np
from contextlib import ExitStack
import concourse.bass as bass
import concourse.mybir as mybir
from concourse.bass_utils import run_bass_kernel_spmd

F32 = mybir.dt.float32
BF16 = mybir.dt.bfloat16
AF = mybir.ActivationFunctionType
ALU = mybir.AluOpType
AX = mybir.AxisListType

D = 2048
KC = 16
NH = 8
HD = 128
DFF = 5632
FC = 44
NOWN = 2080
NS = 32
SCALE = HD ** -0.5
EPS = 1e-6
BIG = 30000.0
NBLK = 20
NAKEY = 3104
NBKEY = 9248
LT = 767


class Sem:
    def __init__(self, h):
        self.h = h
        self.count = 0


class Buf:
    __slots__ = ("name", "w", "r", "excl")

    def __init__(self, name="", excl=False):
        self.name = name
        self.w = None
        self.r = {}
        self.excl = excl


class Sched:
    ENG = ("pe", "act", "dve", "pool", "sp")

    def __init__(self, nc, es):
        self.nc = nc
        self.es = es
        self.q = {e: [] for e in self.ENG}
        self.esem = {e: Sem(es.enter_context(nc.semaphore("s_" + e))) for e in self.ENG}
        self.seen = {e: {} for e in self.ENG}
        self.dsems = []
        self.nsem = 0

    def newsem(self):
        self.nsem += 1
        s = Sem(self.es.enter_context(self.nc.semaphore("d%d" % self.nsem)))
        self.dsems.append(s)
        return s

    def _waits(self, eng, reads, writes):
        need = {}

        def add(s, n):
            if need.get(s, 0) < n:
                need[s] = n
        for b in reads:
            if b.w is not None:
                add(*b.w)
        for b in writes:
            if b.w is not None:
                add(*b.w)
            for s, n in b.r.items():
                add(s, n)
        out = []
        seen = self.seen[eng]
        for s, n in need.items():
            if eng == "pe" and s is self.esem["pe"]:
                continue
            if seen.get(s, 0) >= n:
                continue
            seen[s] = n
            out.append((s, n))
        return out

    def _commit(self, t, reads, writes):
        s, n = t
        for b in reads:
            if b.r.get(s, 0) < n:
                b.r[s] = n
        for b in writes:
            b.w = t
            b.r = {}

    def op(self, eng, fn, reads=(), writes=()):
        if any(b.excl for b in reads):
            writes = list(writes) + [b for b in reads if b.excl and b not in writes]
            reads = [b for b in reads if not b.excl]
        waits = self._waits(eng, reads, writes)
        s = self.esem[eng]
        s.count += 1
        t = (s, s.count)
        self.q[eng].append((waits, fn, s, 1))
        self._commit(t, reads, writes)
        return t

    def dma(self, eng, sem, pairs, reads=(), writes=(), transpose=False):
        waits = self._waits(eng, reads, writes)
        if sem.count > 0 and self.seen[eng].get(sem, 0) < sem.count:
            self.seen[eng][sem] = sem.count
            waits = waits + [(sem, sem.count)]
        sem.count += 16 * len(pairs)
        t = (sem, sem.count)
        for i, (o, i_) in enumerate(pairs):
            if transpose:
                self.q[eng].append((waits if i == 0 else [], (lambda e, o=o, i_=i_: e.dma_start_transpose(out=o, in_=i_)), sem, 16))
            else:
                self.q[eng].append((waits if i == 0 else [], (lambda e, o=o, i_=i_: e.dma_start(out=o, in_=i_)), sem, 16))
        self._commit(t, reads, writes)
        return t

    def barrier(self):
        allw = [(s, s.count) for s in list(self.esem.values()) + self.dsems if s.count > 0]
        for e in self.ENG:
            seen = self.seen[e]
            w = []
            for s, n in allw:
                if seen.get(s, 0) < n:
                    seen[s] = n
                    w.append((s, n))
            if w:
                self.q[e].append((w, None, None, 0))

    def replay(self, name, e):
        for waits, fn, s, inc in self.q[name]:
            for ws, n in waits:
                e.wait_ge(ws.h, n)
            if fn is not None:
                fn(e).then_inc(s.h, inc)


def build_program(STOP=99):
    nc = bass.Bass("TRN2", target_bir_lowering=False)

    def din(name, shape, dt=F32):
        return nc.dram_tensor(name, list(shape), dt, kind="ExternalInput").ap()

    def dout(name, shape):
        return nc.dram_tensor(name, list(shape), F32, kind="ExternalOutput").ap()

    def dscr(name, shape, dt):
        return nc.dram_tensor(name, list(shape), dt, kind="Internal").ap()

    xin = din("xin", [NBLK, 512, D])
    xs_in = din("xs", [NS, D])
    cvec = din("cvec", [128, KC, 2])
    gcols = din("gcols", [128, 2, KC])
    grow = din("grow", [2, D])
    gfin = din("gfin", [1, D])
    flags_in = din("flags", [128, 16])
    w_ada = din("w_ada", [D, 6 * D])
    b_ada = din("b_ada", [1, 6 * D])
    w_in = din("w_in", [D, 10240])
    relb = din("relb", [NH, 257])
    w_a_out = din("w_a_out", [1024, D])
    w_b_out = din("w_b_out", [1024, D])
    w_o = din("w_o", [D, D])
    w_gu = din("w_gu", [D, 2 * DFF])
    w_dn = din("w_dn", [DFF, D])
    ca_k = din("ca_k", [512, 1024])
    ca_v = din("ca_v", [512, 1024])
    cb_k = din("cb_k", [2048, 1024])
    cb_v = din("cb_v", [2048, 1024])

    y_o = dout("y", [NOWN, D])
    ka_o = dout("ka_o", [544, 1024])
    va_o = dout("va_o", [544, 1024])
    kb_o = dout("kb_o", [NOWN, 1024])
    vb_o = dout("vb_o", [NOWN, 1024])

    HT = dscr("HT", [NBLK + 1, 128, KC, 512], BF16)
    adaD = dscr("adaD", [2, 6 * D], F32)
    GT = dscr("GT", [NH, 128, LT], F32)
    gD = dscr("gD", [NH, LT], F32)
    QaT = dscr("QaT", [NH, 128, NOWN], BF16)
    QbT = dscr("QbT", [NH, 128, NOWN], BF16)
    KaT = dscr("KaT", [NH, 128, NAKEY], BF16)
    KbT = dscr("KbT", [NH, 128, NBKEY], BF16)
    VaS = dscr("VaS", [25, 128, 1024], BF16)
    VbS = dscr("VbS", [73, 128, 1024], BF16)
    GA = dscr("GA", [KC, 128, NOWN], F32)
    GB = dscr("GB", [KC, 128, NOWN], F32)
    OaT = dscr("OaT", [NH, 128, NOWN], BF16)
    ObT = dscr("ObT", [NH, 128, NOWN], BF16)
    MT = dscr("MT", [5, 128, KC, 512], BF16)
    X1 = dscr("X1", [17, 128, D], F32)
    H2T = dscr("H2T", [5, 128, KC, 512], BF16)
    ActT = dscr("ActT", [9, 128, FC, 256], BF16)
    X2 = dscr("X2", [17, 128, D], F32)

    es = ExitStack()
    with es:
        S = Sched(nc, es)
        ARW = 45056
        arena = es.enter_context(nc.sbuf_tensor("arena", [128, ARW], F32))
        identf = es.enter_context(nc.sbuf_tensor("identf", [128, 128], F32))
        identb = es.enter_context(nc.sbuf_tensor("identb", [128, 128], BF16))
        maskL = es.enter_context(nc.sbuf_tensor("maskL", [128, 128], F32))
        Tb = es.enter_context(nc.sbuf_tensor("Tb", [128, NH, 640], F32))
        modc = es.enter_context(nc.sbuf_tensor("modc", [128, 2, 4, KC], F32))
        flg = es.enter_context(nc.sbuf_tensor("flg", [128, 16], F32))
        small = es.enter_context(nc.sbuf_tensor("small", [128, 64], F32))
        PS = [es.enter_context(nc.psum_tensor("ps%d" % i, [128, 512], F32)) for i in range(8)]
        PSB = [Buf("ps%d" % i, excl=True) for i in range(8)]
        B_const = Buf("const")

        apos = [0]

        def areset():
            apos[0] = 0

        def af32(n):
            o = apos[0]
            apos[0] += n
            assert apos[0] <= ARW, apos[0]
            return arena[:, o:o + n]

        def abf(n):
            assert n % 2 == 0
            return af32(n // 2).bitcast(BF16)

        def wview(w, c0, nc_, kc=KC):
            return w.rearrange("(kc p) c -> p kc c", p=128)[:, :, c0:c0 + nc_]

        _ph = [0]
        for _once in (0,):
            sem_c = S.newsem()
            S.op("pool", lambda e: e.memset(identf[:], 0.0), writes=[B_const])
            S.op("pool", lambda e: e.affine_select(out=identf[:], in_=identf[:], pattern=[[-1, 128]],
                                                   compare_op=ALU.not_equal, fill=1.0, base=0,
                                                   channel_multiplier=1), writes=[B_const])
            S.op("pool", lambda e: e.tensor_copy(out=identb[:], in_=identf[:]), reads=[B_const], writes=[B_const])
            S.op("pool", lambda e: e.memset(maskL[:], 0.0), writes=[B_const])
            S.op("pool", lambda e: e.affine_select(out=maskL[:], in_=maskL[:], pattern=[[1, 128]], compare_op=ALU.is_gt,
                                                   fill=1.0, base=0, channel_multiplier=-1), writes=[B_const])
            S.op("pool", lambda e: e.memset(small[:, 63:64], EPS), writes=[B_const])
            S.dma("sp", sem_c, [(flg[:], flags_in[:, :])], writes=[B_const])

            B_g = Buf("g")
            B_GT = Buf("GT")
            gs = af32(LT)
            S.dma("sp", sem_c, [(gs[0:NH, 0:256], relb[:, 1:257])], writes=[B_g])
            S.op("dve", lambda e: e.tensor_copy(out=gs[0:NH, 256:LT], in_=gs[0:NH, 255:256].to_broadcast([NH, LT - 256])),
                 reads=[B_g], writes=[B_g])
            sem_g = S.newsem()
            B_gD = Buf("gD")
            S.dma("sp", sem_g, [(gD[:, :], gs[0:NH, :])], reads=[B_g], writes=[B_gD])
            sem_g2 = S.newsem()
            S.dma("sp", sem_g2, [(GT[h], bass.AP(gD.tensor, h * LT, [[0, 128], [1, LT]])) for h in range(NH)],
                  reads=[B_gD], writes=[B_GT])
            B_T = Buf("T")
            S.dma("sp", sem_c, [(Tb[:, h, :], bass.AP(GT.tensor, h * 128 * LT + 127, [[LT - 1, 128], [1, 640]]))
                                for h in range(NH)], reads=[B_GT], writes=[B_T])
            S.op("pool", lambda e: e.memset(Tb[0:64, :, 576:640], -BIG), reads=[B_T], writes=[B_T])
            S.op("pool", lambda e: e.memset(Tb[64:128, :, 0:64], -BIG), reads=[B_T], writes=[B_T])
            S.barrier()
            areset()
            _ph[0] += 1
            if _ph[0] >= STOP:
                break

            cv = af32(32).rearrange("p (k v) -> p k v", v=2)
            cs = af32(32).rearrange("p (k v) -> p k v", v=2)
            sc_b = abf(32).rearrange("p (k v) -> p k v", v=2)
            gc = af32(32).rearrange("p (g k) -> p g k", g=2)
            adaR = af32(6 * D)
            b2 = af32(6 * D)
            slabs = [abf(KC * 512).rearrange("p (k c) -> p k c", k=KC) for _ in range(2)]
            B_slab = [Buf("slab0"), Buf("slab1")]
            sem_slab = [S.newsem(), S.newsem()]
            B_cv = Buf("cv")
            B_adaR = Buf("adaR")
            B_b2 = Buf("b2")
            B_adaD = Buf("adaD")
            B_modc = Buf("modc")
            sem_m = S.newsem()
            S.dma("sp", sem_m, [(cv, cvec[:, :, :]), (gc, gcols[:, :, :])], writes=[B_cv])
            S.dma("sp", sem_m, [(b2[0:1, :], b_ada[0:1, :]), (b2[1:2, :], b_ada[0:1, :])], writes=[B_b2])
            S.op("act", lambda e: e.activation(out=cs, in_=cv, func=AF.Sigmoid), reads=[B_cv], writes=[B_cv])
            S.op("dve", lambda e: e.tensor_tensor(out=sc_b, in0=cv, in1=cs, op=ALU.mult), reads=[B_cv], writes=[B_cv])
            for gi in range(24):
                sl = slabs[gi % 2]
                bs = B_slab[gi % 2]
                S.dma("pool", sem_slab[gi % 2], [(sl, wview(w_ada, gi * 512, 512))], writes=[bs])
                pb = PSB[gi % 2]
                ps = PS[gi % 2]
                for kc in range(KC):
                    S.op("pe", lambda e, ps=ps, sl=sl, kc=kc: e.matmul(ps[0:2, :], lhsT=sc_b[:, kc, :], rhs=sl[:, kc, :],
                                                                         start=(kc == 0), stop=(kc == KC - 1)),
                         reads=[B_cv, bs], writes=[pb])
                S.op("dve", lambda e, ps=ps, gi=gi: e.tensor_tensor(out=adaR[0:2, gi * 512:(gi + 1) * 512], in0=ps[0:2, :],
                                                                   in1=b2[0:2, gi * 512:(gi + 1) * 512], op=ALU.add),
                     reads=[pb, B_b2], writes=[B_adaR])
            S.dma("sp", sem_m, [(adaD[:, :], adaR[0:2, :])], reads=[B_adaR], writes=[B_adaD])
            adaC = af32(192).rearrange("p (c v) -> p c v", v=2)
            B_adaC = Buf("adaC")
            for c in range(96):
                S.op("pe", lambda e, c=c: e.transpose(PS[2][:, 2 * c:2 * c + 2], adaR[0:2, c * 128:(c + 1) * 128], identf[0:2, 0:2]),
                     reads=[B_adaR, B_const], writes=[PSB[2]])
            S.op("dve", lambda e: e.tensor_copy(out=adaC, in_=PS[2][:, 0:192].rearrange("p (c v) -> p c v", v=2)),
                 reads=[PSB[2]], writes=[B_adaC])
            for v in range(2):
                for (j, jsc, jsh, g) in ((0, 1, 0, 0), (2, 4, 3, 1)):
                    S.op("dve", lambda e, v=v, j=j, jsc=jsc, g=g: e.scalar_tensor_tensor(
                        out=modc[:, v, j, :], in0=adaC[:, jsc * 16:(jsc + 1) * 16, v], scalar=1.0, in1=gc[:, g, :],
                        op0=ALU.add, op1=ALU.mult), reads=[B_adaC, B_cv], writes=[B_modc])
                    S.op("dve", lambda e, v=v, j=j, jsh=jsh: e.tensor_copy(out=modc[:, v, j + 1, :],
                                                                            in_=adaC[:, jsh * 16:(jsh + 1) * 16, v]),
                         reads=[B_adaC], writes=[B_modc])
            S.barrier()
            areset()
            _ph[0] += 1
            if _ph[0] >= STOP:
                break

            def norm_tile(xt, bx, ntok, v, jm, hTdst, bh, junk, xn, bxn, ssq, rstd, bsm, psbase):
                S.op("act", lambda e: e.activation(out=junk[0:ntok, :], in_=xt[0:ntok, :], func=AF.Square,
                                                   accum_out=ssq[0:ntok, :]), reads=[bx], writes=[bxn, bsm])
                S.op("act", lambda e: e.activation(out=rstd[0:ntok, :], in_=ssq[0:ntok, :], func=AF.Sqrt,
                                                   scale=1.0 / D, bias=small[0:ntok, 63:64]), reads=[bsm, B_const], writes=[bsm])
                S.op("dve", lambda e: e.reciprocal(out=rstd[0:ntok, :], in_=rstd[0:ntok, :]), reads=[bsm], writes=[bsm])
                S.op("act", lambda e: e.activation(out=xn[0:ntok, :], in_=xt[0:ntok, :], func=AF.Identity,
                                                   scale=rstd[0:ntok, :]), reads=[bx, bsm], writes=[bxn])
                for q4 in range(4):
                    pi = psbase + q4
                    for i in range(4):
                        kc = q4 * 4 + i
                        S.op("pe", lambda e, pi=pi, i=i, kc=kc: e.transpose(PS[pi][:, i * 128:i * 128 + ntok],
                                                                              xn[0:ntok, kc * 128:(kc + 1) * 128],
                                                                              identf[0:ntok, 0:ntok]),
                             reads=[bxn, B_const], writes=[PSB[pi]])
                    for i in range(4):
                        kc = q4 * 4 + i
                        if q4 % 2 == 0:
                            S.op("dve", lambda e, pi=pi, i=i, kc=kc: e.tensor_scalar(
                                out=hTdst[:, kc, 0:ntok], in0=PS[pi][:, i * 128:i * 128 + ntok],
                                scalar1=modc[:, v, jm, kc:kc + 1], scalar2=modc[:, v, jm + 1, kc:kc + 1],
                                op0=ALU.mult, op1=ALU.add), reads=[PSB[pi], B_modc], writes=[bh])
                        else:
                            S.op("act", lambda e, pi=pi, i=i, kc=kc: e.activation(
                                out=hTdst[:, kc, 0:ntok], in_=PS[pi][:, i * 128:i * 128 + ntok], func=AF.Identity,
                                scale=modc[:, v, jm, kc:kc + 1], bias=modc[:, v, jm + 1, kc:kc + 1]),
                                reads=[PSB[pi], B_modc], writes=[bh])

            def pipeline(units, stages):
                n = len(units)
                K = len(stages)
                for i, U in enumerate(units):
                    U.idx = i
                for t in range(n + K - 1):
                    for k in range(K):
                        i = t - k
                        if 0 <= i < n:
                            U = units[i]
                            if k == 0 and U.pre is not None:
                                U.pre()
                            stages[k](U)
                            if k == K - 1 and U.post is not None:
                                U.post()

            class NU:
                pass

            def norm_pass(tiles, sc_off, sh_off, gidx, store_blk, dep_bufs):
                xt = [af32(D) for _ in range(3)]
                Bx = [Buf("x0"), Buf("x1"), Buf("x2")]
                semx = [S.newsem(), S.newsem(), S.newsem()]
                tb = [af32(D) for _ in range(2)]
                Bt = [Buf("t0"), Buf("t1")]
                junk = af32(D)
                Bjunk = Buf("junk")
                hb16 = [abf(D) for _ in range(2)]
                Bhb16 = [Buf("hb16_0"), Buf("hb16_1")]
                hst = [abf(KC * 512).rearrange("p (k c) -> p k c", k=KC) for _ in range(2)]
                Bh = [Buf("h0"), Buf("h1")]
                semh = [S.newsem(), S.newsem()]
                Bsm = [Buf("sm0"), Buf("sm1"), Buf("sm2"), Buf("sm3")]
                Ab = [af32(D) for _ in range(2)]
                Sb = [af32(D) for _ in range(2)]
                gtmp = af32(D)
                Bmodb = Buf("modb")
                semmb = S.newsem()
                S.dma("sp", semmb, [(gtmp, bass.AP(grow.tensor, gidx * D, [[0, 128], [1, D]]))] +
                      [(Ab[v], bass.AP(adaD.tensor, v * 6 * D + sc_off * D, [[0, 128], [1, D]])) for v in range(2)] +
                      [(Sb[v], bass.AP(adaD.tensor, v * 6 * D + sh_off * D, [[0, 128], [1, D]])) for v in range(2)],
                      reads=[B_adaD], writes=[Bmodb])
                for v in range(2):
                    S.op("dve", lambda e, v=v: e.scalar_tensor_tensor(out=Ab[v], in0=Ab[v], scalar=1.0, in1=gtmp, op0=ALU.add, op1=ALU.mult),
                         reads=[Bmodb], writes=[Bmodb])

                def s0(U):
                    xb, sb = U.idx % 3, U.idx % 4
                    ntok = U.ntok
                    S.dma("sp", semx[xb], [(xt[xb][0:ntok, :], U.src)], reads=U.srcbufs, writes=[Bx[xb]])
                    S.op("act", lambda e: e.activation(out=junk[0:ntok, :], in_=xt[xb][0:ntok, :], func=AF.Square,
                                                       accum_out=small[0:ntok, 2 * sb:2 * sb + 1]), reads=[Bx[xb]], writes=[Bjunk, Bsm[sb]])

                def s1(U):
                    xb, sb, s4 = U.idx % 3, U.idx % 2, U.idx % 4
                    ntok, v = U.ntok, U.v
                    rstd = small[:, 2 * s4 + 1:2 * s4 + 2]
                    S.op("act", lambda e: e.activation(out=rstd[0:ntok, :], in_=small[0:ntok, 2 * s4:2 * s4 + 1], func=AF.Sqrt,
                                                       scale=1.0 / D, bias=small[0:ntok, 63:64]), reads=[Bsm[s4], B_const], writes=[Bsm[s4]])
                    S.op("dve", lambda e: e.reciprocal(out=rstd[0:ntok, :], in_=rstd[0:ntok, :]), reads=[Bsm[s4]], writes=[Bsm[s4]])
                    S.op("dve", lambda e: e.scalar_tensor_tensor(out=tb[sb][0:ntok, :], in0=xt[xb][0:ntok, :], scalar=rstd[0:ntok, :],
                                                                 in1=Ab[v][0:ntok, :], op0=ALU.mult, op1=ALU.mult),
                         reads=[Bx[xb], Bsm[s4], Bmodb], writes=[Bt[sb]])
                    S.op("pool", lambda e: e.tensor_tensor(out=hb16[sb][0:ntok, :], in0=tb[sb][0:ntok, :], in1=Sb[v][0:ntok, :], op=ALU.add),
                         reads=[Bt[sb], Bmodb], writes=[Bhb16[sb]])

                def s2(U):
                    sb = U.idx % 2
                    ntok = U.ntok
                    for q4 in range(4):
                        pi = 4 * sb + q4
                        for i in range(4):
                            kc = q4 * 4 + i
                            S.op("pe", lambda e, pi=pi, i=i, kc=kc: e.matmul(PS[pi][:, i * 128:i * 128 + ntok],
                                                                             lhsT=hb16[sb][0:ntok, kc * 128:(kc + 1) * 128],
                                                                             rhs=identb[0:ntok, 0:ntok], start=True, stop=True),
                                 reads=[Bhb16[sb], B_const], writes=[PSB[pi]])

                def s3(U):
                    sb = U.idx % 2
                    ntok = U.ntok
                    for q4 in range(4):
                        pi = 4 * sb + q4
                        src = PS[pi][:, :].rearrange("p (k c) -> p k c", k=4)[:, :, 0:ntok]
                        dst = U.hTdst[:, q4 * 4:(q4 + 1) * 4, 0:ntok]
                        if q4 % 2 == 0:
                            S.op("act", lambda e, src=src, dst=dst: e.activation(out=dst, in_=src, func=AF.Copy), reads=[PSB[pi]], writes=[U.bh])
                        else:
                            S.op("dve", lambda e, src=src, dst=dst: e.tensor_copy(out=dst, in_=src), reads=[PSB[pi]], writes=[U.bh])

                units = []
                for (src, srcbufs, ntok, v, blk, j, last) in tiles:
                    U = NU()
                    U.pre = None
                    U.post = None
                    U.src, U.srcbufs, U.ntok, U.v = src, srcbufs, ntok, v
                    hb = blk % 2
                    U.hTdst = hst[hb][:, :, j * 128:(j + 1) * 128]
                    U.bh = Bh[hb]
                    if last:
                        def post(blk=blk, hb=hb):
                            store_blk(blk, hst[hb], Bh[hb], semh[hb])
                        U.post = post
                    units.append(U)
                pipeline(units, [s0, s1, s2, s3])

            B_HT = [Buf("HT%d" % i) for i in range(NBLK + 1)]
            tiles1 = []
            for blk in range(NBLK + 1):
                nt = 4 if blk < NBLK else 1
                for j in range(nt):
                    src = xin[blk, j * 128:(j + 1) * 128, :] if blk < NBLK else xs_in[:, :]
                    tiles1.append((src, [], 128 if blk < NBLK else NS, 0 if blk < NBLK else 1, blk, j, j == nt - 1))

            def store1(blk, st, bst, sem):
                S.dma("sp", sem, [(HT[blk], st)], reads=[bst], writes=[B_HT[blk]])
            norm_pass(tiles1, 1, 0, 0, store1, None)
            S.barrier()
            areset()
            _ph[0] += 1
            if _ph[0] >= STOP:
                break

            wg = [abf(KC * 1024).rearrange("p (k c) -> p k c", k=KC) for _ in range(2)]
            Bwg = [Buf("wg0"), Buf("wg1")]
            semw = [S.newsem(), S.newsem()]
            hbk = [abf(KC * 512).rearrange("p (k c) -> p k c", k=KC) for _ in range(2)]
            Bhb = [Buf("hb0"), Buf("hb1")]
            semhb = [S.newsem(), S.newsem()]
            kvf = [af32(1024) for _ in range(2)]
            Bkvf = [Buf("kvf0"), Buf("kvf1")]
            semkvf = [S.newsem(), S.newsem()]
            kvb = [abf(1024) for _ in range(2)]
            Bkvb = [Buf("kvb0"), Buf("kvb1")]
            semkvb = [S.newsem(), S.newsem()]
            ktb = [abf(NH * 512).rearrange("p (h c) -> p h c", h=NH) for _ in range(2)]
            Bktb = [Buf("ktb0"), Buf("ktb1")]
            semktb = [S.newsem(), S.newsem()]
            qst = [abf(NH * 512).rearrange("p (h c) -> p h c", h=NH) for _ in range(2)]
            Bqst = [Buf("qst0"), Buf("qst1")]
            semqst = [S.newsem(), S.newsem()]
            gst = [af32(NH * 512).rearrange("p (h c) -> p h c", h=NH) for _ in range(2)]
            Bgst = [Buf("gst0"), Buf("gst1")]
            semgst = [S.newsem(), S.newsem()]
            B_scr = Buf("scr_w")

            own_blocks = [(0, 512, 0), (1, 512, 512), (2, 512, 1024), (3, 512, 1536), (NBLK, NS, 2048)]
            a_blocks = [(0, 512, 0), (1, 512, 512), (4, 512, 1024), (2, 512, 1536), (3, 512, 2048), (5, 512, 2560), (NBLK, NS, 3072)]
            b_blocks = [(0, 512, 0), (1, 512, 512), (2, 512, 1024), (3, 512, 1536)] + \
                       [(6 + i, 512, 2048 + 512 * i) for i in range(14)] + [(NBLK, NS, 9216)]
            own_row = {0: 0, 1: 512, 2: 1024, 3: 1536, NBLK: 2048}
            a_out_row = {2: 0, NBLK: 512}
            groups = [("q", 0, QaT), ("k", 1024, "a"), ("v", 2048, "a"), ("q", 3072, QbT), ("k", 4096, "b"), ("v", 5120, "b"),
                      ("g", 6144, (GA, 0)), ("g", 7168, (GA, 8)), ("g", 8192, (GB, 0)), ("g", 9216, (GB, 8))]
            cnt = {"hb": 0, "kv": 0, "kt": 0, "q": 0, "g": 0, "ps": 0}
            def blist_of(kind, dst):
                if kind in ("q", "g"):
                    return own_blocks
                return a_blocks if dst == "a" else b_blocks

            def load_w2(gi):
                c0_ = groups[gi][1]
                W_ = wg[gi % 2]
                S.dma("pool", semw[gi % 2], [(W_[:, :, 0:512], wview(w_in, c0_, 512)), (W_[:, :, 512:1024], wview(w_in, c0_ + 512, 512))],
                      writes=[Bwg[gi % 2]])
            seq2 = [blk for (kind, c0, dst) in groups for (blk, ntk, idx0) in blist_of(kind, dst)]

            def load_h2(n):
                S.dma("sp", semhb[n % 2], [(hbk[n % 2], HT[seq2[n]])], reads=[B_HT[seq2[n]]], writes=[Bhb[n % 2]])
            load_w2(0)
            load_h2(0)
            pendk = [None]
            for gi, (kind, c0, dst) in enumerate(groups):
                wb = gi % 2
                W = wg[wb]
                if gi + 1 < len(groups):
                    load_w2(gi + 1)
                blist = blist_of(kind, dst)
                for (blk, ntk, idx0) in blist:
                    hi_ = cnt["hb"] % 2
                    hb_ = hbk[hi_]
                    if cnt["hb"] + 1 < len(seq2):
                        load_h2(cnt["hb"] + 1)
                    cnt["hb"] += 1
                    if kind == "q":
                        qi = cnt["q"] % 2
                        cnt["q"] += 1
                        for cc in range(8):
                            pi = cnt["ps"] % 8
                            cnt["ps"] += 1
                            for kc in range(KC):
                                S.op("pe", lambda e, pi=pi, cc=cc, kc=kc, W=W, hb_=hb_, ntk=ntk: e.matmul(
                                    PS[pi][:, 0:ntk], lhsT=W[:, kc, cc * 128:(cc + 1) * 128], rhs=hb_[:, kc, 0:ntk],
                                    start=(kc == 0), stop=(kc == KC - 1)), reads=[Bwg[wb], Bhb[hi_]], writes=[PSB[pi]])
                            eng = "act" if cc % 2 == 0 else "dve"
                            if eng == "act":
                                S.op("act", lambda e, pi=pi, cc=cc, qi=qi, ntk=ntk: e.activation(
                                    out=qst[qi][:, cc, 0:ntk], in_=PS[pi][:, 0:ntk], func=AF.Copy), reads=[PSB[pi]], writes=[Bqst[qi]])
                            else:
                                S.op("dve", lambda e, pi=pi, cc=cc, qi=qi, ntk=ntk: e.tensor_copy(
                                    out=qst[qi][:, cc, 0:ntk], in_=PS[pi][:, 0:ntk]), reads=[PSB[pi]], writes=[Bqst[qi]])
                        r0 = own_row[blk]
                        S.dma("sp", semqst[qi], [(dst[:, :, r0:r0 + ntk].rearrange("h p c -> p h c"), qst[qi][:, :, 0:ntk])],
                              reads=[Bqst[qi]], writes=[B_scr])
                    elif kind == "g":
                        G, ch0 = dst
                        qi = cnt["g"] % 2
                        cnt["g"] += 1
                        for cc in range(8):
                            pi = cnt["ps"] % 8
                            cnt["ps"] += 1
                            for kc in range(KC):
                                S.op("pe", lambda e, pi=pi, cc=cc, kc=kc, W=W, hb_=hb_, ntk=ntk: e.matmul(
                                    PS[pi][:, 0:ntk], lhsT=W[:, kc, cc * 128:(cc + 1) * 128], rhs=hb_[:, kc, 0:ntk],
                                    start=(kc == 0), stop=(kc == KC - 1)), reads=[Bwg[wb], Bhb[hi_]], writes=[PSB[pi]])
                            S.op("act", lambda e, pi=pi, cc=cc, qi=qi, ntk=ntk: e.activation(
                                out=gst[qi][:, cc, 0:ntk], in_=PS[pi][:, 0:ntk], func=AF.Sigmoid), reads=[PSB[pi]], writes=[Bgst[qi]])
                        r0 = own_row[blk]
                        S.dma("sp", semgst[qi], [(G[ch0:ch0 + 8, :, r0:r0 + ntk].rearrange("h p c -> p h c"), gst[qi][:, :, 0:ntk])],
                              reads=[Bgst[qi]], writes=[B_scr])
                    else:
                        isA = (dst == "a")
                        KT_, VS_ = (KaT, VaS) if isA else (KbT, VbS)
                        nt = 4 if ntk == 512 else 1
                        ki = cnt["kt"] % 2
                        if kind == "k":
                            cnt["kt"] += 1
                        for j in range(nt):
                            ntok = 128 if ntk == 512 else NS
                            fi = cnt["kv"] % 2
                            cnt["kv"] += 1
                            for half in range(2):
                                pi = cnt["ps"] % 8
                                cnt["ps"] += 1
                                for kc in range(KC):
                                    S.op("pe", lambda e, pi=pi, half=half, kc=kc, W=W, hb_=hb_, j=j, ntok=ntok: e.matmul(
                                        PS[pi][0:ntok, :], lhsT=hb_[:, kc, j * 128:j * 128 + ntok], rhs=W[:, kc, half * 512:(half + 1) * 512],
                                        start=(kc == 0), stop=(kc == KC - 1)), reads=[Bwg[wb], Bhb[hi_]], writes=[PSB[pi]])
                                S.op("act", lambda e, pi=pi, half=half, fi=fi, ntok=ntok: e.activation(
                                    out=kvf[fi][0:ntok, half * 512:(half + 1) * 512], in_=PS[pi][0:ntok, :], func=AF.Copy),
                                    reads=[PSB[pi]], writes=[Bkvf[fi]])
                                S.op("pool", lambda e, half=half, fi=fi, ntok=ntok: e.tensor_copy(
                                    out=kvb[fi][0:ntok, half * 512:(half + 1) * 512],
                                    in_=kvf[fi][0:ntok, half * 512:(half + 1) * 512]),
                                    reads=[Bkvf[fi]], writes=[Bkvb[fi]])
                            outs = []
                            if isA and blk in a_out_row:
                                o_t = ka_o if kind == "k" else va_o
                                r0 = a_out_row[blk] + j * 128
                                outs.append((o_t[r0:r0 + ntok, :], kvf[fi][0:ntok, :]))
                            if (not isA) and blk in own_row:
                                o_t = kb_o if kind == "k" else vb_o
                                r0 = own_row[blk] + j * 128
                                outs.append((o_t[r0:r0 + ntok, :], kvf[fi][0:ntok, :]))
                            if outs:
                                S.dma("sp", semkvf[fi], outs, reads=[Bkvf[fi]], writes=[])
                            if kind == "v":
                                tix = (idx0 + j * 128) // 128
                                S.dma("sp", semkvb[fi], [(VS_[tix, 0:ntok, :], kvb[fi][0:ntok, :])], reads=[Bkvb[fi]], writes=[B_scr])
                            else:
                                def ktrans(fi=fi, ntok=ntok, ki=ki, j=j):
                                    pi = cnt["ps"] % 8
                                    cnt["ps"] += 1
                                    psb16 = PS[pi][:, :].bitcast(BF16)
                                    for h in range(NH):
                                        S.op("pe", lambda e, psb16=psb16, h=h, fi=fi, ntok=ntok: e.transpose(
                                            psb16[:, h * 128:h * 128 + ntok], kvb[fi][0:ntok, h * 128:(h + 1) * 128], identb[0:ntok, 0:ntok]),
                                            reads=[Bkvb[fi], B_const], writes=[PSB[pi]])
                                    S.op("dve", lambda e, psb16=psb16, ki=ki, j=j, ntok=ntok: e.tensor_copy(
                                        out=ktb[ki][:, :, j * 128:j * 128 + ntok],
                                        in_=psb16.rearrange("p (h c) -> p h c", h=NH)[:, :, 0:ntok]), reads=[PSB[pi]], writes=[Bktb[ki]])
                                if pendk[0] is not None:
                                    pendk[0]()
                                pendk[0] = ktrans
                        if kind == "k" and pendk[0] is not None:
                            pendk[0]()
                            pendk[0] = None
                        if kind == "k":
                            S.dma("sp", semktb[ki], [(KT_[:, :, idx0:idx0 + ntk].rearrange("h p c -> p h c"), ktb[ki][:, :, 0:ntk])],
                                  reads=[Bktb[ki]], writes=[B_scr])
            S.barrier()
            areset()
            _ph[0] += 1
            if _ph[0] >= STOP:
                break

            def sm(i):
                return small[:, i:i + 1]
            qtA = [abf(NH * 128).rearrange("p (h c) -> p h c", h=NH) for _ in range(2)]
            ktA = [abf(NH * 640).rearrange("p (h c) -> p h c", h=NH) for _ in range(2)]
            vtA = [abf(5 * 1024).rearrange("p (t c) -> p t c", t=5) for _ in range(2)]
            BldA = [Buf("ldA0"), Buf("ldA1")]
            semldA = [S.newsem(), S.newsem()]
            s_sb = [af32(640) for _ in range(2)]
            Bs = [Buf("s0"), Buf("s1")]
            p_sb = [abf(640) for _ in range(2)]
            Bp = [Buf("p0"), Buf("p1")]
            pT = [abf(5 * 128).rearrange("p (t c) -> p t c", t=5) for _ in range(2)]
            BpT = [Buf("pT0"), Buf("pT1")]
            oa = [abf(1024) for _ in range(3)]
            Boa = [Buf("oa0"), Buf("oa1"), Buf("oa2")]
            oaTs = [abf(NH * 512).rearrange("p (h c) -> p h c", h=NH) for _ in range(2)]
            BoaT = [Buf("oaT0"), Buf("oaT1")]
            semoaT = [S.newsem(), S.newsem()]
            Bst = [Buf("st%d" % i) for i in range(8)]
            B_OaT = Buf("OaT")
            ckA = abf(4 * 1024).rearrange("p (t c) -> p t c", t=4)
            cvA = abf(4 * 1024).rearrange("p (t c) -> p t c", t=4)
            ktS = abf(NH * 544).rearrange("p (h c) -> p h c", h=NH)
            qtS = abf(NH * NS).rearrange("p (h c) -> p h c", h=NH)
            vS0 = abf(1024)
            BckA = Buf("ckA")
            BcvA = Buf("cvA")
            BktS = Buf("ktS")
            semcA = S.newsem()
            semcA2 = S.newsem()

            class AU:
                pass

            def mkA(nq, hsel, q_ap, k_ap, nk, vblocks, bq, bk, bv, flag_c0, oa_t, boa):
                U = AU()
                U.nq, U.hsel, U.q_ap, U.k_ap, U.nk, U.vblocks = nq, hsel, q_ap, k_ap, nk, vblocks
                U.bq, U.bk, U.bv, U.flag_c0, U.oa_t, U.boa = bq, bk, bv, flag_c0, oa_t, boa
                U.pre = None
                U.post = None
                return U

            def a0(U):
                u = U.idx % 2
                nq, nk = U.nq, U.nk
                n1 = min(nk, 512)
                S.op("pe", lambda e: e.matmul(PS[2 * u][0:nq, 0:n1], lhsT=U.q_ap, rhs=U.k_ap[:, 0:n1], start=True, stop=True),
                     reads=[U.bq, U.bk], writes=[PSB[2 * u]])
                if nk > 512:
                    S.op("pe", lambda e: e.matmul(PS[2 * u + 1][0:nq, 0:nk - 512], lhsT=U.q_ap, rhs=U.k_ap[:, 512:nk], start=True, stop=True),
                         reads=[U.bq, U.bk], writes=[PSB[2 * u + 1]])

            def a1(U):
                u = U.idx % 2
                nq, nk, hsel = U.nq, U.nk, U.hsel
                n1 = min(nk, 512)
                s_ = s_sb[u]
                st = 8 + 4 * (U.idx % 8)
                bst = Bst[U.idx % 8]
                S.op("dve", lambda e: e.scalar_tensor_tensor(out=s_[0:nq, 0:n1], in0=PS[2 * u][0:nq, 0:n1], scalar=SCALE,
                                                             in1=Tb[0:nq, hsel, 0:n1], op0=ALU.mult, op1=ALU.add),
                     reads=[PSB[2 * u], B_T], writes=[Bs[u]])
                if nk > 512:
                    S.op("dve", lambda e: e.scalar_tensor_tensor(out=s_[0:nq, 512:nk], in0=PS[2 * u + 1][0:nq, 0:nk - 512], scalar=SCALE,
                                                                 in1=Tb[0:nq, hsel, 512:nk], op0=ALU.mult, op1=ALU.add),
                         reads=[PSB[2 * u + 1], B_T], writes=[Bs[u]])
                if U.flag_c0 is not None:
                    fc = U.flag_c0
                    S.op("dve", lambda e: e.tensor_scalar(out=s_[0:nq, fc:nk], in0=s_[0:nq, fc:nk], scalar1=flg[0:nq, 0:1],
                                                          scalar2=None, op0=ALU.add), reads=[Bs[u], B_const], writes=[Bs[u]])
                S.op("dve", lambda e: e.reduce_max(out=sm(st)[0:nq, :], in_=s_[0:nq, 0:nk], axis=AX.X), reads=[Bs[u]], writes=[bst])
                S.op("dve", lambda e: e.tensor_scalar(out=sm(st + 1)[0:nq, :], in0=sm(st)[0:nq, :], scalar1=-1.0, scalar2=None,
                                                      op0=ALU.mult), reads=[bst], writes=[bst])

            def a2(U):
                u = U.idx % 2
                nq, nk = U.nq, U.nk
                st = 8 + 4 * (U.idx % 8)
                bst = Bst[U.idx % 8]
                S.op("act", lambda e: e.activation(out=p_sb[u][0:nq, 0:nk], in_=s_sb[u][0:nq, 0:nk], func=AF.Exp, bias=sm(st + 1)[0:nq, :],
                                                   scale=1.0, accum_out=sm(st + 2)[0:nq, :]), reads=[Bs[u], bst], writes=[Bp[u], bst])

            def a3(U):
                u = U.idx % 2
                nq = U.nq
                st = 8 + 4 * (U.idx % 8)
                bst = Bst[U.idx % 8]
                S.op("dve", lambda e: e.reciprocal(out=sm(st + 3)[0:nq, :], in_=sm(st + 2)[0:nq, :]), reads=[bst], writes=[bst])
                psb16 = PS[4 + u][:, :].bitcast(BF16)
                for bi, (koff, nkb, v_ap) in enumerate(U.vblocks):
                    S.op("pe", lambda e, bi=bi, koff=koff, nkb=nkb: e.transpose(psb16[0:nkb, bi * 128:bi * 128 + nq],
                                                                                 p_sb[u][0:nq, koff:koff + nkb], identb[0:nq, 0:nq]),
                         reads=[Bp[u], B_const], writes=[PSB[4 + u]])

            def a4(U):
                u = U.idx % 2
                nq = U.nq
                nb = len(U.vblocks)
                psb16 = PS[4 + u][:, :].bitcast(BF16)
                S.op("act", lambda e: e.activation(out=pT[u][:, 0:nb, 0:nq],
                                                   in_=psb16[:, 0:nb * 128].rearrange("p (t c) -> p t c", t=nb)[:, :, 0:nq],
                                                   func=AF.Copy), reads=[PSB[4 + u]], writes=[BpT[u]])

            def a5(U):
                u = U.idx % 2
                nq = U.nq
                nb = len(U.vblocks)
                for bi, (koff, nkb, v_ap) in enumerate(U.vblocks):
                    S.op("pe", lambda e, bi=bi, nkb=nkb, v_ap=v_ap: e.matmul(PS[6 + u][0:nq, 0:128], lhsT=pT[u][0:nkb, bi, 0:nq], rhs=v_ap,
                                                                             start=(bi == 0), stop=(bi == nb - 1)),
                         reads=[BpT[u], U.bv], writes=[PSB[6 + u]])

            def a6(U):
                u = U.idx % 2
                nq, hsel = U.nq, U.hsel
                st = 8 + 4 * (U.idx % 8)
                bst = Bst[U.idx % 8]
                S.op("act", lambda e: e.activation(out=U.oa_t[0:nq, hsel * 128:(hsel + 1) * 128], in_=PS[6 + u][0:nq, 0:128],
                                                   func=AF.Identity, scale=sm(st + 3)[0:nq, :]), reads=[PSB[6 + u], bst], writes=[U.boa])

            afc = [0]

            def a_finish(nq, oa_t, boa, stage, bstage, col0):
                pi = 5
                psb16 = PS[pi][:, :].bitcast(BF16)
                for h in range(NH):
                    S.op("pe", lambda e, h=h: e.transpose(psb16[:, h * 128:h * 128 + nq], oa_t[0:nq, h * 128:(h + 1) * 128],
                                                           identb[0:nq, 0:nq]), reads=[boa, B_const], writes=[PSB[pi]])
                S.op("dve", lambda e: e.tensor_copy(out=stage[:, :, col0:col0 + nq],
                                                    in_=psb16.rearrange("p (h c) -> p h c", h=NH)[:, :, 0:nq]),
                     reads=[PSB[pi]], writes=[bstage])

            unitsA = []
            pc = 0
            for piece in range(2):
                for j in range(8):
                    li = pc % 2
                    oi = pc % 3
                    sti = (pc // 4) % 2
                    tok0 = piece * 1024 + 128 * j
                    k0 = piece * 1536 + 128 * j
                    t0_ = k0 // 128
                    flag_c0 = (1024 - 128 * j) if (piece == 0 and j >= 4) else None

                    def preA(li=li, tok0=tok0, k0=k0, t0_=t0_):
                        S.dma("sp", semldA[li], [
                            (qtA[li], QaT[:, :, tok0:tok0 + 128].rearrange("h p c -> p h c")),
                            (ktA[li], KaT[:, :, k0:k0 + 640].rearrange("h p c -> p h c")),
                            (vtA[li], VaS[t0_:t0_ + 5].rearrange("t p c -> p t c"))], reads=[B_scr], writes=[BldA[li]])

                    def postA(pc=pc, oi=oi, sti=sti):
                        a_finish(128, oa[oi], Boa[oi], oaTs[sti], BoaT[sti], (pc % 4) * 128)
                        if pc % 4 == 3:
                            r0 = (pc // 4) * 512
                            S.dma("sp", semoaT[sti], [(OaT[:, :, r0:r0 + 512].rearrange("h p c -> p h c"), oaTs[sti])],
                                  reads=[BoaT[sti]], writes=[B_OaT])
                    for h in range(NH):
                        vbl = [(128 * t, 128, vtA[li][:, t, h * 128:(h + 1) * 128]) for t in range(5)]
                        U = mkA(128, h, qtA[li][:, h, :], ktA[li][:, h, :], 640, vbl, BldA[li], BldA[li], BldA[li], flag_c0, oa[oi], Boa[oi])
                        if h == 0:
                            U.pre = preA
                        if h == NH - 1:
                            U.post = postA
                        unitsA.append(U)
                    pc += 1
            oiS = pc % 3

            def preS():
                S.dma("pool", semcA, [(ckA, ca_k.rearrange("(t p) c -> p t c", p=128)), (cvA, ca_v.rearrange("(t p) c -> p t c", p=128))],
                      writes=[BckA, BcvA])
                S.dma("sp", semcA2, [(qtS, QaT[:, :, 2048:2080].rearrange("h p c -> p h c")),
                                     (ktS[:, :, 0:NS], KaT[:, :, 3072:3104].rearrange("h p c -> p h c")),
                                     (vS0[0:NS, :], VaS[24, 0:NS, :])], reads=[B_scr], writes=[BktS])
                for h in range(NH):
                    pi = h % 2
                    psb16 = PS[pi][:, :].bitcast(BF16)
                    for t in range(4):
                        S.op("pe", lambda e, psb16=psb16, h=h, t=t: e.transpose(psb16[:, t * 128:(t + 1) * 128], ckA[:, t, h * 128:(h + 1) * 128],
                                                                                  identb[:, :]), reads=[BckA, B_const], writes=[PSB[pi]])
                    S.op("dve", lambda e, psb16=psb16, h=h: e.tensor_copy(out=ktS[:, h, NS:NS + 512], in_=psb16[:, 0:512]),
                         reads=[PSB[pi]], writes=[BktS])

            def postS():
                a_finish(NS, oa[oiS], Boa[oiS], oaTs[0], BoaT[0], 0)
                S.dma("sp", semoaT[0], [(OaT[:, :, 2048:2080].rearrange("h p c -> p h c"), oaTs[0][:, :, 0:NS])],
                      reads=[BoaT[0]], writes=[B_OaT])
            for h in range(NH):
                vbl = [(0, NS, vS0[0:NS, h * 128:(h + 1) * 128])] + \
                      [(NS + 128 * t, 128, cvA[:, t, h * 128:(h + 1) * 128]) for t in range(4)]
                U = mkA(NS, h, qtS[:, h, :], ktS[:, h, :], 544, vbl, BktS, BktS, BcvA, None, oa[oiS], Boa[oiS])
                if h == 0:
                    U.pre = preS
                if h == NH - 1:
                    U.post = postS
                unitsA.append(U)
            pipeline(unitsA, [a0, a1, a2, a3, a4, a5, a6])
            S.barrier()
            areset()
            _ph[0] += 1
            if _ph[0] >= STOP:
                break

            qb = abf(NH * 1024).rearrange("p (h c) -> p h c", h=NH)
            Bqb = Buf("qb")
            semqb = S.newsem()
            ktB = [abf(NH * 1024).rearrange("p (h c) -> p h c", h=NH) for _ in range(2)]
            vtB = [abf(8 * 1024).rearrange("p (t c) -> p t c", t=8) for _ in range(2)]
            BktB = [Buf("ktB0"), Buf("ktB1")]
            BvtB = [Buf("vtB0"), Buf("vtB1")]
            semkB = [S.newsem(), S.newsem()]
            semvB = [S.newsem(), S.newsem()]
            acc = af32(NH * 1024).rearrange("p (h c) -> p h c", h=NH)
            Bacc = Buf("acc")
            carry = af32(64)
            Bcar = Buf("carry")
            m_sb = [af32(1024) for _ in range(2)]
            Bm = [Buf("m0"), Buf("m1")]
            Pb = [af32(1026) for _ in range(3)]
            BPb = [Buf("P0"), Buf("P1"), Buf("P2")]
            A_sb = [abf(1024) for _ in range(2)]
            BA = [Buf("A0"), Buf("A1")]
            AT_sb = [abf(8 * 128).rearrange("p (t c) -> p t c", t=8) for _ in range(2)]
            BAT = [Buf("AT0"), Buf("AT1")]
            obst = abf(NH * 1024).rearrange("p (h c) -> p h c", h=NH)
            Bobst = Buf("obst")
            semob = S.newsem()
            B_ObT = Buf("ObT")

            class BU:
                pass

            def mk_unit(nq, q_ap, bq, k_ap, bk, c0, c1, blocks, bv, diag, bias_ap, car_ap, acc_ap, first):
                U = BU()
                U.nq, U.q_ap, U.bq, U.k_ap, U.bk, U.c0, U.c1 = nq, q_ap, bq, k_ap, bk, c0, c1
                U.blocks, U.bv, U.diag, U.bias_ap, U.car_ap, U.acc_ap, U.first = blocks, bv, diag, bias_ap, car_ap, acc_ap, first
                U.pre = None
                U.post = None
                ch = []
                c = c0
                while c < c1:
                    w = min(512, c1 - c)
                    ch.append((c, w))
                    c += w
                U.chunks = ch
                return U

            def st_z(U):
                u = U.idx % 2
                zb = [2 * u, 2 * u + 1]
                for ci, (cc, w) in enumerate(U.chunks):
                    S.op("pe", lambda e, ci=ci, cc=cc, w=w: e.matmul(PS[zb[ci]][0:U.nq, 0:w], lhsT=U.q_ap, rhs=U.k_ap[:, cc:cc + w],
                                                                     start=True, stop=True), reads=[U.bq, U.bk], writes=[PSB[zb[ci]]])

            def st_sig(U):
                u = U.idx % 2
                zb = [2 * u, 2 * u + 1]
                m_ = m_sb[u]
                nq = U.nq
                for ci, (cc, w) in enumerate(U.chunks):
                    if U.bias_ap is None:
                        S.op("act", lambda e, ci=ci, cc=cc, w=w: e.activation(out=m_[0:nq, cc:cc + w], in_=PS[zb[ci]][0:nq, 0:w],
                                                                              func=AF.Sigmoid, scale=-SCALE),
                             reads=[PSB[zb[ci]]], writes=[Bm[u]])
                    else:
                        S.op("act", lambda e, ci=ci, cc=cc, w=w: e.activation(out=m_[0:nq, cc:cc + w], in_=PS[zb[ci]][0:nq, 0:w],
                                                                              func=AF.Sigmoid, scale=-SCALE, bias=U.bias_ap[0:nq, :]),
                             reads=[PSB[zb[ci]], B_const], writes=[Bm[u]])

            def st_pinit(U):
                u3 = U.idx % 3
                P_ = Pb[u3]
                nq, c0 = U.nq, U.c0
                if U.first:
                    S.op("pool", lambda e: e.memset(P_[0:nq, c0:c0 + 1], 1.0), writes=[BPb[u3]])
                else:
                    S.op("act", lambda e: e.activation(out=P_[0:nq, c0:c0 + 1], in_=U.car_ap[0:nq, :], func=AF.Copy),
                         reads=[Bcar], writes=[BPb[u3]])

            def st_sig2(U):
                st_sig(U)
                st_pinit(U)

            def st_scan(U):
                u = U.idx % 2
                u3 = U.idx % 3
                m_, P_ = m_sb[u], Pb[u3]
                nq, c0, c1 = U.nq, U.c0, U.c1
                if U.diag:
                    nkb0 = U.blocks[0][1]
                    S.op("dve", lambda e: e.tensor_tensor(out=m_[0:nq, c0:c0 + nkb0], in0=m_[0:nq, c0:c0 + nkb0],
                                                          in1=maskL[0:nq, 0:nkb0], op=ALU.max),
                         reads=[Bm[u], B_const], writes=[Bm[u]])
                S.op("dve", lambda e: e.tensor_tensor_scan(out=P_[0:nq, c0 + 1:c1 + 1], data0=m_[0:nq, c0:c1], data1=m_[0:nq, c0:c1],
                                                           initial=P_[0:nq, c0:c0 + 1], op0=ALU.mult, op1=ALU.bypass),
                     reads=[Bm[u], BPb[u3]], writes=[BPb[u3]])

            def st_sub(U):
                u = U.idx % 2
                u3 = U.idx % 3
                P_, A_ = Pb[u3], A_sb[u]
                nq, c0, c1 = U.nq, U.c0, U.c1
                S.op("pool", lambda e: e.tensor_tensor(out=A_[0:nq, c0:c1], in0=P_[0:nq, c0:c1], in1=P_[0:nq, c0 + 1:c1 + 1],
                                                       op=ALU.subtract), reads=[BPb[u3]], writes=[BA[u]])
                S.op("act", lambda e: e.activation(out=U.car_ap[0:nq, :], in_=P_[0:nq, c1:c1 + 1], func=AF.Copy),
                     reads=[BPb[u3]], writes=[Bcar])

            def st_tr(U):
                u = U.idx % 2
                A_ = A_sb[u]
                nq = U.nq
                psb16 = PS[4 + u][:, :].bitcast(BF16)
                for bi, (off, nkb, v_ap) in enumerate(U.blocks):
                    S.op("pe", lambda e, bi=bi, off=off, nkb=nkb: e.transpose(psb16[0:nkb, bi * 128:bi * 128 + nq],
                                                                               A_[0:nq, off:off + nkb], identb[0:nq, 0:nq]),
                         reads=[BA[u], B_const], writes=[PSB[4 + u]])

            def st_cp(U):
                u = U.idx % 2
                nq = U.nq
                nb = len(U.blocks)
                psb16 = PS[4 + u][:, :].bitcast(BF16)
                S.op("act", lambda e: e.activation(out=AT_sb[u][:, 0:nb, 0:nq],
                                                   in_=psb16[:, 0:nb * 128].rearrange("p (t c) -> p t c", t=nb)[:, :, 0:nq],
                                                   func=AF.Copy), reads=[PSB[4 + u]], writes=[BAT[u]])

            def st_av(U):
                u = U.idx % 2
                nq = U.nq
                nb = len(U.blocks)
                for bi, (off, nkb, v_ap) in enumerate(U.blocks):
                    S.op("pe", lambda e, bi=bi, nkb=nkb, v_ap=v_ap: e.matmul(PS[6 + u][:, 0:nq], lhsT=v_ap, rhs=AT_sb[u][0:nkb, bi, 0:nq],
                                                                             start=(bi == 0), stop=(bi == nb - 1)),
                         reads=[BAT[u], U.bv], writes=[PSB[6 + u]])

            def st_acc(U):
                u = U.idx % 2
                nq = U.nq
                if U.first:
                    S.op("dve", lambda e: e.tensor_copy(out=U.acc_ap, in_=PS[6 + u][:, 0:nq]), reads=[PSB[6 + u]], writes=[Bacc])
                else:
                    S.op("dve", lambda e: e.tensor_tensor(out=U.acc_ap, in0=PS[6 + u][:, 0:nq], in1=U.acc_ap, op=ALU.add),
                         reads=[PSB[6 + u], Bacc], writes=[Bacc])

            def pipeline(units, stages):
                n = len(units)
                K = len(stages)
                for i, U in enumerate(units):
                    U.idx = i
                for t in range(n + K - 1):
                    for k in range(K):
                        i = t - k
                        if 0 <= i < n:
                            U = units[i]
                            if k == 0 and U.pre is not None:
                                U.pre()
                            stages[k](U)
                            if k == K - 1 and U.post is not None:
                                U.post()

            unitsB = []
            ldc = 0
            for piece in range(2):
                if piece == 0:
                    klist = [(0, None, True)] + [(2048 + 1024 * s_, 1 + i, False) for i, s_ in enumerate([2, 1, 0])]
                else:
                    klist = [(1024, None, True)] + [(2048 + 1024 * s_, 4 + i, False) for i, s_ in enumerate([6, 5, 4, 3, 2, 1, 0])]
                for ui, (kidx, fcol, diag) in enumerate(klist):
                    li = ldc % 2
                    ldc += 1

                    def pre(piece=piece, ui=ui, li=li, kidx=kidx):
                        if ui == 0:
                            S.dma("sp", semqb, [(qb, QbT[:, :, piece * 1024:(piece + 1) * 1024].rearrange("h p c -> p h c"))],
                                  reads=[B_scr], writes=[Bqb])
                        S.dma("sp", semkB[li], [(ktB[li], KbT[:, :, kidx:kidx + 1024].rearrange("h p c -> p h c"))],
                              reads=[B_scr], writes=[BktB[li]])
                        S.dma("sp", semvB[li], [(vtB[li], VbS[kidx // 128:kidx // 128 + 8].rearrange("t p c -> p t c"))],
                              reads=[B_scr], writes=[BvtB[li]])
                    firstU = True
                    for h in range(NH):
                        for qt_ in range(8):
                            kb0 = qt_ if diag else 0
                            blocks = [(128 * b_, 128, vtB[li][:, b_, h * 128:(h + 1) * 128]) for b_ in range(kb0, 8)]
                            U = mk_unit(128, qb[:, h, qt_ * 128:(qt_ + 1) * 128], Bqb, ktB[li][:, h, :], BktB[li], kb0 * 128, 1024, blocks,
                                        BvtB[li], diag, None if fcol is None else flg[:, fcol:fcol + 1],
                                        carry[:, h * 8 + qt_:h * 8 + qt_ + 1], acc[:, h, qt_ * 128:(qt_ + 1) * 128], ui == 0)
                            if firstU:
                                U.pre = pre
                                firstU = False
                            unitsB.append(U)

                def post(piece=piece):
                    S.op("act", lambda e: e.activation(out=obst, in_=acc, func=AF.Copy), reads=[Bacc], writes=[Bobst])
                    S.dma("sp", semob, [(ObT[:, :, piece * 1024:(piece + 1) * 1024].rearrange("h p c -> p h c"), obst)],
                          reads=[Bobst], writes=[B_ObT])
                unitsB[-1].post = post
            kS0 = abf(NH * NS).rearrange("p (h c) -> p h c", h=NH)
            vS0b = abf(1024)
            BkS0 = Buf("kS0")
            semkS0 = S.newsem()
            semck = S.newsem()
            semcvb = [S.newsem(), S.newsem()]
            obs2 = abf(NH * NS).rearrange("p (h c) -> p h c", h=NH)
            Bobs2 = Buf("obs2")

            def pre_s0():
                S.dma("sp", semqb, [(qb[:, :, 0:NS], QbT[:, :, 2048:2080].rearrange("h p c -> p h c"))], reads=[B_scr], writes=[Bqb])
                S.dma("sp", semkS0, [(kS0, KbT[:, :, 9216:9248].rearrange("h p c -> p h c")), (vS0b[0:NS, :], VbS[72, 0:NS, :])],
                      reads=[B_scr], writes=[BkS0])
            for h in range(NH):
                U = mk_unit(NS, qb[:, h, 0:NS], Bqb, kS0[:, h, :], BkS0, 0, NS, [(0, NS, vS0b[0:NS, h * 128:(h + 1) * 128])], BkS0, True, None,
                            carry[:, h:h + 1], acc[:, h, 0:NS], True)
                if h == 0:
                    U.pre = pre_s0
                unitsB.append(U)
            for half in range(2):
                li = half

                def pre_c(half=half, li=li):
                    S.dma("pool", semck, [(obst, cb_k[half * 1024:(half + 1) * 1024, :].rearrange("(t p) c -> p t c", p=128))],
                          writes=[Bobst])
                    S.dma("pool", semcvb[li], [(vtB[li], cb_v[half * 1024:(half + 1) * 1024, :].rearrange("(t p) c -> p t c", p=128))],
                          writes=[BvtB[li]])
                    for h in range(NH):
                        for g2 in range(2):
                            pi = (h * 2 + g2) % 2
                            psb16 = PS[pi][:, :].bitcast(BF16)
                            for t in range(4):
                                tt = g2 * 4 + t
                                S.op("pe", lambda e, psb16=psb16, h=h, t=t, tt=tt: e.transpose(
                                    psb16[:, t * 128:(t + 1) * 128], obst[:, tt, h * 128:(h + 1) * 128], identb[:, :]),
                                    reads=[Bobst, B_const], writes=[PSB[pi]])
                            S.op("dve", lambda e, psb16=psb16, h=h, g2=g2, li=li: e.tensor_copy(
                                out=ktB[li][:, h, g2 * 512:(g2 + 1) * 512], in_=psb16[:, 0:512]), reads=[PSB[pi]], writes=[BktB[li]])
                for h in range(NH):
                    blocks = [(128 * b_, 128, vtB[li][:, b_, h * 128:(h + 1) * 128]) for b_ in range(8)]
                    U = mk_unit(NS, qb[:, h, 0:NS], Bqb, ktB[li][:, h, :], BktB[li], 0, 1024, blocks, BvtB[li], False, None,
                                carry[:, h:h + 1], acc[:, h, 0:NS], False)
                    if h == 0:
                        U.pre = pre_c
                    unitsB.append(U)

            def post_s():
                S.op("act", lambda e: e.activation(out=obs2, in_=acc[:, :, 0:NS], func=AF.Copy), reads=[Bacc], writes=[Bobs2])
                S.dma("sp", semob, [(ObT[:, :, 2048:2080].rearrange("h p c -> p h c"), obs2)], reads=[Bobs2], writes=[B_ObT])
            unitsB[-1].post = post_s
            pipeline(unitsB, [st_z, st_sig2, st_scan, st_sub, st_tr, st_cp, st_av, st_acc])
            S.barrier()
            areset()
            _ph[0] += 1
            if _ph[0] >= STOP:
                break

            wa = abf(NH * D).rearrange("p (k c) -> p k c", k=NH)
            wbm = abf(NH * D).rearrange("p (k c) -> p k c", k=NH)
            Bwab = Buf("wab")
            semwab = S.newsem()
            S.dma("pool", semwab, [(wa[:, :, 0:1024], wview(w_a_out, 0, 1024)), (wa[:, :, 1024:2048], wview(w_a_out, 1024, 1024)),
                                   (wbm[:, :, 0:1024], wview(w_b_out, 0, 1024)), (wbm[:, :, 1024:2048], wview(w_b_out, 1024, 1024))],
                  writes=[Bwab])
            oat = [abf(NH * 512).rearrange("p (h c) -> p h c", h=NH) for _ in range(2)]
            obt = [abf(NH * 512).rearrange("p (h c) -> p h c", h=NH) for _ in range(2)]
            Boab = [Buf("oab0"), Buf("oab1")]
            semoab = [S.newsem(), S.newsem()]
            ga8 = af32(8 * 512).rearrange("p (h c) -> p h c", h=8)
            gb8 = af32(8 * 512).rearrange("p (h c) -> p h c", h=8)
            Bg8 = Buf("g8")
            semg8 = S.newsem()
            t1 = [af32(512) for _ in range(2)]
            t2 = [af32(512) for _ in range(2)]
            Bt12 = [Buf("t12_0"), Buf("t12_1")]
            mst = abf(KC * 512).rearrange("p (k c) -> p k c", k=KC)
            Bmst = Buf("mst")
            semmst = S.newsem()
            B_MT = [Buf("MT%d" % i) for i in range(5)]
            cc5 = 0
            for bi5, (blk, ntk, r0) in enumerate(own_blocks):
                li = bi5 % 2
                S.dma("sp", semoab[li], [(oat[li][:, :, 0:ntk], OaT[:, :, r0:r0 + ntk].rearrange("h p c -> p h c")),
                                         (obt[li][:, :, 0:ntk], ObT[:, :, r0:r0 + ntk].rearrange("h p c -> p h c"))],
                      reads=[B_OaT, B_ObT], writes=[Boab[li]])
                for half in range(2):
                    S.dma("sp", semg8, [(ga8[:, :, 0:ntk], GA[8 * half:8 * half + 8, :, r0:r0 + ntk].rearrange("h p c -> p h c")),
                                        (gb8[:, :, 0:ntk], GB[8 * half:8 * half + 8, :, r0:r0 + ntk].rearrange("h p c -> p h c"))],
                          reads=[B_scr], writes=[Bg8])
                    for cc in range(8):
                        c = 8 * half + cc
                        u = cc5 % 2
                        cc5 += 1
                        pa, pb_ = 2 * u, 2 * u + 1
                        for kc in range(NH):
                            S.op("pe", lambda e, pa=pa, kc=kc, c=c, li=li, ntk=ntk: e.matmul(
                                PS[pa][:, 0:ntk], lhsT=wa[:, kc, c * 128:(c + 1) * 128], rhs=oat[li][:, kc, 0:ntk],
                                start=(kc == 0), stop=(kc == NH - 1)), reads=[Bwab, Boab[li]], writes=[PSB[pa]])
                        for kc in range(NH):
                            S.op("pe", lambda e, pb_=pb_, kc=kc, c=c, li=li, ntk=ntk: e.matmul(
                                PS[pb_][:, 0:ntk], lhsT=wbm[:, kc, c * 128:(c + 1) * 128], rhs=obt[li][:, kc, 0:ntk],
                                start=(kc == 0), stop=(kc == NH - 1)), reads=[Bwab, Boab[li]], writes=[PSB[pb_]])
                        S.op("dve", lambda e, pa=pa, u=u, cc=cc, ntk=ntk: e.tensor_tensor(
                            out=t1[u][:, 0:ntk], in0=PS[pa][:, 0:ntk], in1=ga8[:, cc, 0:ntk], op=ALU.mult),
                            reads=[PSB[pa], Bg8], writes=[Bt12[u]])
                        S.op("dve", lambda e, pb_=pb_, u=u, cc=cc, ntk=ntk: e.tensor_tensor(
                            out=t2[u][:, 0:ntk], in0=PS[pb_][:, 0:ntk], in1=gb8[:, cc, 0:ntk], op=ALU.mult),
                            reads=[PSB[pb_], Bg8], writes=[Bt12[u]])
                        S.op("pool", lambda e, u=u, c=c, ntk=ntk: e.tensor_tensor(
                            out=mst[:, c, 0:ntk], in0=t1[u][:, 0:ntk], in1=t2[u][:, 0:ntk], op=ALU.add),
                            reads=[Bt12[u]], writes=[Bmst])
                S.dma("sp", semmst, [(MT[bi5], mst)], reads=[Bmst], writes=[B_MT[bi5]])
            S.barrier()
            areset()
            _ph[0] += 1
            if _ph[0] >= STOP:
                break

            wos = [abf(KC * 512).rearrange("p (k c) -> p k c", k=KC) for _ in range(2)]
            Bwos = [Buf("wos0"), Buf("wos1")]
            semwos = [S.newsem(), S.newsem()]
            mtb = [abf(KC * 512).rearrange("p (k c) -> p k c", k=KC) for _ in range(2)]
            Bmtb = [Buf("mtb0"), Buf("mtb1")]
            semmtb = [S.newsem(), S.newsem()]
            xc5 = [af32(512) for _ in range(2)]
            Bxc5 = [Buf("xc5_0"), Buf("xc5_1")]
            semxc5 = [S.newsem(), S.newsem()]
            semx1 = [S.newsem(), S.newsem()]
            t5 = [af32(512) for _ in range(2)]
            Bt5 = [Buf("t5_0"), Buf("t5_1")]
            gtb = [af32(D) for _ in range(2)]
            Bgtb = Buf("gtb")
            semgtb = S.newsem()
            S.dma("sp", semgtb, [(gtb[v], bass.AP(adaD.tensor, v * 6 * D + 2 * D, [[0, 128], [1, D]])) for v in range(2)],
                  reads=[B_adaD], writes=[Bgtb])
            B_X1 = [Buf("X1_%d" % i) for i in range(17)]

            def load_w5(s_):
                S.dma("pool", semwos[s_ % 2], [(wos[s_ % 2], wview(w_o, s_ * 512, 512))], writes=[Bwos[s_ % 2]])

            def load_m5(n_):
                S.dma("sp", semmtb[n_ % 2], [(mtb[n_ % 2], MT[n_ % 5])], reads=[B_MT[n_ % 5]], writes=[Bmtb[n_ % 2]])
            load_w5(0)
            load_m5(0)
            l5 = 0
            c5 = 0
            for s5 in range(4):
                wi = s5 % 2
                if s5 + 1 < 4:
                    load_w5(s5 + 1)
                tix = 0
                for bi5, (blk, ntk, r0) in enumerate(own_blocks):
                    li = l5 % 2
                    if l5 + 1 < 20:
                        load_m5(l5 + 1)
                    l5 += 1
                    v = 0 if blk < NBLK else 1
                    nt = 4 if ntk == 512 else 1
                    for j in range(nt):
                        ntok = 128 if ntk == 512 else NS
                        u = c5 % 2
                        pi = c5 % 4
                        c5 += 1
                        src = xin[blk, j * 128:(j + 1) * 128, s5 * 512:(s5 + 1) * 512] if blk < NBLK else xs_in[:, s5 * 512:(s5 + 1) * 512]
                        S.dma("sp", semxc5[u], [(xc5[u][0:ntok, :], src)], writes=[Bxc5[u]])
                        for kc in range(KC):
                            S.op("pe", lambda e, pi=pi, kc=kc, li=li, j=j, ntok=ntok, wi=wi: e.matmul(
                                PS[pi][0:ntok, :], lhsT=mtb[li][:, kc, j * 128:j * 128 + ntok], rhs=wos[wi][:, kc, :],
                                start=(kc == 0), stop=(kc == KC - 1)), reads=[Bwos[wi], Bmtb[li]], writes=[PSB[pi]])
                        S.op("dve", lambda e, pi=pi, u=u, v=v, ntok=ntok, s5=s5: e.tensor_tensor(
                            out=t5[u][0:ntok, :], in0=PS[pi][0:ntok, :], in1=gtb[v][0:ntok, s5 * 512:(s5 + 1) * 512], op=ALU.mult),
                            reads=[PSB[pi], Bgtb], writes=[Bt5[u]])
                        S.op("pool", lambda e, u=u, ntok=ntok: e.tensor_tensor(
                            out=xc5[u][0:ntok, :], in0=xc5[u][0:ntok, :], in1=t5[u][0:ntok, :], op=ALU.add),
                            reads=[Bt5[u], Bxc5[u]], writes=[Bxc5[u]])
                        S.dma("sp", semx1[u], [(X1[tix, 0:ntok, s5 * 512:(s5 + 1) * 512], xc5[u][0:ntok, :])],
                              reads=[Bxc5[u]], writes=[B_X1[tix]])
                        tix += 1
            S.barrier()
            areset()
            B_H2T = [Buf("H2T%d" % i) for i in range(5)]
            tiles5 = []
            tix = 0
            for bi5, (blk, ntk, r0) in enumerate(own_blocks):
                nt = 4 if ntk == 512 else 1
                for j in range(nt):
                    ntok = 128 if ntk == 512 else NS
                    tiles5.append((X1[tix, 0:ntok, :], [B_X1[tix]], ntok, 0 if blk < NBLK else 1, bi5, j, j == nt - 1))
                    tix += 1

            def store5(bi5, st, bst, sem):
                S.dma("sp", sem, [(H2T[bi5], st)], reads=[bst], writes=[B_H2T[bi5]])
            norm_pass(tiles5, 4, 3, 1, store5, None)
            S.barrier()
            areset()
            _ph[0] += 1
            if _ph[0] >= STOP:
                break

            wgu = [abf(KC * 1024).rearrange("p (k c) -> p k c", k=KC) for _ in range(2)]
            Bwgu = [Buf("wgu0"), Buf("wgu1")]
            semwgu = [S.newsem(), S.newsem()]
            h2b = [abf(KC * 512).rearrange("p (k c) -> p k c", k=KC) for _ in range(2)]
            Bh2b = [Buf("h2b0"), Buf("h2b1")]
            semh2b = [S.newsem(), S.newsem()]
            sg = [af32(512) for _ in range(2)]
            Bsg = [Buf("sg0"), Buf("sg1")]
            ast = [abf(4 * 512).rearrange("p (k c) -> p k c", k=4) for _ in range(2)]
            Bast = [Buf("ast0"), Buf("ast1")]
            semast = [S.newsem(), S.newsem()]
            B_ActT = Buf("ActT")
            c6 = 0
            l6 = 0
            a6 = 0
            def load_w6(g_):
                S.dma("pool", semwgu[g_ % 2], [(wgu[g_ % 2][:, :, 0:512], wview(w_gu, g_ * 512, 512)),
                                               (wgu[g_ % 2][:, :, 512:1024], wview(w_gu, DFF + g_ * 512, 512))], writes=[Bwgu[g_ % 2]])

            def load_h6(n_):
                S.dma("sp", semh2b[n_ % 2], [(h2b[n_ % 2], H2T[n_ % 5])], reads=[B_H2T[n_ % 5]], writes=[Bh2b[n_ % 2]])
            load_w6(0)
            load_h6(0)
            for g in range(11):
                wi = g % 2
                if g + 1 < 11:
                    load_w6(g + 1)
                for bi5, (blk, ntk, r0) in enumerate(own_blocks):
                    li = l6 % 2
                    if l6 + 1 < 55:
                        load_h6(l6 + 1)
                    l6 += 1
                    ai = a6 % 2
                    a6 += 1
                    for cc in range(4):
                        u = c6 % 2
                        c6 += 1
                        pg, pu = 2 * u, 2 * u + 1
                        for kc in range(KC):
                            S.op("pe", lambda e, pg=pg, kc=kc, cc=cc, wi=wi, li=li, ntk=ntk: e.matmul(
                                PS[pg][:, 0:ntk], lhsT=wgu[wi][:, kc, cc * 128:(cc + 1) * 128], rhs=h2b[li][:, kc, 0:ntk],
                                start=(kc == 0), stop=(kc == KC - 1)), reads=[Bwgu[wi], Bh2b[li]], writes=[PSB[pg]])
                        for kc in range(KC):
                            S.op("pe", lambda e, pu=pu, kc=kc, cc=cc, wi=wi, li=li, ntk=ntk: e.matmul(
                                PS[pu][:, 0:ntk], lhsT=wgu[wi][:, kc, 512 + cc * 128:512 + (cc + 1) * 128], rhs=h2b[li][:, kc, 0:ntk],
                                start=(kc == 0), stop=(kc == KC - 1)), reads=[Bwgu[wi], Bh2b[li]], writes=[PSB[pu]])
                        S.op("act", lambda e, pg=pg, u=u, ntk=ntk: e.activation(out=sg[u][:, 0:ntk], in_=PS[pg][:, 0:ntk], func=AF.Silu),
                             reads=[PSB[pg]], writes=[Bsg[u]])
                        S.op("dve", lambda e, pu=pu, u=u, ai=ai, cc=cc, ntk=ntk: e.tensor_tensor(
                            out=ast[ai][:, cc, 0:ntk], in0=PS[pu][:, 0:ntk], in1=sg[u][:, 0:ntk], op=ALU.mult),
                            reads=[PSB[pu], Bsg[u]], writes=[Bast[ai]])
                    S.dma("sp", semast[ai], [(ActT[2 * bi5 + hf_, :, 4 * g:4 * g + 4, 0:min(256, ntk - 256 * hf_)],
                                              ast[ai][:, :, 256 * hf_:256 * hf_ + min(256, ntk - 256 * hf_)])
                                             for hf_ in range(2) if ntk > 256 * hf_],
                          reads=[Bast[ai]], writes=[B_ActT])
            S.barrier()
            areset()
            _ph[0] += 1
            if _ph[0] >= STOP:
                break

            NSL = 4
            SW = D // NSL
            wd = [abf(FC * SW).rearrange("p (k c) -> p k c", k=FC) for _ in range(2)]
            Bwd = [Buf("wd0"), Buf("wd1")]
            semwd = [S.newsem(), S.newsem()]
            actb = [abf(FC * 256).rearrange("p (k c) -> p k c", k=FC) for _ in range(2)]
            Bactb = [Buf("actb0"), Buf("actb1")]
            semactb = [S.newsem(), S.newsem()]
            x1c = [af32(SW) for _ in range(2)]
            Bx1c = [Buf("x1c0"), Buf("x1c1")]
            semx1c = [S.newsem(), S.newsem()]
            semx2 = [S.newsem(), S.newsem()]
            t7 = [af32(SW) for _ in range(2)]
            Bt7 = [Buf("t7_0"), Buf("t7_1")]
            gt2b = [af32(D) for _ in range(2)]
            Bgt2 = Buf("gt2")
            semgt2 = S.newsem()
            S.dma("sp", semgt2, [(gt2b[v], bass.AP(adaD.tensor, v * 6 * D + 5 * D, [[0, 128], [1, D]])) for v in range(2)],
                  reads=[B_adaD], writes=[Bgt2])
            B_X2 = [Buf("X2_%d" % i) for i in range(17)]
            hbl = []
            for bi5 in range(4):
                for half in range(2):
                    hbl.append((2 * bi5 + half, 256, bi5 * 4 + half * 2, 2, 128, 0))
            hbl.append((8, NS, 16, 1, NS, 1))
            NHB = len(hbl)

            def load_w7(s_):
                S.dma("pool", semwd[s_ % 2], [(wd[s_ % 2][:, 0:22, :], wview(w_dn, s_ * SW, SW)[:, 0:22, :]),
                                              (wd[s_ % 2][:, 22:44, :], wview(w_dn, s_ * SW, SW)[:, 22:44, :])], writes=[Bwd[s_ % 2]])

            def load_a7(n_):
                ai_, w_, _, _, _, _ = hbl[n_ % NHB]
                S.dma("sp", semactb[n_ % 2], [(actb[n_ % 2][:, :, 0:w_], ActT[ai_, :, :, 0:w_])], reads=[B_ActT], writes=[Bactb[n_ % 2]])
            load_w7(0)
            load_a7(0)
            l7 = 0
            c7 = 0
            for s7 in range(NSL):
                wi = s7 % 2
                if s7 + 1 < NSL:
                    load_w7(s7 + 1)
                for (ai_, w_, tix0, ntl, ntok, v) in hbl:
                    li = l7 % 2
                    if l7 + 1 < NSL * NHB:
                        load_a7(l7 + 1)
                    l7 += 1
                    for j in range(ntl):
                        tix = tix0 + j
                        u = c7 % 2
                        pi = c7 % 4
                        c7 += 1
                        S.dma("sp", semx1c[u], [(x1c[u][0:ntok, :], X1[tix, 0:ntok, s7 * SW:(s7 + 1) * SW])],
                              reads=[B_X1[tix]], writes=[Bx1c[u]])
                        for kc in range(FC):
                            S.op("pe", lambda e, pi=pi, kc=kc, li=li, j=j, ntok=ntok, wi=wi: e.matmul(
                                PS[pi][0:ntok, 0:SW], lhsT=actb[li][:, kc, j * 128:j * 128 + ntok], rhs=wd[wi][:, kc, :],
                                start=(kc == 0), stop=(kc == FC - 1)), reads=[Bwd[wi], Bactb[li]], writes=[PSB[pi]])
                        S.op("dve", lambda e, pi=pi, u=u, v=v, ntok=ntok, s7=s7: e.tensor_tensor(
                            out=t7[u][0:ntok, :], in0=PS[pi][0:ntok, 0:SW], in1=gt2b[v][0:ntok, s7 * SW:(s7 + 1) * SW], op=ALU.mult),
                            reads=[PSB[pi], Bgt2], writes=[Bt7[u]])
                        S.op("pool", lambda e, u=u, ntok=ntok: e.tensor_tensor(
                            out=x1c[u][0:ntok, :], in0=x1c[u][0:ntok, :], in1=t7[u][0:ntok, :], op=ALU.add),
                            reads=[Bt7[u], Bx1c[u]], writes=[Bx1c[u]])
                        S.dma("sp", semx2[u], [(X2[tix, 0:ntok, s7 * SW:(s7 + 1) * SW], x1c[u][0:ntok, :])],
                              reads=[Bx1c[u]], writes=[B_X2[tix]])
            S.barrier()
            areset()
            _ph[0] += 1
            if _ph[0] >= STOP:
                break

            gfb = af32(D)
            Bgfb = Buf("gfb")
            semgfb = S.newsem()
            S.dma("sp", semgfb, [(gfb, bass.AP(gfin.tensor, 0, [[0, 128], [1, D]]))], writes=[Bgfb])
            x8 = [af32(D) for _ in range(2)]
            Bx8 = [Buf("x8_0"), Buf("x8_1")]
            semx8 = [S.newsem(), S.newsem()]
            y8 = [af32(D) for _ in range(2)]
            By8 = [Buf("y8_0"), Buf("y8_1")]
            semy8 = [S.newsem(), S.newsem()]
            Bsm8 = [Buf("sm8_0"), Buf("sm8_1")]
            for tix in range(17):
                u = tix % 2
                ntok = 128 if tix < 16 else NS
                S.dma("sp", semx8[u], [(x8[u][0:ntok, :], X2[tix, 0:ntok, :])], reads=[B_X2[tix]], writes=[Bx8[u]])
                ssq = small[:, 48 + 2 * u:49 + 2 * u]
                rstd = small[:, 49 + 2 * u:50 + 2 * u]
                S.op("act", lambda e, u=u, ntok=ntok, ssq=ssq: e.activation(out=y8[u][0:ntok, :], in_=x8[u][0:ntok, :], func=AF.Square,
                                                                            accum_out=ssq[0:ntok, :]), reads=[Bx8[u]], writes=[By8[u], Bsm8[u]])
                S.op("act", lambda e, ntok=ntok, ssq=ssq, rstd=rstd: e.activation(out=rstd[0:ntok, :], in_=ssq[0:ntok, :], func=AF.Sqrt,
                                                                                  scale=1.0 / D, bias=small[0:ntok, 63:64]),
                     reads=[Bsm8[u], B_const], writes=[Bsm8[u]])
                S.op("dve", lambda e, ntok=ntok, rstd=rstd: e.reciprocal(out=rstd[0:ntok, :], in_=rstd[0:ntok, :]),
                     reads=[Bsm8[u]], writes=[Bsm8[u]])
                S.op("act", lambda e, u=u, ntok=ntok, rstd=rstd: e.activation(out=y8[u][0:ntok, :], in_=x8[u][0:ntok, :], func=AF.Identity,
                                                                              scale=rstd[0:ntok, :]), reads=[Bx8[u], Bsm8[u]], writes=[By8[u]])
                S.op("dve", lambda e, u=u, ntok=ntok: e.tensor_tensor(out=y8[u][0:ntok, :], in0=y8[u][0:ntok, :], in1=gfb[0:ntok, :],
                                                                      op=ALU.mult), reads=[By8[u], Bgfb], writes=[By8[u]])
                S.dma("sp", semy8[u], [(y_o[tix * 128:tix * 128 + ntok, :], y8[u][0:ntok, :])], reads=[By8[u]], writes=[])
            S.barrier()

        with nc.Block() as block:
            @block.tensor
            def _(e):
                S.replay("pe", e)

            @block.scalar
            def _(e):
                S.replay("act", e)

            @block.vector
            def _(e):
                S.replay("dve", e)

            @block.gpsimd
            def _(e):
                S.replay("pool", e)

            @block.sync
            def _(e):
                S.replay("sp", e)
    return nc


_NC_CACHE = {}


def _prep_inputs(x_prompt, x_sample, c_prompt, c_sample, cache_a_k, cache_a_v, cache_b_k, cache_b_v,
                 w_ada, b_ada, g_mix, w_in, rel_bias, w_a_out, w_b_out, w_o, g_ffn, w_gate_up, w_down, g_final):
    f = lambda a: np.ascontiguousarray(np.asarray(a, dtype=np.float32))
    shared = {
        "w_ada": f(w_ada[0]), "b_ada": f(b_ada[0]).reshape(1, -1), "w_in": f(w_in[0]), "relb": f(rel_bias[0]),
        "w_a_out": f(w_a_out[0]), "w_b_out": f(w_b_out[0]), "w_o": f(w_o[0]), "w_gu": f(w_gate_up[0]),
        "w_dn": f(w_down[0]), "gfin": f(g_final).reshape(1, -1),
    }
    gcols = np.stack([f(g_mix[0]).reshape(KC, 128).T, f(g_ffn[0]).reshape(KC, 128).T], axis=1)
    shared["gcols"] = np.ascontiguousarray(gcols)
    shared["grow"] = np.ascontiguousarray(np.stack([f(g_mix[0]), f(g_ffn[0])], axis=0))
    xp = np.asarray(x_prompt, dtype=np.float32)
    xs = np.asarray(x_sample, dtype=np.float32)
    maps = []
    for c in range(8):
        b, r = c // 4, c % 4
        xr = xp[b, ::-1]
        def piece(p):
            return xr[1024 * (7 - p):1024 * (8 - p)]
        def halo(p):
            if p == 0:
                return xr[0:512]
            return xr[1024 * (8 - p):1024 * (8 - p) + 512]
        lo, hi = r, 7 - r
        blocks = [piece(lo)[:512], piece(lo)[512:], piece(hi)[:512], piece(hi)[512:], halo(lo), halo(hi)]
        for s in range(7):
            p = s if s <= 6 - r else 0
            blocks += [piece(p)[:512], piece(p)[512:]]
        xin = np.ascontiguousarray(np.stack(blocks, axis=0))
        cv = np.stack([np.asarray(c_prompt[b], np.float32).reshape(KC, 128).T,
                       np.asarray(c_sample[c], np.float32).reshape(KC, 128).T], axis=2)
        flags = np.zeros((128, 16), np.float32)
        flags[:, 0] = -BIG if r == 0 else 0.0
        for i, s in enumerate([2, 1, 0]):
            flags[:, 1 + i] = 0.0 if s <= r - 1 else BIG
        for i, s in enumerate([6, 5, 4, 3, 2, 1, 0]):
            flags[:, 4 + i] = 0.0 if s <= 6 - r else BIG
        m = dict(shared)
        m.update({
            "xin": xin, "xs": np.ascontiguousarray(xs[c, ::-1]), "cvec": np.ascontiguousarray(cv), "flags": flags,
            "ca_k": np.ascontiguousarray(np.asarray(cache_a_k[0, c], np.float32)[::-1].reshape(512, 1024)),
            "ca_v": np.ascontiguousarray(np.asarray(cache_a_v[0, c], np.float32)[::-1].reshape(512, 1024)),
            "cb_k": np.ascontiguousarray(np.asarray(cache_b_k[0, c], np.float32)[::-1].reshape(2048, 1024)),
            "cb_v": np.ascontiguousarray(np.asarray(cache_b_v[0, c], np.float32)[::-1].reshape(2048, 1024)),
        })
        maps.append(m)
    return maps


def kernel(**inputs):
    maps = _prep_inputs(**inputs)
    if "nc" not in _NC_CACHE:
        _NC_CACHE["nc"] = build_program()
    nc = _NC_CACHE["nc"]
    res = run_bass_kernel_spmd(nc, maps, core_ids=list(range(8)))
    R = res.results
    y_p = np.zeros((2, 8192, D), np.float32)
    y_s = np.zeros((8, NS, D), np.float32)
    ak_p = np.zeros((1, 2, 512, NH, HD), np.float32)
    av_p = np.zeros((1, 2, 512, NH, HD), np.float32)
    bk_p = np.zeros((1, 2, 8192, NH, HD), np.float32)
    bv_p = np.zeros((1, 2, 8192, NH, HD), np.float32)
    ak_s = np.zeros((1, 8, NS, NH, HD), np.float32)
    av_s = np.zeros((1, 8, NS, NH, HD), np.float32)
    bk_s = np.zeros((1, 8, NS, NH, HD), np.float32)
    bv_s = np.zeros((1, 8, NS, NH, HD), np.float32)
    for c in range(8):
        b, r = c // 4, c % 4
        o = R[c]
        for pi, p in enumerate((r, 7 - r)):
            sl = slice(1024 * p, 1024 * (p + 1))
            rows = slice(1024 * pi, 1024 * (pi + 1))
            y_p[b, sl] = o["y"][rows][::-1]
            bk_p[0, b, sl] = o["kb_o"][rows][::-1].reshape(1024, NH, HD)
            bv_p[0, b, sl] = o["vb_o"][rows][::-1].reshape(1024, NH, HD)
        if r == 0:
            ak_p[0, b] = o["ka_o"][0:512][::-1].reshape(512, NH, HD)
            av_p[0, b] = o["va_o"][0:512][::-1].reshape(512, NH, HD)
        y_s[c] = o["y"][2048:2080][::-1]
        ak_s[0, c] = o["ka_o"][512:544][::-1].reshape(NS, NH, HD)
        av_s[0, c] = o["va_o"][512:544][::-1].reshape(NS, NH, HD)
        bk_s[0, c] = o["kb_o"][2048:2080][::-1].reshape(NS, NH, HD)
        bv_s[0, c] = o["vb_o"][2048:2080][::-1].reshape(NS, NH, HD)
    return (y_p, y_s, ak_p, av_p, bk_p, bv_p, ak_s, av_s, bk_s, bv_s)
```

```python
import numpy as np
from contextlib import ExitStack
import concourse.bass as bass
import concourse.mybir as mybir
from concourse.bass_utils import run_bass_kernel_spmd

F32 = mybir.dt.float32
BF16 = mybir.dt.bfloat16
AF = mybir.ActivationFunctionType
ALU = mybir.AluOpType
AX = mybir.AxisListType

D = 2048
KC = 16
NH = 8
HD = 128
DFF = 5632
FC = 44
NOWN = 2080
NS = 32
SCALE = HD ** -0.5
EPS = 1e-6
BIG = 30000.0
NBLK = 20
NAKEY = 3104
NBKEY = 9248
LT = 767


class Sem:
    def __init__(self, h):
        self.h = h
        self.count = 0


class Buf:
    __slots__ = ("name", "w", "r", "excl")

    def __init__(self, name="", excl=False):
        self.name = name
        self.w = None
        self.r = {}
        self.excl = excl


class Sched:
    ENG = ("pe", "act", "dve", "pool", "sp")

    def __init__(self, nc, es):
        self.nc = nc
        self.es = es
        self.q = {e: [] for e in self.ENG}
        self.esem = {e: Sem(es.enter_context(nc.semaphore("s_" + e))) for e in self.ENG}
        self.seen = {e: {} for e in self.ENG}
        self.dsems = []
        self.nsem = 0

    def newsem(self):
        self.nsem += 1
        s = Sem(self.es.enter_context(self.nc.semaphore("d%d" % self.nsem)))
        self.dsems.append(s)
        return s

    def _waits(self, eng, reads, writes):
        need = {}

        def add(s, n):
            if need.get(s, 0) < n:
                need[s] = n
        for b in reads:
            if b.w is not None:
                add(*b.w)
        for b in writes:
            if b.w is not None:
                add(*b.w)
            for s, n in b.r.items():
                add(s, n)
        out = []
        seen = self.seen[eng]
        for s, n in need.items():
            if eng == "pe" and s is self.esem["pe"]:
                continue
            if seen.get(s, 0) >= n:
                continue
            seen[s] = n
            out.append((s, n))
        return out

    def _commit(self, t, reads, writes):
        s, n = t
        for b in reads:
            if b.r.get(s, 0) < n:
                b.r[s] = n
        for b in writes:
            b.w = t
            b.r = {}

    def op(self, eng, fn, reads=(), writes=()):
        if any(b.excl for b in reads):
            writes = list(writes) + [b for b in reads if b.excl and b not in writes]
            reads = [b for b in reads if not b.excl]
        waits = self._waits(eng, reads, writes)
        s = self.esem[eng]
        s.count += 1
        t = (s, s.count)
        self.q[eng].append((waits, fn, s, 1))
        self._commit(t, reads, writes)
        return t

    def dma(self, eng, sem, pairs, reads=(), writes=(), transpose=False):
        waits = self._waits(eng, reads, writes)
        if sem.count > 0 and self.seen[eng].get(sem, 0) < sem.count:
            self.seen[eng][sem] = sem.count
            waits = waits + [(sem, sem.count)]
        sem.count += 16 * len(pairs)
        t = (sem, sem.count)
        for i, (o, i_) in enumerate(pairs):
            if transpose:
                self.q[eng].append((waits if i == 0 else [], (lambda e, o=o, i_=i_: e.dma_start_transpose(out=o, in_=i_)), sem, 16))
            else:
                self.q[eng].append((waits if i == 0 else [], (lambda e, o=o, i_=i_: e.dma_start(out=o, in_=i_)), sem, 16))
        self._commit(t, reads, writes)
        return t

    def barrier(self):
        allw = [(s, s.count) for s in list(self.esem.values()) + self.dsems if s.count > 0]
        for e in self.ENG:
            seen = self.seen[e]
            w = []
            for s, n in allw:
                if seen.get(s, 0) < n:
                    seen[s] = n
                    w.append((s, n))
            if w:
                self.q[e].append((w, None, None, 0))

    def replay(self, name, e):
        for waits, fn, s, inc in self.q[name]:
            for ws, n in waits:
                e.wait_ge(ws.h, n)
            if fn is not None:
                fn(e).then_inc(s.h, inc)


def build_program(STOP=99):
    nc = bass.Bass("TRN2", target_bir_lowering=False)

    def din(name, shape, dt=F32):
        return nc.dram_tensor(name, list(shape), dt, kind="ExternalInput").ap()

    def dout(name, shape):
        return nc.dram_tensor(name, list(shape), F32, kind="ExternalOutput").ap()

    def dscr(name, shape, dt):
        return nc.dram_tensor(name, list(shape), dt, kind="Internal").ap()

    xin = din("xin", [NBLK, 512, D])
    xs_in = din("xs", [NS, D])
    cvec = din("cvec", [128, KC, 2])
    gcols = din("gcols", [128, 2, KC])
    grow = din("grow", [2, D])
    gfin = din("gfin", [1, D])
    flags_in = din("flags", [128, 16])
    w_ada = din("w_ada", [D, 6 * D])
    b_ada = din("b_ada", [1, 6 * D])
    w_in = din("w_in", [D, 10240])
    relb = din("relb", [NH, 257])
    w_a_out = din("w_a_out", [1024, D])
    w_b_out = din("w_b_out", [1024, D])
    w_o = din("w_o", [D, D])
    w_gu = din("w_gu", [D, 2 * DFF])
    w_dn = din("w_dn", [DFF, D])
    ca_k = din("ca_k", [512, 1024])
    ca_v = din("ca_v", [512, 1024])
    cb_k = din("cb_k", [2048, 1024])
    cb_v = din("cb_v", [2048, 1024])

    y_o = dout("y", [NOWN, D])
    ka_o = dout("ka_o", [544, 1024])
    va_o = dout("va_o", [544, 1024])
    kb_o = dout("kb_o", [NOWN, 1024])
    vb_o = dout("vb_o", [NOWN, 1024])

    HT = dscr("HT", [NBLK + 1, 128, KC, 512], BF16)
    adaD = dscr("adaD", [2, 6 * D], F32)
    GT = dscr("GT", [NH, 128, LT], F32)
    gD = dscr("gD", [NH, LT], F32)
    QaT = dscr("QaT", [NH, 128, NOWN], BF16)
    QbT = dscr("QbT", [NH, 128, NOWN], BF16)
    KaT = dscr("KaT", [NH, 128, NAKEY], BF16)
    KbT = dscr("KbT", [NH, 128, NBKEY], BF16)
    VaS = dscr("VaS", [25, 128, 1024], BF16)
    VbS = dscr("VbS", [73, 128, 1024], BF16)
    GA = dscr("GA", [KC, 128, NOWN], F32)
    GB = dscr("GB", [KC, 128, NOWN], F32)
    OaT = dscr("OaT", [NH, 128, NOWN], BF16)
    ObT = dscr("ObT", [NH, 128, NOWN], BF16)
    MT = dscr("MT", [5, 128, KC, 512], BF16)
    X1 = dscr("X1", [17, 128, D], F32)
    H2T = dscr("H2T", [5, 128, KC, 512], BF16)
    ActT = dscr("ActT", [9, 128, FC, 256], BF16)
    X2 = dscr("X2", [17, 128, D], F32)

    es = ExitStack()
    with es:
        S = Sched(nc, es)
        ARW = 45056
        arena = es.enter_context(nc.sbuf_tensor("arena", [128, ARW], F32))
        identf = es.enter_context(nc.sbuf_tensor("identf", [128, 128], F32))
        identb = es.enter_context(nc.sbuf_tensor("identb", [128, 128], BF16))
        maskL = es.enter_context(nc.sbuf_tensor("maskL", [128, 128], F32))
        Tb = es.enter_context(nc.sbuf_tensor("Tb", [128, NH, 640], F32))
        modc = es.enter_context(nc.sbuf_tensor("modc", [128, 2, 4, KC], F32))
        flg = es.enter_context(nc.sbuf_tensor("flg", [128, 16], F32))
        small = es.enter_context(nc.sbuf_tensor("small", [128, 64], F32))
        PS = [es.enter_context(nc.psum_tensor("ps%d" % i, [128, 512], F32)) for i in range(8)]
        PSB = [Buf("ps%d" % i, excl=True) for i in range(8)]
        B_const = Buf("const")

        apos = [0]

        def areset():
            apos[0] = 0

        def af32(n):
            o = apos[0]
            apos[0] += n
            assert apos[0] <= ARW, apos[0]
            return arena[:, o:o + n]

        def abf(n):
            assert n % 2 == 0
            return af32(n // 2).bitcast(BF16)

        def wview(w, c0, nc_, kc=KC):
            return w.rearrange("(kc p) c -> p kc c", p=128)[:, :, c0:c0 + nc_]

        _ph = [0]
        for _once in (0,):
            sem_c = S.newsem()
            S.op("pool", lambda e: e.memset(identf[:], 0.0), writes=[B_const])
            S.op("pool", lambda e: e.affine_select(out=identf[:], in_=identf[:], pattern=[[-1, 128]],
                                                   compare_op=ALU.not_equal, fill=1.0, base=0,
                                                   channel_multiplier=1), writes=[B_const])
            S.op("pool", lambda e: e.tensor_copy(out=identb[:], in_=identf[:]), reads=[B_const], writes=[B_const])
            S.op("pool", lambda e: e.memset(maskL[:], 0.0), writes=[B_const])
            S.op("pool", lambda e: e.affine_select(out=maskL[:], in_=maskL[:], pattern=[[1, 128]], compare_op=ALU.is_gt,
                                                   fill=1.0, base=0, channel_multiplier=-1), writes=[B_const])
            S.op("pool", lambda e: e.memset(small[:, 63:64], EPS), writes=[B_const])
            S.dma("sp", sem_c, [(flg[:], flags_in[:, :])], writes=[B_const])

            B_g = Buf("g")
            B_GT = Buf("GT")
            gs = af32(LT)
            S.dma("sp", sem_c, [(gs[0:NH, 0:256], relb[:, 1:257])], writes=[B_g])
            S.op("dve", lambda e: e.tensor_copy(out=gs[0:NH, 256:LT], in_=gs[0:NH, 255:256].to_broadcast([NH, LT - 256])),
                 reads=[B_g], writes=[B_g])
            sem_g = S.newsem()
            B_gD = Buf("gD")
            S.dma("sp", sem_g, [(gD[:, :], gs[0:NH, :])], reads=[B_g], writes=[B_gD])
            sem_g2 = S.newsem()
            S.dma("sp", sem_g2, [(GT[h], bass.AP(gD.tensor, h * LT, [[0, 128], [1, LT]])) for h in range(NH)],
                  reads=[B_gD], writes=[B_GT])
            B_T = Buf("T")
            S.dma("sp", sem_c, [(Tb[:, h, :], bass.AP(GT.tensor, h * 128 * LT + 127, [[LT - 1, 128], [1, 640]]))
                                for h in range(NH)], reads=[B_GT], writes=[B_T])
            S.op("pool", lambda e: e.memset(Tb[0:64, :, 576:640], -BIG), reads=[B_T], writes=[B_T])
            S.op("pool", lambda e: e.memset(Tb[64:128, :, 0:64], -BIG), reads=[B_T], writes=[B_T])
            S.barrier()
            areset()
            _ph[0] += 1
            if _ph[0] >= STOP:
                break

            cv = af32(32).rearrange("p (k v) -> p k v", v=2)
            cs = af32(32).rearrange("p (k v) -> p k v", v=2)
            sc_b = abf(32).rearrange("p (k v) -> p k v", v=2)
            gc = af32(32).rearrange("p (g k) -> p g k", g=2)
            adaR = af32(6 * D)
            b2 = af32(6 * D)
            slabs = [abf(KC * 512).rearrange("p (k c) -> p k c", k=KC) for _ in range(2)]
            B_slab = [Buf("slab0"), Buf("slab1")]
            sem_slab = [S.newsem(), S.newsem()]
            B_cv = Buf("cv")
            B_adaR = Buf("adaR")
            B_b2 = Buf("b2")
            B_adaD = Buf("adaD")
            B_modc = Buf("modc")
            sem_m = S.newsem()
            S.dma("sp", sem_m, [(cv, cvec[:, :, :]), (gc, gcols[:, :, :])], writes=[B_cv])
            S.dma("sp", sem_m, [(b2[0:1, :], b_ada[0:1, :]), (b2[1:2, :], b_ada[0:1, :])], writes=[B_b2])
            S.op("act", lambda e: e.activation(out=cs, in_=cv, func=AF.Sigmoid), reads=[B_cv], writes=[B_cv])
            S.op("dve", lambda e: e.tensor_tensor(out=sc_b, in0=cv, in1=cs, op=ALU.mult), reads=[B_cv], writes=[B_cv])
            for gi in range(24):
                sl = slabs[gi % 2]
                bs = B_slab[gi % 2]
                S.dma("pool", sem_slab[gi % 2], [(sl, wview(w_ada, gi * 512, 512))], writes=[bs])
                pb = PSB[gi % 2]
                ps = PS[gi % 2]
                for kc in range(KC):
                    S.op("pe", lambda e, ps=ps, sl=sl, kc=kc: e.matmul(ps[0:2, :], lhsT=sc_b[:, kc, :], rhs=sl[:, kc, :],
                                                                         start=(kc == 0), stop=(kc == KC - 1)),
                         reads=[B_cv, bs], writes=[pb])
                S.op("dve", lambda e, ps=ps, gi=gi: e.tensor_tensor(out=adaR[0:2, gi * 512:(gi + 1) * 512], in0=ps[0:2, :],
                                                                   in1=b2[0:2, gi * 512:(gi + 1) * 512], op=ALU.add),
                     reads=[pb, B_b2], writes=[B_adaR])
            S.dma("sp", sem_m, [(adaD[:, :], adaR[0:2, :])], reads=[B_adaR], writes=[B_adaD])
            adaC = af32(192).rearrange("p (c v) -> p c v", v=2)
            B_adaC = Buf("adaC")
            for c in range(96):
                S.op("pe", lambda e, c=c: e.transpose(PS[2][:, 2 * c:2 * c + 2], adaR[0:2, c * 128:(c + 1) * 128], identf[0:2, 0:2]),
                     reads=[B_adaR, B_const], writes=[PSB[2]])
            S.op("dve", lambda e: e.tensor_copy(out=adaC, in_=PS[2][:, 0:192].rearrange("p (c v) -> p c v", v=2)),
                 reads=[PSB[2]], writes=[B_adaC])
            for v in range(2):
                for (j, jsc, jsh, g) in ((0, 1, 0, 0), (2, 4, 3, 1)):
                    S.op("dve", lambda e, v=v, j=j, jsc=jsc, g=g: e.scalar_tensor_tensor(
                        out=modc[:, v, j, :], in0=adaC[:, jsc * 16:(jsc + 1) * 16, v], scalar=1.0, in1=gc[:, g, :],
                        op0=ALU.add, op1=ALU.mult), reads=[B_adaC, B_cv], writes=[B_modc])
                    S.op("dve", lambda e, v=v, j=j, jsh=jsh: e.tensor_copy(out=modc[:, v, j + 1, :],
                                                                            in_=adaC[:, jsh * 16:(jsh + 1) * 16, v]),
                         reads=[B_adaC], writes=[B_modc])
            S.barrier()
            areset()
            _ph[0] += 1
            if _ph[0] >= STOP:
                break

            def norm_tile(xt, bx, ntok, v, jm, hTdst, bh, junk, xn, bxn, ssq, rstd, bsm, psbase):
                S.op("act", lambda e: e.activation(out=junk[0:ntok, :], in_=xt[0:ntok, :], func=AF.Square,
                                                   accum_out=ssq[0:ntok, :]), reads=[bx], writes=[bxn, bsm])
                S.op("act", lambda e: e.activation(out=rstd[0:ntok, :], in_=ssq[0:ntok, :], func=AF.Sqrt,
                                                   scale=1.0 / D, bias=small[0:ntok, 63:64]), reads=[bsm, B_const], writes=[bsm])
                S.op("dve", lambda e: e.reciprocal(out=rstd[0:ntok, :], in_=rstd[0:ntok, :]), reads=[bsm], writes=[bsm])
                S.op("act", lambda e: e.activation(out=xn[0:ntok, :], in_=xt[0:ntok, :], func=AF.Identity,
                                                   scale=rstd[0:ntok, :]), reads=[bx, bsm], writes=[bxn])
                for q4 in range(4):
                    pi = psbase + q4
                    for i in range(4):
                        kc = q4 * 4 + i
                        S.op("pe", lambda e, pi=pi, i=i, kc=kc: e.transpose(PS[pi][:, i * 128:i * 128 + ntok],
                                                                              xn[0:ntok, kc * 128:(kc + 1) * 128],
                                                                              identf[0:ntok, 0:ntok]),
                             reads=[bxn, B_const], writes=[PSB[pi]])
                    for i in range(4):
                        kc = q4 * 4 + i
                        if q4 % 2 == 0:
                            S.op("dve", lambda e, pi=pi, i=i, kc=kc: e.tensor_scalar(
                                out=hTdst[:, kc, 0:ntok], in0=PS[pi][:, i * 128:i * 128 + ntok],
                                scalar1=modc[:, v, jm, kc:kc + 1], scalar2=modc[:, v, jm + 1, kc:kc + 1],
                                op0=ALU.mult, op1=ALU.add), reads=[PSB[pi], B_modc], writes=[bh])
                        else:
                            S.op("act", lambda e, pi=pi, i=i, kc=kc: e.activation(
                                out=hTdst[:, kc, 0:ntok], in_=PS[pi][:, i * 128:i * 128 + ntok], func=AF.Identity,
                                scale=modc[:, v, jm, kc:kc + 1], bias=modc[:, v, jm + 1, kc:kc + 1]),
                                reads=[PSB[pi], B_modc], writes=[bh])

            def pipeline(units, stages):
                n = len(units)
                K = len(stages)
                for i, U in enumerate(units):
                    U.idx = i
                for t in range(n + K - 1):
                    for k in range(K):
                        i = t - k
                        if 0 <= i < n:
                            U = units[i]
                            if k == 0 and U.pre is not None:
                                U.pre()
                            stages[k](U)
                            if k == K - 1 and U.post is not None:
                                U.post()

            class NU:
                pass

            def norm_pass(tiles, sc_off, sh_off, gidx, store_blk, dep_bufs):
                xt = [af32(D) for _ in range(3)]
                Bx = [Buf("x0"), Buf("x1"), Buf("x2")]
                semx = [S.newsem(), S.newsem(), S.newsem()]
                tb = [af32(D) for _ in range(2)]
                Bt = [Buf("t0"), Buf("t1")]
                junk = af32(D)
                Bjunk = Buf("junk")
                hb16 = [abf(D) for _ in range(2)]
                Bhb16 = [Buf("hb16_0"), Buf("hb16_1")]
                hst = [abf(KC * 512).rearrange("p (k c) -> p k c", k=KC) for _ in range(2)]
                Bh = [Buf("h0"), Buf("h1")]
                semh = [S.newsem(), S.newsem()]
                Bsm = [Buf("sm0"), Buf("sm1"), Buf("sm2"), Buf("sm3")]
                Ab = [af32(D) for _ in range(2)]
                Sb = [af32(D) for _ in range(2)]
                gtmp = af32(D)
                Bmodb = Buf("modb")
                semmb = S.newsem()
                S.dma("sp", semmb, [(gtmp, bass.AP(grow.tensor, gidx * D, [[0, 128], [1, D]]))] +
                      [(Ab[v], bass.AP(adaD.tensor, v * 6 * D + sc_off * D, [[0, 128], [1, D]])) for v in range(2)] +
                      [(Sb[v], bass.AP(adaD.tensor, v * 6 * D + sh_off * D, [[0, 128], [1, D]])) for v in range(2)],
                      reads=[B_adaD], writes=[Bmodb])
                for v in range(2):
                    S.op("dve", lambda e, v=v: e.scalar_tensor_tensor(out=Ab[v], in0=Ab[v], scalar=1.0, in1=gtmp, op0=ALU.add, op1=ALU.mult),
                         reads=[Bmodb], writes=[Bmodb])

                def s0(U):
                    xb, sb = U.idx % 3, U.idx % 4
                    ntok = U.ntok
                    S.dma("sp", semx[xb], [(xt[xb][0:ntok, :], U.src)], reads=U.srcbufs, writes=[Bx[xb]])
                    S.op("act", lambda e: e.activation(out=junk[0:ntok, :], in_=xt[xb][0:ntok, :], func=AF.Square,
                                                       accum_out=small[0:ntok, 2 * sb:2 * sb + 1]), reads=[Bx[xb]], writes=[Bjunk, Bsm[sb]])

                def s1(U):
                    xb, sb, s4 = U.idx % 3, U.idx % 2, U.idx % 4
                    ntok, v = U.ntok, U.v
                    rstd = small[:, 2 * s4 + 1:2 * s4 + 2]
                    S.op("act", lambda e: e.activation(out=rstd[0:ntok, :], in_=small[0:ntok, 2 * s4:2 * s4 + 1], func=AF.Sqrt,
                                                       scale=1.0 / D, bias=small[0:ntok, 63:64]), reads=[Bsm[s4], B_const], writes=[Bsm[s4]])
                    S.op("dve", lambda e: e.reciprocal(out=rstd[0:ntok, :], in_=rstd[0:ntok, :]), reads=[Bsm[s4]], writes=[Bsm[s4]])
                    S.op("dve", lambda e: e.scalar_tensor_tensor(out=tb[sb][0:ntok, :], in0=xt[xb][0:ntok, :], scalar=rstd[0:ntok, :],
                                                                 in1=Ab[v][0:ntok, :], op0=ALU.mult, op1=ALU.mult),
                         reads=[Bx[xb], Bsm[s4], Bmodb], writes=[Bt[sb]])
                    S.op("pool", lambda e: e.tensor_tensor(out=hb16[sb][0:ntok, :], in0=tb[sb][0:ntok, :], in1=Sb[v][0:ntok, :], op=ALU.add),
                         reads=[Bt[sb], Bmodb], writes=[Bhb16[sb]])

                def s2(U):
                    sb = U.idx % 2
                    ntok = U.ntok
                    for q4 in range(4):
                        pi = 4 * sb + q4
                        for i in range(4):
                            kc = q4 * 4 + i
                            S.op("pe", lambda e, pi=pi, i=i, kc=kc: e.matmul(PS[pi][:, i * 128:i * 128 + ntok],
                                                                             lhsT=hb16[sb][0:ntok, kc * 128:(kc + 1) * 128],
                                                                             rhs=identb[0:ntok, 0:ntok], start=True, stop=True),
                                 reads=[Bhb16[sb], B_const], writes=[PSB[pi]])

                def s3(U):
                    sb = U.idx % 2
                    ntok = U.ntok
                    for q4 in range(4):
                        pi = 4 * sb + q4
                        src = PS[pi][:, :].rearrange("p (k c) -> p k c", k=4)[:, :, 0:ntok]
                        dst = U.hTdst[:, q4 * 4:(q4 + 1) * 4, 0:ntok]
                        if q4 % 2 == 0:
                            S.op("act", lambda e, src=src, dst=dst: e.activation(out=dst, in_=src, func=AF.Copy), reads=[PSB[pi]], writes=[U.bh])
                        else:
                            S.op("dve", lambda e, src=src, dst=dst: e.tensor_copy(out=dst, in_=src), reads=[PSB[pi]], writes=[U.bh])

                units = []
                for (src, srcbufs, ntok, v, blk, j, last) in tiles:
                    U = NU()
                    U.pre = None
                    U.post = None
                    U.src, U.srcbufs, U.ntok, U.v = src, srcbufs, ntok, v
                    hb = blk % 2
                    U.hTdst = hst[hb][:, :, j * 128:(j + 1) * 128]
                    U.bh = Bh[hb]
                    if last:
                        def post(blk=blk, hb=hb):
                            store_blk(blk, hst[hb], Bh[hb], semh[hb])
                        U.post = post
                    units.append(U)
                pipeline(units, [s0, s1, s2, s3])

            B_HT = [Buf("HT%d" % i) for i in range(NBLK + 1)]
            tiles1 = []
            for blk in range(NBLK + 1):
                nt = 4 if blk < NBLK else 1
                for j in range(nt):
                    src = xin[blk, j * 128:(j + 1) * 128, :] if blk < NBLK else xs_in[:, :]
                    tiles1.append((src, [], 128 if blk < NBLK else NS, 0 if blk < NBLK else 1, blk, j, j == nt - 1))

            def store1(blk, st, bst, sem):
                S.dma("sp", sem, [(HT[blk], st)], reads=[bst], writes=[B_HT[blk]])
            norm_pass(tiles1, 1, 0, 0, store1, None)
            S.barrier()
            areset()
            _ph[0] += 1
            if _ph[0] >= STOP:
                break

            wg = [abf(KC * 1024).rearrange("p (k c) -> p k c", k=KC) for _ in range(2)]
            Bwg = [Buf("wg0"), Buf("wg1")]
            semw = [S.newsem(), S.newsem()]
            hbk = [abf(KC * 512).rearrange("p (k c) -> p k c", k=KC) for _ in range(2)]
            Bhb = [Buf("hb0"), Buf("hb1")]
            semhb = [S.newsem(), S.newsem()]
            kvf = [af32(1024) for _ in range(2)]
            Bkvf = [Buf("kvf0"), Buf("kvf1")]
            semkvf = [S.newsem(), S.newsem()]
            kvb = [abf(1024) for _ in range(2)]
            Bkvb = [Buf("kvb0"), Buf("kvb1")]
            semkvb = [S.newsem(), S.newsem()]
            ktb = [abf(NH * 512).rearrange("p (h c) -> p h c", h=NH) for _ in range(2)]
            Bktb = [Buf("ktb0"), Buf("ktb1")]
            semktb = [S.newsem(), S.newsem()]
            qst = [abf(NH * 512).rearrange("p (h c) -> p h c", h=NH) for _ in range(2)]
            Bqst = [Buf("qst0"), Buf("qst1")]
            semqst = [S.newsem(), S.newsem()]
            gst = [af32(NH * 512).rearrange("p (h c) -> p h c", h=NH) for _ in range(2)]
            Bgst = [Buf("gst0"), Buf("gst1")]
            semgst = [S.newsem(), S.newsem()]
            B_scr = Buf("scr_w")

            own_blocks = [(0, 512, 0), (1, 512, 512), (2, 512, 1024), (3, 512, 1536), (NBLK, NS, 2048)]
            a_blocks = [(0, 512, 0), (1, 512, 512), (4, 512, 1024), (2, 512, 1536), (3, 512, 2048), (5, 512, 2560), (NBLK, NS, 3072)]
            b_blocks = [(0, 512, 0), (1, 512, 512), (2, 512, 1024), (3, 512, 1536)] + \
                       [(6 + i, 512, 2048 + 512 * i) for i in range(14)] + [(NBLK, NS, 9216)]
            own_row = {0: 0, 1: 512, 2: 1024, 3: 1536, NBLK: 2048}
            a_out_row = {2: 0, NBLK: 512}
            groups = [("q", 0, QaT), ("k", 1024, "a"), ("v", 2048, "a"), ("q", 3072, QbT), ("k", 4096, "b"), ("v", 5120, "b"),
                      ("g", 6144, (GA, 0)), ("g", 7168, (GA, 8)), ("g", 8192, (GB, 0)), ("g", 9216, (GB, 8))]
            cnt = {"hb": 0, "kv": 0, "kt": 0, "q": 0, "g": 0, "ps": 0}
            def blist_of(kind, dst):
                if kind in ("q", "g"):
                    return own_blocks
                return a_blocks if dst == "a" else b_blocks

            def load_w2(gi):
                c0_ = groups[gi][1]
                W_ = wg[gi % 2]
                S.dma("pool", semw[gi % 2], [(W_[:, :, 0:512], wview(w_in, c0_, 512)), (W_[:, :, 512:1024], wview(w_in, c0_ + 512, 512))],
                      writes=[Bwg[gi % 2]])
            seq2 = [blk for (kind, c0, dst) in groups for (blk, ntk, idx0) in blist_of(kind, dst)]

            def load_h2(n):
                S.dma("sp", semhb[n % 2], [(hbk[n % 2], HT[seq2[n]])], reads=[B_HT[seq2[n]]], writes=[Bhb[n % 2]])
            load_w2(0)
            load_h2(0)
            pendk = [None]
            for gi, (kind, c0, dst) in enumerate(groups):
                wb = gi % 2
                W = wg[wb]
                if gi + 1 < len(groups):
                    load_w2(gi + 1)
                blist = blist_of(kind, dst)
                for (blk, ntk, idx0) in blist:
                    hi_ = cnt["hb"] % 2
                    hb_ = hbk[hi_]
                    if cnt["hb"] + 1 < len(seq2):
                        load_h2(cnt["hb"] + 1)
                    cnt["hb"] += 1
                    if kind == "q":
                        qi = cnt["q"] % 2
                        cnt["q"] += 1
                        for cc in range(8):
                            pi = cnt["ps"] % 8
                            cnt["ps"] += 1
                            for kc in range(KC):
                                S.op("pe", lambda e, pi=pi, cc=cc, kc=kc, W=W, hb_=hb_, ntk=ntk: e.matmul(
                                    PS[pi][:, 0:ntk], lhsT=W[:, kc, cc * 128:(cc + 1) * 128], rhs=hb_[:, kc, 0:ntk],
                                    start=(kc == 0), stop=(kc == KC - 1)), reads=[Bwg[wb], Bhb[hi_]], writes=[PSB[pi]])
                            eng = "act" if cc % 2 == 0 else "dve"
                            if eng == "act":
                                S.op("act", lambda e, pi=pi, cc=cc, qi=qi, ntk=ntk: e.activation(
                                    out=qst[qi][:, cc, 0:ntk], in_=PS[pi][:, 0:ntk], func=AF.Copy), reads=[PSB[pi]], writes=[Bqst[qi]])
                            else:
                                S.op("dve", lambda e, pi=pi, cc=cc, qi=qi, ntk=ntk: e.tensor_copy(
                                    out=qst[qi][:, cc, 0:ntk], in_=PS[pi][:, 0:ntk]), reads=[PSB[pi]], writes=[Bqst[qi]])
                        r0 = own_row[blk]
                        S.dma("sp", semqst[qi], [(dst[:, :, r0:r0 + ntk].rearrange("h p c -> p h c"), qst[qi][:, :, 0:ntk])],
                              reads=[Bqst[qi]], writes=[B_scr])
                    elif kind == "g":
                        G, ch0 = dst
                        qi = cnt["g"] % 2
                        cnt["g"] += 1
                        for cc in range(8):
                            pi = cnt["ps"] % 8
                            cnt["ps"] += 1
                            for kc in range(KC):
                                S.op("pe", lambda e, pi=pi, cc=cc, kc=kc, W=W, hb_=hb_, ntk=ntk: e.matmul(
                                    PS[pi][:, 0:ntk], lhsT=W[:, kc, cc * 128:(cc + 1) * 128], rhs=hb_[:, kc, 0:ntk],
                                    start=(kc == 0), stop=(kc == KC - 1)), reads=[Bwg[wb], Bhb[hi_]], writes=[PSB[pi]])
                            S.op("act", lambda e, pi=pi, cc=cc, qi=qi, ntk=ntk: e.activation(
                                out=gst[qi][:, cc, 0:ntk], in_=PS[pi][:, 0:ntk], func=AF.Sigmoid), reads=[PSB[pi]], writes=[Bgst[qi]])
                        r0 = own_row[blk]
                        S.dma("sp", semgst[qi], [(G[ch0:ch0 + 8, :, r0:r0 + ntk].rearrange("h p c -> p h c"), gst[qi][:, :, 0:ntk])],
                              reads=[Bgst[qi]], writes=[B_scr])
                    else:
                        isA = (dst == "a")
                        KT_, VS_ = (KaT, VaS) if isA else (KbT, VbS)
                        nt = 4 if ntk == 512 else 1
                        ki = cnt["kt"] % 2
                        if kind == "k":
                            cnt["kt"] += 1
                        for j in range(nt):
                            ntok = 128 if ntk == 512 else NS
                            fi = cnt["kv"] % 2
                            cnt["kv"] += 1
                            for half in range(2):
                                pi = cnt["ps"] % 8
                                cnt["ps"] += 1
                                for kc in range(KC):
                                    S.op("pe", lambda e, pi=pi, half=half, kc=kc, W=W, hb_=hb_, j=j, ntok=ntok: e.matmul(
                                        PS[pi][0:ntok, :], lhsT=hb_[:, kc, j * 128:j * 128 + ntok], rhs=W[:, kc, half * 512:(half + 1) * 512],
                                        start=(kc == 0), stop=(kc == KC - 1)), reads=[Bwg[wb], Bhb[hi_]], writes=[PSB[pi]])
                                S.op("act", lambda e, pi=pi, half=half, fi=fi, ntok=ntok: e.activation(
                                    out=kvf[fi][0:ntok, half * 512:(half + 1) * 512], in_=PS[pi][0:ntok, :], func=AF.Copy),
                                    reads=[PSB[pi]], writes=[Bkvf[fi]])
                                S.op("pool", lambda e, half=half, fi=fi, ntok=ntok: e.tensor_copy(
                                    out=kvb[fi][0:ntok, half * 512:(half + 1) * 512],
                                    in_=kvf[fi][0:ntok, half * 512:(half + 1) * 512]),
                                    reads=[Bkvf[fi]], writes=[Bkvb[fi]])
                            outs = []
                            if isA and blk in a_out_row:
                                o_t = ka_o if kind == "k" else va_o
                                r0 = a_out_row[blk] + j * 128
                                outs.append((o_t[r0:r0 + ntok, :], kvf[fi][0:ntok, :]))
                            if (not isA) and blk in own_row:
                                o_t = kb_o if kind == "k" else vb_o
                                r0 = own_row[blk] + j * 128
                                outs.append((o_t[r0:r0 + ntok, :], kvf[fi][0:ntok, :]))
                            if outs:
                                S.dma("sp", semkvf[fi], outs, reads=[Bkvf[fi]], writes=[])
                            if kind == "v":
                                tix = (idx0 + j * 128) // 128
                                S.dma("sp", semkvb[fi], [(VS_[tix, 0:ntok, :], kvb[fi][0:ntok, :])], reads=[Bkvb[fi]], writes=[B_scr])
                            else:
                                def ktrans(fi=fi, ntok=ntok, ki=ki, j=j):
                                    pi = cnt["ps"] % 8
                                    cnt["ps"] += 1
                                    psb16 = PS[pi][:, :].bitcast(BF16)
                                    for h in range(NH):
                                        S.op("pe", lambda e, psb16=psb16, h=h, fi=fi, ntok=ntok: e.transpose(
                                            psb16[:, h * 128:h * 128 + ntok], kvb[fi][0:ntok, h * 128:(h + 1) * 128], identb[0:ntok, 0:ntok]),
                                            reads=[Bkvb[fi], B_const], writes=[PSB[pi]])
                                    S.op("dve", lambda e, psb16=psb16, ki=ki, j=j, ntok=ntok: e.tensor_copy(
                                        out=ktb[ki][:, :, j * 128:j * 128 + ntok],
                                        in_=psb16.rearrange("p (h c) -> p h c", h=NH)[:, :, 0:ntok]), reads=[PSB[pi]], writes=[Bktb[ki]])
                                if pendk[0] is not None:
                                    pendk[0]()
                                pendk[0] = ktrans
                        if kind == "k" and pendk[0] is not None:
                            pendk[0]()
                            pendk[0] = None
                        if kind == "k":
                            S.dma("sp", semktb[ki], [(KT_[:, :, idx0:idx0 + ntk].rearrange("h p c -> p h c"), ktb[ki][:, :, 0:ntk])],
                                  reads=[Bktb[ki]], writes=[B_scr])
            S.barrier()
            areset()
            _ph[0] += 1
            if _ph[0] >= STOP:
                break

            def sm(i):
                return small[:, i:i + 1]
            qtA = [abf(NH * 128).rearrange("p (h c) -> p h c", h=NH) for _ in range(2)]
            ktA = [abf(NH * 640).rearrange("p (h c) -> p h c", h=NH) for _ in range(2)]
            vtA = [abf(5 * 1024).rearrange("p (t c) -> p t c", t=5) for _ in range(2)]
            BldA = [Buf("ldA0"), Buf("ldA1")]
            semldA = [S.newsem(), S.newsem()]
            s_sb = [af32(640) for _ in range(2)]
            Bs = [Buf("s0"), Buf("s1")]
            p_sb = [abf(640) for _ in range(2)]
            Bp = [Buf("p0"), Buf("p1")]
            pT = [abf(5 * 128).rearrange("p (t c) -> p t c", t=5) for _ in range(2)]
            BpT = [Buf("pT0"), Buf("pT1")]
            oa = [abf(1024) for _ in range(3)]
            Boa = [Buf("oa0"), Buf("oa1"), Buf("oa2")]
            oaTs = [abf(NH * 512).rearrange("p (h c) -> p h c", h=NH) for _ in range(2)]
            BoaT = [Buf("oaT0"), Buf("oaT1")]
            semoaT = [S.newsem(), S.newsem()]
            Bst = [Buf("st%d" % i) for i in range(8)]
            B_OaT = Buf("OaT")
            ckA = abf(4 * 1024).rearrange("p (t c) -> p t c", t=4)
            cvA = abf(4 * 1024).rearrange("p (t c) -> p t c", t=4)
            ktS = abf(NH * 544).rearrange("p (h c) -> p h c", h=NH)
            qtS = abf(NH * NS).rearrange("p (h c) -> p h c", h=NH)
            vS0 = abf(1024)
            BckA = Buf("ckA")
            BcvA = Buf("cvA")
            BktS = Buf("ktS")
            semcA = S.newsem()
            semcA2 = S.newsem()

            class AU:
                pass

            def mkA(nq, hsel, q_ap, k_ap, nk, vblocks, bq, bk, bv, flag_c0, oa_t, boa):
                U = AU()
                U.nq, U.hsel, U.q_ap, U.k_ap, U.nk, U.vblocks = nq, hsel, q_ap, k_ap, nk, vblocks
                U.bq, U.bk, U.bv, U.flag_c0, U.oa_t, U.boa = bq, bk, bv, flag_c0, oa_t, boa
                U.pre = None
                U.post = None
                return U

            def a0(U):
                u = U.idx % 2
                nq, nk = U.nq, U.nk
                n1 = min(nk, 512)
                S.op("pe", lambda e: e.matmul(PS[2 * u][0:nq, 0:n1], lhsT=U.q_ap, rhs=U.k_ap[:, 0:n1], start=True, stop=True),
                     reads=[U.bq, U.bk], writes=[PSB[2 * u]])
                if nk > 512:
                    S.op("pe", lambda e: e.matmul(PS[2 * u + 1][0:nq, 0:nk - 512], lhsT=U.q_ap, rhs=U.k_ap[:, 512:nk], start=True, stop=True),
                         reads=[U.bq, U.bk], writes=[PSB[2 * u + 1]])

            def a1(U):
                u = U.idx % 2
                nq, nk, hsel = U.nq, U.nk, U.hsel
                n1 = min(nk, 512)
                s_ = s_sb[u]
                st = 8 + 4 * (U.idx % 8)
                bst = Bst[U.idx % 8]
                S.op("dve", lambda e: e.scalar_tensor_tensor(out=s_[0:nq, 0:n1], in0=PS[2 * u][0:nq, 0:n1], scalar=SCALE,
                                                             in1=Tb[0:nq, hsel, 0:n1], op0=ALU.mult, op1=ALU.add),
                     reads=[PSB[2 * u], B_T], writes=[Bs[u]])
                if nk > 512:
                    S.op("dve", lambda e: e.scalar_tensor_tensor(out=s_[0:nq, 512:nk], in0=PS[2 * u + 1][0:nq, 0:nk - 512], scalar=SCALE,
                                                                 in1=Tb[0:nq, hsel, 512:nk], op0=ALU.mult, op1=ALU.add),
                         reads=[PSB[2 * u + 1], B_T], writes=[Bs[u]])
                if U.flag_c0 is not None:
                    fc = U.flag_c0
                    S.op("dve", lambda e: e.tensor_scalar(out=s_[0:nq, fc:nk], in0=s_[0:nq, fc:nk], scalar1=flg[0:nq, 0:1],
                                                          scalar2=None, op0=ALU.add), reads=[Bs[u], B_const], writes=[Bs[u]])
                S.op("dve", lambda e: e.reduce_max(out=sm(st)[0:nq, :], in_=s_[0:nq, 0:nk], axis=AX.X), reads=[Bs[u]], writes=[bst])
                S.op("dve", lambda e: e.tensor_scalar(out=sm(st + 1)[0:nq, :], in0=sm(st)[0:nq, :], scalar1=-1.0, scalar2=None,
                                                      op0=ALU.mult), reads=[bst], writes=[bst])

            def a2(U):
                u = U.idx % 2
                nq, nk = U.nq, U.nk
                st = 8 + 4 * (U.idx % 8)
                bst = Bst[U.idx % 8]
                S.op("act", lambda e: e.activation(out=p_sb[u][0:nq, 0:nk], in_=s_sb[u][0:nq, 0:nk], func=AF.Exp, bias=sm(st + 1)[0:nq, :],
                                                   scale=1.0, accum_out=sm(st + 2)[0:nq, :]), reads=[Bs[u], bst], writes=[Bp[u], bst])

            def a3(U):
                u = U.idx % 2
                nq = U.nq
                st = 8 + 4 * (U.idx % 8)
                bst = Bst[U.idx % 8]
                S.op("dve", lambda e: e.reciprocal(out=sm(st + 3)[0:nq, :], in_=sm(st + 2)[0:nq, :]), reads=[bst], writes=[bst])
                psb16 = PS[4 + u][:, :].bitcast(BF16)
                for bi, (koff, nkb, v_ap) in enumerate(U.vblocks):
                    S.op("pe", lambda e, bi=bi, koff=koff, nkb=nkb: e.transpose(psb16[0:nkb, bi * 128:bi * 128 + nq],
                                                                                 p_sb[u][0:nq, koff:koff + nkb], identb[0:nq, 0:nq]),
                         reads=[Bp[u], B_const], writes=[PSB[4 + u]])

            def a4(U):
                u = U.idx % 2
                nq = U.nq
                nb = len(U.vblocks)
                psb16 = PS[4 + u][:, :].bitcast(BF16)
                S.op("act", lambda e: e.activation(out=pT[u][:, 0:nb, 0:nq],
                                                   in_=psb16[:, 0:nb * 128].rearrange("p (t c) -> p t c", t=nb)[:, :, 0:nq],
                                                   func=AF.Copy), reads=[PSB[4 + u]], writes=[BpT[u]])

            def a5(U):
                u = U.idx % 2
                nq = U.nq
                nb = len(U.vblocks)
                for bi, (koff, nkb, v_ap) in enumerate(U.vblocks):
                    S.op("pe", lambda e, bi=bi, nkb=nkb, v_ap=v_ap: e.matmul(PS[6 + u][0:nq, 0:128], lhsT=pT[u][0:nkb, bi, 0:nq], rhs=v_ap,
                                                                             start=(bi == 0), stop=(bi == nb - 1)),
                         reads=[BpT[u], U.bv], writes=[PSB[6 + u]])

            def a6(U):
                u = U.idx % 2
                nq, hsel = U.nq, U.hsel
                st = 8 + 4 * (U.idx % 8)
                bst = Bst[U.idx % 8]
                S.op("act", lambda e: e.activation(out=U.oa_t[0:nq, hsel * 128:(hsel + 1) * 128], in_=PS[6 + u][0:nq, 0:128],
                                                   func=AF.Identity, scale=sm(st + 3)[0:nq, :]), reads=[PSB[6 + u], bst], writes=[U.boa])

            afc = [0]

            def a_finish(nq, oa_t, boa, stage, bstage, col0):
                pi = 5
                psb16 = PS[pi][:, :].bitcast(BF16)
                for h in range(NH):
                    S.op("pe", lambda e, h=h: e.transpose(psb16[:, h * 128:h * 128 + nq], oa_t[0:nq, h * 128:(h + 1) * 128],
                                                           identb[0:nq, 0:nq]), reads=[boa, B_const], writes=[PSB[pi]])
                S.op("dve", lambda e: e.tensor_copy(out=stage[:, :, col0:col0 + nq],
                                                    in_=psb16.rearrange("p (h c) -> p h c", h=NH)[:, :, 0:nq]),
                     reads=[PSB[pi]], writes=[bstage])

            unitsA = []
            pc = 0
            for piece in range(2):
                for j in range(8):
                    li = pc % 2
                    oi = pc % 3
                    sti = (pc // 4) % 2
                    tok0 = piece * 1024 + 128 * j
                    k0 = piece * 1536 + 128 * j
                    t0_ = k0 // 128
                    flag_c0 = (1024 - 128 * j) if (piece == 0 and j >= 4) else None

                    def preA(li=li, tok0=tok0, k0=k0, t0_=t0_):
                        S.dma("sp", semldA[li], [
                            (qtA[li], QaT[:, :, tok0:tok0 + 128].rearrange("h p c -> p h c")),
                            (ktA[li], KaT[:, :, k0:k0 + 640].rearrange("h p c -> p h c")),
                            (vtA[li], VaS[t0_:t0_ + 5].rearrange("t p c -> p t c"))], reads=[B_scr], writes=[BldA[li]])

                    def postA(pc=pc, oi=oi, sti=sti):
                        a_finish(128, oa[oi], Boa[oi], oaTs[sti], BoaT[sti], (pc % 4) * 128)
                        if pc % 4 == 3:
                            r0 = (pc // 4) * 512
                            S.dma("sp", semoaT[sti], [(OaT[:, :, r0:r0 + 512].rearrange("h p c -> p h c"), oaTs[sti])],
                                  reads=[BoaT[sti]], writes=[B_OaT])
                    for h in range(NH):
                        vbl = [(128 * t, 128, vtA[li][:, t, h * 128:(h + 1) * 128]) for t in range(5)]
                        U = mkA(128, h, qtA[li][:, h, :], ktA[li][:, h, :], 640, vbl, BldA[li], BldA[li], BldA[li], flag_c0, oa[oi], Boa[oi])
                        if h == 0:
                            U.pre = preA
                        if h == NH - 1:
                            U.post = postA
                        unitsA.append(U)
                    pc += 1
            oiS = pc % 3

            def preS():
                S.dma("pool", semcA, [(ckA, ca_k.rearrange("(t p) c -> p t c", p=128)), (cvA, ca_v.rearrange("(t p) c -> p t c", p=128))],
                      writes=[BckA, BcvA])
                S.dma("sp", semcA2, [(qtS, QaT[:, :, 2048:2080].rearrange("h p c -> p h c")),
                                     (ktS[:, :, 0:NS], KaT[:, :, 3072:3104].rearrange("h p c -> p h c")),
                                     (vS0[0:NS, :], VaS[24, 0:NS, :])], reads=[B_scr], writes=[BktS])
                for h in range(NH):
                    pi = h % 2
                    psb16 = PS[pi][:, :].bitcast(BF16)
                    for t in range(4):
                        S.op("pe", lambda e, psb16=psb16, h=h, t=t: e.transpose(psb16[:, t * 128:(t + 1) * 128], ckA[:, t, h * 128:(h + 1) * 128],
                                                                                  identb[:, :]), reads=[BckA, B_const], writes=[PSB[pi]])
                    S.op("dve", lambda e, psb16=psb16, h=h: e.tensor_copy(out=ktS[:, h, NS:NS + 512], in_=psb16[:, 0:512]),
                         reads=[PSB[pi]], writes=[BktS])

            def postS():
                a_finish(NS, oa[oiS], Boa[oiS], oaTs[0], BoaT[0], 0)
                S.dma("sp", semoaT[0], [(OaT[:, :, 2048:2080].rearrange("h p c -> p h c"), oaTs[0][:, :, 0:NS])],
                      reads=[BoaT[0]], writes=[B_OaT])
            for h in range(NH):
                vbl = [(0, NS, vS0[0:NS, h * 128:(h + 1) * 128])] + \
                      [(NS + 128 * t, 128, cvA[:, t, h * 128:(h + 1) * 128]) for t in range(4)]
                U = mkA(NS, h, qtS[:, h, :], ktS[:, h, :], 544, vbl, BktS, BktS, BcvA, None, oa[oiS], Boa[oiS])
                if h == 0:
                    U.pre = preS
                if h == NH - 1:
                    U.post = postS
                unitsA.append(U)
            pipeline(unitsA, [a0, a1, a2, a3, a4, a5, a6])
            S.barrier()
            areset()
            _ph[0] += 1
            if _ph[0] >= STOP:
                break

            qb = abf(NH * 1024).rearrange("p (h c) -> p h c", h=NH)
            Bqb = Buf("qb")
            semqb = S.newsem()
            ktB = [abf(NH * 1024).rearrange("p (h c) -> p h c", h=NH) for _ in range(2)]
            vtB = [abf(8 * 1024).rearrange("p (t c) -> p t c", t=8) for _ in range(2)]
            BktB = [Buf("ktB0"), Buf("ktB1")]
            BvtB = [Buf("vtB0"), Buf("vtB1")]
            semkB = [S.newsem(), S.newsem()]
            semvB = [S.newsem(), S.newsem()]
            acc = af32(NH * 1024).rearrange("p (h c) -> p h c", h=NH)
            Bacc = Buf("acc")
            carry = af32(64)
            Bcar = Buf("carry")
            m_sb = [af32(1024) for _ in range(2)]
            Bm = [Buf("m0"), Buf("m1")]
            Pb = [af32(1026) for _ in range(3)]
            BPb = [Buf("P0"), Buf("P1"), Buf("P2")]
            A_sb = [abf(1024) for _ in range(2)]
            BA = [Buf("A0"), Buf("A1")]
            AT_sb = [abf(8 * 128).rearrange("p (t c) -> p t c", t=8) for _ in range(2)]
            BAT = [Buf("AT0"), Buf("AT1")]
            obst = abf(NH * 1024).rearrange("p (h c) -> p h c", h=NH)
            Bobst = Buf("obst")
            semob = S.newsem()
            B_ObT = Buf("ObT")

            class BU:
                pass

            def mk_unit(nq, q_ap, bq, k_ap, bk, c0, c1, blocks, bv, diag, bias_ap, car_ap, acc_ap, first):
                U = BU()
                U.nq, U.q_ap, U.bq, U.k_ap, U.bk, U.c0, U.c1 = nq, q_ap, bq, k_ap, bk, c0, c1
                U.blocks, U.bv, U.diag, U.bias_ap, U.car_ap, U.acc_ap, U.first = blocks, bv, diag, bias_ap, car_ap, acc_ap, first
                U.pre = None
                U.post = None
                ch = []
                c = c0
                while c < c1:
                    w = min(512, c1 - c)
                    ch.append((c, w))
                    c += w
                U.chunks = ch
                return U

            def st_z(U):
                u = U.idx % 2
                zb = [2 * u, 2 * u + 1]
                for ci, (cc, w) in enumerate(U.chunks):
                    S.op("pe", lambda e, ci=ci, cc=cc, w=w: e.matmul(PS[zb[ci]][0:U.nq, 0:w], lhsT=U.q_ap, rhs=U.k_ap[:, cc:cc + w],
                                                                     start=True, stop=True), reads=[U.bq, U.bk], writes=[PSB[zb[ci]]])

            def st_sig(U):
                u = U.idx % 2
                zb = [2 * u, 2 * u + 1]
                m_ = m_sb[u]
                nq = U.nq
                for ci, (cc, w) in enumerate(U.chunks):
                    if U.bias_ap is None:
                        S.op("act", lambda e, ci=ci, cc=cc, w=w: e.activation(out=m_[0:nq, cc:cc + w], in_=PS[zb[ci]][0:nq, 0:w],
                                                                              func=AF.Sigmoid, scale=-SCALE),
                             reads=[PSB[zb[ci]]], writes=[Bm[u]])
                    else:
                        S.op("act", lambda e, ci=ci, cc=cc, w=w: e.activation(out=m_[0:nq, cc:cc + w], in_=PS[zb[ci]][0:nq, 0:w],
                                                                              func=AF.Sigmoid, scale=-SCALE, bias=U.bias_ap[0:nq, :]),
                             reads=[PSB[zb[ci]], B_const], writes=[Bm[u]])

            def st_pinit(U):
                u3 = U.idx % 3
                P_ = Pb[u3]
                nq, c0 = U.nq, U.c0
                if U.first:
                    S.op("pool", lambda e: e.memset(P_[0:nq, c0:c0 + 1], 1.0), writes=[BPb[u3]])
                else:
                    S.op("act", lambda e: e.activation(out=P_[0:nq, c0:c0 + 1], in_=U.car_ap[0:nq, :], func=AF.Copy),
                         reads=[Bcar], writes=[BPb[u3]])

            def st_sig2(U):
                st_sig(U)
                st_pinit(U)

            def st_scan(U):
                u = U.idx % 2
                u3 = U.idx % 3
                m_, P_ = m_sb[u], Pb[u3]
                nq, c0, c1 = U.nq, U.c0, U.c1
                if U.diag:
                    nkb0 = U.blocks[0][1]
                    S.op("dve", lambda e: e.tensor_tensor(out=m_[0:nq, c0:c0 + nkb0], in0=m_[0:nq, c0:c0 + nkb0],
                                                          in1=maskL[0:nq, 0:nkb0], op=ALU.max),
                         reads=[Bm[u], B_const], writes=[Bm[u]])
                S.op("dve", lambda e: e.tensor_tensor_scan(out=P_[0:nq, c0 + 1:c1 + 1], data0=m_[0:nq, c0:c1], data1=m_[0:nq, c0:c1],
                                                           initial=P_[0:nq, c0:c0 + 1], op0=ALU.mult, op1=ALU.bypass),
                     reads=[Bm[u], BPb[u3]], writes=[BPb[u3]])

            def st_sub(U):
                u = U.idx % 2
                u3 = U.idx % 3
                P_, A_ = Pb[u3], A_sb[u]
                nq, c0, c1 = U.nq, U.c0, U.c1
                S.op("pool", lambda e: e.tensor_tensor(out=A_[0:nq, c0:c1], in0=P_[0:nq, c0:c1], in1=P_[0:nq, c0 + 1:c1 + 1],
                                                       op=ALU.subtract), reads=[BPb[u3]], writes=[BA[u]])
                S.op("act", lambda e: e.activation(out=U.car_ap[0:nq, :], in_=P_[0:nq, c1:c1 + 1], func=AF.Copy),
                     reads=[BPb[u3]], writes=[Bcar])

            def st_tr(U):
                u = U.idx % 2
                A_ = A_sb[u]
                nq = U.nq
                psb16 = PS[4 + u][:, :].bitcast(BF16)
                for bi, (off, nkb, v_ap) in enumerate(U.blocks):
                    S.op("pe", lambda e, bi=bi, off=off, nkb=nkb: e.transpose(psb16[0:nkb, bi * 128:bi * 128 + nq],
                                                                               A_[0:nq, off:off + nkb], identb[0:nq, 0:nq]),
                         reads=[BA[u], B_const], writes=[PSB[4 + u]])

            def st_cp(U):
                u = U.idx % 2
                nq = U.nq
                nb = len(U.blocks)
                psb16 = PS[4 + u][:, :].bitcast(BF16)
                S.op("act", lambda e: e.activation(out=AT_sb[u][:, 0:nb, 0:nq],
                                                   in_=psb16[:, 0:nb * 128].rearrange("p (t c) -> p t c", t=nb)[:, :, 0:nq],
                                                   func=AF.Copy), reads=[PSB[4 + u]], writes=[BAT[u]])

            def st_av(U):
                u = U.idx % 2
                nq = U.nq
                nb = len(U.blocks)
                for bi, (off, nkb, v_ap) in enumerate(U.blocks):
                    S.op("pe", lambda e, bi=bi, nkb=nkb, v_ap=v_ap: e.matmul(PS[6 + u][:, 0:nq], lhsT=v_ap, rhs=AT_sb[u][0:nkb, bi, 0:nq],
                                                                             start=(bi == 0), stop=(bi == nb - 1)),
                         reads=[BAT[u], U.bv], writes=[PSB[6 + u]])

            def st_acc(U):
                u = U.idx % 2
                nq = U.nq
                if U.first:
                    S.op("dve", lambda e: e.tensor_copy(out=U.acc_ap, in_=PS[6 + u][:, 0:nq]), reads=[PSB[6 + u]], writes=[Bacc])
                else:
                    S.op("dve", lambda e: e.tensor_tensor(out=U.acc_ap, in0=PS[6 + u][:, 0:nq], in1=U.acc_ap, op=ALU.add),
                         reads=[PSB[6 + u], Bacc], writes=[Bacc])

            def pipeline(units, stages):
                n = len(units)
                K = len(stages)
                for i, U in enumerate(units):
                    U.idx = i
                for t in range(n + K - 1):
                    for k in range(K):
                        i = t - k
                        if 0 <= i < n:
                            U = units[i]
                            if k == 0 and U.pre is not None:
                                U.pre()
                            stages[k](U)
                            if k == K - 1 and U.post is not None:
                                U.post()

            unitsB = []
            ldc = 0
            for piece in range(2):
                if piece == 0:
                    klist = [(0, None, True)] + [(2048 + 1024 * s_, 1 + i, False) for i, s_ in enumerate([2, 1, 0])]
                else:
                    klist = [(1024, None, True)] + [(2048 + 1024 * s_, 4 + i, False) for i, s_ in enumerate([6, 5, 4, 3, 2, 1, 0])]
                for ui, (kidx, fcol, diag) in enumerate(klist):
                    li = ldc % 2
                    ldc += 1

                    def pre(piece=piece, ui=ui, li=li, kidx=kidx):
                        if ui == 0:
                            S.dma("sp", semqb, [(qb, QbT[:, :, piece * 1024:(piece + 1) * 1024].rearrange("h p c -> p h c"))],
                                  reads=[B_scr], writes=[Bqb])
                        S.dma("sp", semkB[li], [(ktB[li], KbT[:, :, kidx:kidx + 1024].rearrange("h p c -> p h c"))],
                              reads=[B_scr], writes=[BktB[li]])
                        S.dma("sp", semvB[li], [(vtB[li], VbS[kidx // 128:kidx // 128 + 8].rearrange("t p c -> p t c"))],
                              reads=[B_scr], writes=[BvtB[li]])
                    firstU = True
                    for h in range(NH):
                        for qt_ in range(8):
                            kb0 = qt_ if diag else 0
                            blocks = [(128 * b_, 128, vtB[li][:, b_, h * 128:(h + 1) * 128]) for b_ in range(kb0, 8)]
                            U = mk_unit(128, qb[:, h, qt_ * 128:(qt_ + 1) * 128], Bqb, ktB[li][:, h, :], BktB[li], kb0 * 128, 1024, blocks,
                                        BvtB[li], diag, None if fcol is None else flg[:, fcol:fcol + 1],
                                        carry[:, h * 8 + qt_:h * 8 + qt_ + 1], acc[:, h, qt_ * 128:(qt_ + 1) * 128], ui == 0)
                            if firstU:
                                U.pre = pre
                                firstU = False
                            unitsB.append(U)

                def post(piece=piece):
                    S.op("act", lambda e: e.activation(out=obst, in_=acc, func=AF.Copy), reads=[Bacc], writes=[Bobst])
                    S.dma("sp", semob, [(ObT[:, :, piece * 1024:(piece + 1) * 1024].rearrange("h p c -> p h c"), obst)],
                          reads=[Bobst], writes=[B_ObT])
                unitsB[-1].post = post
            kS0 = abf(NH * NS).rearrange("p (h c) -> p h c", h=NH)
            vS0b = abf(1024)
            BkS0 = Buf("kS0")
            semkS0 = S.newsem()
            semck = S.newsem()
            semcvb = [S.newsem(), S.newsem()]
            obs2 = abf(NH * NS).rearrange("p (h c) -> p h c", h=NH)
            Bobs2 = Buf("obs2")

            def pre_s0():
                S.dma("sp", semqb, [(qb[:, :, 0:NS], QbT[:, :, 2048:2080].rearrange("h p c -> p h c"))], reads=[B_scr], writes=[Bqb])
                S.dma("sp", semkS0, [(kS0, KbT[:, :, 9216:9248].rearrange("h p c -> p h c")), (vS0b[0:NS, :], VbS[72, 0:NS, :])],
                      reads=[B_scr], writes=[BkS0])
            for h in range(NH):
                U = mk_unit(NS, qb[:, h, 0:NS], Bqb, kS0[:, h, :], BkS0, 0, NS, [(0, NS, vS0b[0:NS, h * 128:(h + 1) * 128])], BkS0, True, None,
                            carry[:, h:h + 1], acc[:, h, 0:NS], True)
                if h == 0:
                    U.pre = pre_s0
                unitsB.append(U)
            for half in range(2):
                li = half

                def pre_c(half=half, li=li):
                    S.dma("pool", semck, [(obst, cb_k[half * 1024:(half + 1) * 1024, :].rearrange("(t p) c -> p t c", p=128))],
                          writes=[Bobst])
                    S.dma("pool", semcvb[li], [(vtB[li], cb_v[half * 1024:(half + 1) * 1024, :].rearrange("(t p) c -> p t c", p=128))],
                          writes=[BvtB[li]])
                    for h in range(NH):
                        for g2 in range(2):
                            pi = (h * 2 + g2) % 2
                            psb16 = PS[pi][:, :].bitcast(BF16)
                            for t in range(4):
                                tt = g2 * 4 + t
                                S.op("pe", lambda e, psb16=psb16, h=h, t=t, tt=tt: e.transpose(
                                    psb16[:, t * 128:(t + 1) * 128], obst[:, tt, h * 128:(h + 1) * 128], identb[:, :]),
                                    reads=[Bobst, B_const], writes=[PSB[pi]])
                            S.op("dve", lambda e, psb16=psb16, h=h, g2=g2, li=li: e.tensor_copy(
                                out=ktB[li][:, h, g2 * 512:(g2 + 1) * 512], in_=psb16[:, 0:512]), reads=[PSB[pi]], writes=[BktB[li]])
                for h in range(NH):
                    blocks = [(128 * b_, 128, vtB[li][:, b_, h * 128:(h + 1) * 128]) for b_ in range(8)]
                    U = mk_unit(NS, qb[:, h, 0:NS], Bqb, ktB[li][:, h, :], BktB[li], 0, 1024, blocks, BvtB[li], False, None,
                                carry[:, h:h + 1], acc[:, h, 0:NS], False)
                    if h == 0:
                        U.pre = pre_c
                    unitsB.append(U)

            def post_s():
                S.op("act", lambda e: e.activation(out=obs2, in_=acc[:, :, 0:NS], func=AF.Copy), reads=[Bacc], writes=[Bobs2])
                S.dma("sp", semob, [(ObT[:, :, 2048:2080].rearrange("h p c -> p h c"), obs2)], reads=[Bobs2], writes=[B_ObT])
            unitsB[-1].post = post_s
            pipeline(unitsB, [st_z, st_sig2, st_scan, st_sub, st_tr, st_cp, st_av, st_acc])
            S.barrier()
            areset()
            _ph[0] += 1
            if _ph[0] >= STOP:
                break

            wa = abf(NH * D).rearrange("p (k c) -> p k c", k=NH)
            wbm = abf(NH * D).rearrange("p (k c) -> p k c", k=NH)
            Bwab = Buf("wab")
            semwab = S.newsem()
            S.dma("pool", semwab, [(wa[:, :, 0:1024], wview(w_a_out, 0, 1024)), (wa[:, :, 1024:2048], wview(w_a_out, 1024, 1024)),
                                   (wbm[:, :, 0:1024], wview(w_b_out, 0, 1024)), (wbm[:, :, 1024:2048], wview(w_b_out, 1024, 1024))],
                  writes=[Bwab])
            oat = [abf(NH * 512).rearrange("p (h c) -> p h c", h=NH) for _ in range(2)]
            obt = [abf(NH * 512).rearrange("p (h c) -> p h c", h=NH) for _ in range(2)]
            Boab = [Buf("oab0"), Buf("oab1")]
            semoab = [S.newsem(), S.newsem()]
            ga8_ = [af32(4 * 512).rearrange("p (h c) -> p h c", h=4) for _ in range(2)]
            gb8_ = [af32(4 * 512).rearrange("p (h c) -> p h c", h=4) for _ in range(2)]
            Bg8_ = [Buf("g8_0"), Buf("g8_1")]
            semg8_ = [S.newsem(), S.newsem()]
            t1 = [af32(512) for _ in range(2)]
            t2 = [af32(512) for _ in range(2)]
            Bt12 = [Buf("t12_0"), Buf("t12_1")]
            mst = abf(KC * 512).rearrange("p (k c) -> p k c", k=KC)
            Bmst = Buf("mst")
            semmst = S.newsem()
            B_MT = [Buf("MT%d" % i) for i in range(5)]
            cc5 = 0

            def load_oab(b_):
                blk_, ntk_, r0_ = own_blocks[b_]
                S.dma("sp", semoab[b_ % 2], [(oat[b_ % 2][:, :, 0:ntk_], OaT[:, :, r0_:r0_ + ntk_].rearrange("h p c -> p h c")),
                                             (obt[b_ % 2][:, :, 0:ntk_], ObT[:, :, r0_:r0_ + ntk_].rearrange("h p c -> p h c"))],
                      reads=[B_OaT, B_ObT], writes=[Boab[b_ % 2]])

            def load_g8(n_):
                blk_, ntk_, r0_ = own_blocks[n_ // 4]
                hf_ = n_ % 4
                S.dma("sp", semg8_[n_ % 2], [(ga8_[n_ % 2][:, :, 0:ntk_], GA[4 * hf_:4 * hf_ + 4, :, r0_:r0_ + ntk_].rearrange("h p c -> p h c")),
                                             (gb8_[n_ % 2][:, :, 0:ntk_], GB[4 * hf_:4 * hf_ + 4, :, r0_:r0_ + ntk_].rearrange("h p c -> p h c"))],
                      reads=[B_scr], writes=[Bg8_[n_ % 2]])
            load_oab(0)
            load_g8(0)
            for bi5, (blk, ntk, r0) in enumerate(own_blocks):
                li = bi5 % 2
                if bi5 + 1 < 5:
                    load_oab(bi5 + 1)
                for half in range(4):
                    n5 = 4 * bi5 + half
                    if n5 + 1 < 20:
                        load_g8(n5 + 1)
                    ga8, gb8, Bg8 = ga8_[n5 % 2], gb8_[n5 % 2], Bg8_[n5 % 2]
                    for cc in range(4):
                        c = 4 * half + cc
                        u = cc5 % 2
                        cc5 += 1
                        pa, pb_ = 2 * u, 2 * u + 1
                        for kc in range(NH):
                            S.op("pe", lambda e, pa=pa, kc=kc, c=c, li=li, ntk=ntk: e.matmul(
                                PS[pa][:, 0:ntk], lhsT=wa[:, kc, c * 128:(c + 1) * 128], rhs=oat[li][:, kc, 0:ntk],
                                start=(kc == 0), stop=(kc == NH - 1)), reads=[Bwab, Boab[li]], writes=[PSB[pa]])
                        for kc in range(NH):
                            S.op("pe", lambda e, pb_=pb_, kc=kc, c=c, li=li, ntk=ntk: e.matmul(
                                PS[pb_][:, 0:ntk], lhsT=wbm[:, kc, c * 128:(c + 1) * 128], rhs=obt[li][:, kc, 0:ntk],
                                start=(kc == 0), stop=(kc == NH - 1)), reads=[Bwab, Boab[li]], writes=[PSB[pb_]])
                        S.op("dve", lambda e, pa=pa, u=u, cc=cc, ntk=ntk, ga8=ga8: e.tensor_tensor(
                            out=t1[u][:, 0:ntk], in0=PS[pa][:, 0:ntk], in1=ga8[:, cc, 0:ntk], op=ALU.mult),
                            reads=[PSB[pa], Bg8], writes=[Bt12[u]])
                        S.op("dve", lambda e, pb_=pb_, u=u, cc=cc, ntk=ntk, gb8=gb8: e.tensor_tensor(
                            out=t2[u][:, 0:ntk], in0=PS[pb_][:, 0:ntk], in1=gb8[:, cc, 0:ntk], op=ALU.mult),
                            reads=[PSB[pb_], Bg8], writes=[Bt12[u]])
                        S.op("pool", lambda e, u=u, c=c, ntk=ntk: e.tensor_tensor(
                            out=mst[:, c, 0:ntk], in0=t1[u][:, 0:ntk], in1=t2[u][:, 0:ntk], op=ALU.add),
                            reads=[Bt12[u]], writes=[Bmst])
                S.dma("sp", semmst, [(MT[bi5], mst)], reads=[Bmst], writes=[B_MT[bi5]])
            S.barrier()
            areset()
            _ph[0] += 1
            if _ph[0] >= STOP:
                break

            wos = [abf(KC * 512).rearrange("p (k c) -> p k c", k=KC) for _ in range(2)]
            Bwos = [Buf("wos0"), Buf("wos1")]
            semwos = [S.newsem(), S.newsem()]
            mtb = [abf(KC * 512).rearrange("p (k c) -> p k c", k=KC) for _ in range(2)]
            Bmtb = [Buf("mtb0"), Buf("mtb1")]
            semmtb = [S.newsem(), S.newsem()]
            xc5 = [af32(512) for _ in range(2)]
            Bxc5 = [Buf("xc5_0"), Buf("xc5_1")]
            semxc5 = [S.newsem(), S.newsem()]
            semx1 = [S.newsem(), S.newsem()]
            t5 = [af32(512) for _ in range(2)]
            Bt5 = [Buf("t5_0"), Buf("t5_1")]
            gtb = [af32(D) for _ in range(2)]
            Bgtb = Buf("gtb")
            semgtb = S.newsem()
            S.dma("sp", semgtb, [(gtb[v], bass.AP(adaD.tensor, v * 6 * D + 2 * D, [[0, 128], [1, D]])) for v in range(2)],
                  reads=[B_adaD], writes=[Bgtb])
            B_X1 = [Buf("X1_%d" % i) for i in range(17)]

            def load_w5(s_):
                S.dma("pool", semwos[s_ % 2], [(wos[s_ % 2], wview(w_o, s_ * 512, 512))], writes=[Bwos[s_ % 2]])

            def load_m5(n_):
                S.dma("sp", semmtb[n_ % 2], [(mtb[n_ % 2], MT[n_ % 5])], reads=[B_MT[n_ % 5]], writes=[Bmtb[n_ % 2]])
            load_w5(0)
            load_m5(0)
            l5 = 0
            c5 = 0
            for s5 in range(4):
                wi = s5 % 2
                if s5 + 1 < 4:
                    load_w5(s5 + 1)
                tix = 0
                for bi5, (blk, ntk, r0) in enumerate(own_blocks):
                    li = l5 % 2
                    if l5 + 1 < 20:
                        load_m5(l5 + 1)
                    l5 += 1
                    v = 0 if blk < NBLK else 1
                    nt = 4 if ntk == 512 else 1
                    for j in range(nt):
                        ntok = 128 if ntk == 512 else NS
                        u = c5 % 2
                        pi = c5 % 4
                        c5 += 1
                        src = xin[blk, j * 128:(j + 1) * 128, s5 * 512:(s5 + 1) * 512] if blk < NBLK else xs_in[:, s5 * 512:(s5 + 1) * 512]
                        S.dma("sp", semxc5[u], [(xc5[u][0:ntok, :], src)], writes=[Bxc5[u]])
                        for kc in range(KC):
                            S.op("pe", lambda e, pi=pi, kc=kc, li=li, j=j, ntok=ntok, wi=wi: e.matmul(
                                PS[pi][0:ntok, :], lhsT=mtb[li][:, kc, j * 128:j * 128 + ntok], rhs=wos[wi][:, kc, :],
                                start=(kc == 0), stop=(kc == KC - 1)), reads=[Bwos[wi], Bmtb[li]], writes=[PSB[pi]])
                        S.op("dve", lambda e, pi=pi, u=u, v=v, ntok=ntok, s5=s5: e.tensor_tensor(
                            out=t5[u][0:ntok, :], in0=PS[pi][0:ntok, :], in1=gtb[v][0:ntok, s5 * 512:(s5 + 1) * 512], op=ALU.mult),
                            reads=[PSB[pi], Bgtb], writes=[Bt5[u]])
                        S.op("pool", lambda e, u=u, ntok=ntok: e.tensor_tensor(
                            out=xc5[u][0:ntok, :], in0=xc5[u][0:ntok, :], in1=t5[u][0:ntok, :], op=ALU.add),
                            reads=[Bt5[u], Bxc5[u]], writes=[Bxc5[u]])
                        S.dma("sp", semx1[u], [(X1[tix, 0:ntok, s5 * 512:(s5 + 1) * 512], xc5[u][0:ntok, :])],
                              reads=[Bxc5[u]], writes=[B_X1[tix]])
                        tix += 1
            S.barrier()
            areset()
            B_H2T = [Buf("H2T%d" % i) for i in range(5)]
            tiles5 = []
            tix = 0
            for bi5, (blk, ntk, r0) in enumerate(own_blocks):
                nt = 4 if ntk == 512 else 1
                for j in range(nt):
                    ntok = 128 if ntk == 512 else NS
                    tiles5.append((X1[tix, 0:ntok, :], [B_X1[tix]], ntok, 0 if blk < NBLK else 1, bi5, j, j == nt - 1))
                    tix += 1

            def store5(bi5, st, bst, sem):
                S.dma("sp", sem, [(H2T[bi5], st)], reads=[bst], writes=[B_H2T[bi5]])
            norm_pass(tiles5, 4, 3, 1, store5, None)
            S.barrier()
            areset()
            _ph[0] += 1
            if _ph[0] >= STOP:
                break

            wgu = [abf(KC * 1024).rearrange("p (k c) -> p k c", k=KC) for _ in range(2)]
            Bwgu = [Buf("wgu0"), Buf("wgu1")]
            semwgu = [S.newsem(), S.newsem()]
            h2b = [abf(KC * 512).rearrange("p (k c) -> p k c", k=KC) for _ in range(2)]
            Bh2b = [Buf("h2b0"), Buf("h2b1")]
            semh2b = [S.newsem(), S.newsem()]
            sg = [af32(512) for _ in range(2)]
            Bsg = [Buf("sg0"), Buf("sg1")]
            ast = [abf(4 * 512).rearrange("p (k c) -> p k c", k=4) for _ in range(2)]
            Bast = [Buf("ast0"), Buf("ast1")]
            semast = [S.newsem(), S.newsem()]
            B_ActT = Buf("ActT")
            c6 = 0
            l6 = 0
            a6 = 0
            def load_w6(g_):
                S.dma("pool", semwgu[g_ % 2], [(wgu[g_ % 2][:, :, 0:512], wview(w_gu, g_ * 512, 512)),
                                               (wgu[g_ % 2][:, :, 512:1024], wview(w_gu, DFF + g_ * 512, 512))], writes=[Bwgu[g_ % 2]])

            def load_h6(n_):
                S.dma("sp", semh2b[n_ % 2], [(h2b[n_ % 2], H2T[n_ % 5])], reads=[B_H2T[n_ % 5]], writes=[Bh2b[n_ % 2]])
            load_w6(0)
            load_h6(0)
            for g in range(11):
                wi = g % 2
                if g + 1 < 11:
                    load_w6(g + 1)
                for bi5, (blk, ntk, r0) in enumerate(own_blocks):
                    li = l6 % 2
                    if l6 + 1 < 55:
                        load_h6(l6 + 1)
                    l6 += 1
                    ai = a6 % 2
                    a6 += 1
                    for cc in range(4):
                        u = c6 % 2
                        c6 += 1
                        pg, pu = 2 * u, 2 * u + 1
                        for kc in range(KC):
                            S.op("pe", lambda e, pg=pg, kc=kc, cc=cc, wi=wi, li=li, ntk=ntk: e.matmul(
                                PS[pg][:, 0:ntk], lhsT=wgu[wi][:, kc, cc * 128:(cc + 1) * 128], rhs=h2b[li][:, kc, 0:ntk],
                                start=(kc == 0), stop=(kc == KC - 1)), reads=[Bwgu[wi], Bh2b[li]], writes=[PSB[pg]])
                        for kc in range(KC):
                            S.op("pe", lambda e, pu=pu, kc=kc, cc=cc, wi=wi, li=li, ntk=ntk: e.matmul(
                                PS[pu][:, 0:ntk], lhsT=wgu[wi][:, kc, 512 + cc * 128:512 + (cc + 1) * 128], rhs=h2b[li][:, kc, 0:ntk],
                                start=(kc == 0), stop=(kc == KC - 1)), reads=[Bwgu[wi], Bh2b[li]], writes=[PSB[pu]])
                        S.op("act", lambda e, pg=pg, u=u, ntk=ntk: e.activation(out=sg[u][:, 0:ntk], in_=PS[pg][:, 0:ntk], func=AF.Silu),
                             reads=[PSB[pg]], writes=[Bsg[u]])
                        S.op("dve", lambda e, pu=pu, u=u, ai=ai, cc=cc, ntk=ntk: e.tensor_tensor(
                            out=ast[ai][:, cc, 0:ntk], in0=PS[pu][:, 0:ntk], in1=sg[u][:, 0:ntk], op=ALU.mult),
                            reads=[PSB[pu], Bsg[u]], writes=[Bast[ai]])
                    S.dma("sp", semast[ai], [(ActT[2 * bi5 + hf_, :, 4 * g:4 * g + 4, 0:min(256, ntk - 256 * hf_)],
                                              ast[ai][:, :, 256 * hf_:256 * hf_ + min(256, ntk - 256 * hf_)])
                                             for hf_ in range(2) if ntk > 256 * hf_],
                          reads=[Bast[ai]], writes=[B_ActT])
            S.barrier()
            areset()
            _ph[0] += 1
            if _ph[0] >= STOP:
                break

            NSL = 4
            SW = D // NSL
            wd = [abf(FC * SW).rearrange("p (k c) -> p k c", k=FC) for _ in range(2)]
            Bwd = [Buf("wd0"), Buf("wd1")]
            semwd = [S.newsem(), S.newsem()]
            actb = [abf(FC * 256).rearrange("p (k c) -> p k c", k=FC) for _ in range(2)]
            Bactb = [Buf("actb0"), Buf("actb1")]
            semactb = [S.newsem(), S.newsem()]
            x1c = [af32(SW) for _ in range(2)]
            Bx1c = [Buf("x1c0"), Buf("x1c1")]
            semx1c = [S.newsem(), S.newsem()]
            semx2 = [S.newsem(), S.newsem()]
            t7 = [af32(SW) for _ in range(2)]
            Bt7 = [Buf("t7_0"), Buf("t7_1")]
            gt2b = [af32(D) for _ in range(2)]
            Bgt2 = Buf("gt2")
            semgt2 = S.newsem()
            S.dma("sp", semgt2, [(gt2b[v], bass.AP(adaD.tensor, v * 6 * D + 5 * D, [[0, 128], [1, D]])) for v in range(2)],
                  reads=[B_adaD], writes=[Bgt2])
            B_X2 = [Buf("X2_%d" % i) for i in range(17)]
            hbl = []
            for bi5 in range(4):
                for half in range(2):
                    hbl.append((2 * bi5 + half, 256, bi5 * 4 + half * 2, 2, 128, 0))
            hbl.append((8, NS, 16, 1, NS, 1))
            NHB = len(hbl)

            def load_w7(s_):
                S.dma("pool", semwd[s_ % 2], [(wd[s_ % 2][:, 0:22, :], wview(w_dn, s_ * SW, SW)[:, 0:22, :]),
                                              (wd[s_ % 2][:, 22:44, :], wview(w_dn, s_ * SW, SW)[:, 22:44, :])], writes=[Bwd[s_ % 2]])

            def load_a7(n_):
                ai_, w_, _, _, _, _ = hbl[n_ % NHB]
                S.dma("sp", semactb[n_ % 2], [(actb[n_ % 2][:, :, 0:w_], ActT[ai_, :, :, 0:w_])], reads=[B_ActT], writes=[Bactb[n_ % 2]])
            load_w7(0)
            load_a7(0)
            l7 = 0
            c7 = 0
            for s7 in range(NSL):
                wi = s7 % 2
                if s7 + 1 < NSL:
                    load_w7(s7 + 1)
                for (ai_, w_, tix0, ntl, ntok, v) in hbl:
                    li = l7 % 2
                    if l7 + 1 < NSL * NHB:
                        load_a7(l7 + 1)
                    l7 += 1
                    for j in range(ntl):
                        tix = tix0 + j
                        u = c7 % 2
                        pi = c7 % 4
                        c7 += 1
                        S.dma("sp", semx1c[u], [(x1c[u][0:ntok, :], X1[tix, 0:ntok, s7 * SW:(s7 + 1) * SW])],
                              reads=[B_X1[tix]], writes=[Bx1c[u]])
                        for kc in range(FC):
                            S.op("pe", lambda e, pi=pi, kc=kc, li=li, j=j, ntok=ntok, wi=wi: e.matmul(
                                PS[pi][0:ntok, 0:SW], lhsT=actb[li][:, kc, j * 128:j * 128 + ntok], rhs=wd[wi][:, kc, :],
                                start=(kc == 0), stop=(kc == FC - 1)), reads=[Bwd[wi], Bactb[li]], writes=[PSB[pi]])
                        S.op("dve", lambda e, pi=pi, u=u, v=v, ntok=ntok, s7=s7: e.tensor_tensor(
                            out=t7[u][0:ntok, :], in0=PS[pi][0:ntok, 0:SW], in1=gt2b[v][0:ntok, s7 * SW:(s7 + 1) * SW], op=ALU.mult),
                            reads=[PSB[pi], Bgt2], writes=[Bt7[u]])
                        S.op("pool", lambda e, u=u, ntok=ntok: e.tensor_tensor(
                            out=x1c[u][0:ntok, :], in0=x1c[u][0:ntok, :], in1=t7[u][0:ntok, :], op=ALU.add),
                            reads=[Bt7[u], Bx1c[u]], writes=[Bx1c[u]])
                        S.dma("sp", semx2[u], [(X2[tix, 0:ntok, s7 * SW:(s7 + 1) * SW], x1c[u][0:ntok, :])],
                              reads=[Bx1c[u]], writes=[B_X2[tix]])
            S.barrier()
            areset()
            _ph[0] += 1
            if _ph[0] >= STOP:
                break

            gfb = af32(D)
            Bgfb = Buf("gfb")
            semgfb = S.newsem()
            S.dma("sp", semgfb, [(gfb, bass.AP(gfin.tensor, 0, [[0, 128], [1, D]]))], writes=[Bgfb])
            x8 = [af32(D) for _ in range(2)]
            Bx8 = [Buf("x8_0"), Buf("x8_1")]
            semx8 = [S.newsem(), S.newsem()]
            y8 = [af32(D) for _ in range(2)]
            By8 = [Buf("y8_0"), Buf("y8_1")]
            semy8 = [S.newsem(), S.newsem()]
            Bsm8 = [Buf("sm8_0"), Buf("sm8_1")]
            for tix in range(17):
                u = tix % 2
                ntok = 128 if tix < 16 else NS
                S.dma("sp", semx8[u], [(x8[u][0:ntok, :], X2[tix, 0:ntok, :])], reads=[B_X2[tix]], writes=[Bx8[u]])
                ssq = small[:, 48 + 2 * u:49 + 2 * u]
                rstd = small[:, 49 + 2 * u:50 + 2 * u]
                S.op("act", lambda e, u=u, ntok=ntok, ssq=ssq: e.activation(out=y8[u][0:ntok, :], in_=x8[u][0:ntok, :], func=AF.Square,
                                                                            accum_out=ssq[0:ntok, :]), reads=[Bx8[u]], writes=[By8[u], Bsm8[u]])
                S.op("act", lambda e, ntok=ntok, ssq=ssq, rstd=rstd: e.activation(out=rstd[0:ntok, :], in_=ssq[0:ntok, :], func=AF.Sqrt,
                                                                                  scale=1.0 / D, bias=small[0:ntok, 63:64]),
                     reads=[Bsm8[u], B_const], writes=[Bsm8[u]])
                S.op("dve", lambda e, ntok=ntok, rstd=rstd: e.reciprocal(out=rstd[0:ntok, :], in_=rstd[0:ntok, :]),
                     reads=[Bsm8[u]], writes=[Bsm8[u]])
                S.op("act", lambda e, u=u, ntok=ntok, rstd=rstd: e.activation(out=y8[u][0:ntok, :], in_=x8[u][0:ntok, :], func=AF.Identity,
                                                                              scale=rstd[0:ntok, :]), reads=[Bx8[u], Bsm8[u]], writes=[By8[u]])
                S.op("dve", lambda e, u=u, ntok=ntok: e.tensor_tensor(out=y8[u][0:ntok, :], in0=y8[u][0:ntok, :], in1=gfb[0:ntok, :],
                                                                      op=ALU.mult), reads=[By8[u], Bgfb], writes=[By8[u]])
                S.dma("sp", semy8[u], [(y_o[tix * 128:tix * 128 + ntok, :], y8[u][0:ntok, :])], reads=[By8[u]], writes=[])
            S.barrier()

        with nc.Block() as block:
            @block.tensor
            def _(e):
                S.replay("pe", e)

            @block.scalar
            def _(e):
                S.replay("act", e)

            @block.vector
            def _(e):
                S.replay("dve", e)

            @block.gpsimd
            def _(e):
                S.replay("pool", e)

            @block.sync
            def _(e):
                S.replay("sp", e)
    return nc


_NC_CACHE = {}


def _prep_inputs(x_prompt, x_sample, c_prompt, c_sample, cache_a_k, cache_a_v, cache_b_k, cache_b_v,
                 w_ada, b_ada, g_mix, w_in, rel_bias, w_a_out, w_b_out, w_o, g_ffn, w_gate_up, w_down, g_final):
    f = lambda a: np.ascontiguousarray(np.asarray(a, dtype=np.float32))
    shared = {
        "w_ada": f(w_ada[0]), "b_ada": f(b_ada[0]).reshape(1, -1), "w_in": f(w_in[0]), "relb": f(rel_bias[0]),
        "w_a_out": f(w_a_out[0]), "w_b_out": f(w_b_out[0]), "w_o": f(w_o[0]), "w_gu": f(w_gate_up[0]),
        "w_dn": f(w_down[0]), "gfin": f(g_final).reshape(1, -1),
    }
    gcols = np.stack([f(g_mix[0]).reshape(KC, 128).T, f(g_ffn[0]).reshape(KC, 128).T], axis=1)
    shared["gcols"] = np.ascontiguousarray(gcols)
    shared["grow"] = np.ascontiguousarray(np.stack([f(g_mix[0]), f(g_ffn[0])], axis=0))
    xp = np.asarray(x_prompt, dtype=np.float32)
    xs = np.asarray(x_sample, dtype=np.float32)
    maps = []
    for c in range(8):
        b, r = c // 4, c % 4
        xr = xp[b, ::-1]
        def piece(p):
            return xr[1024 * (7 - p):1024 * (8 - p)]
        def halo(p):
            if p == 0:
                return xr[0:512]
            return xr[1024 * (8 - p):1024 * (8 - p) + 512]
        lo, hi = r, 7 - r
        blocks = [piece(lo)[:512], piece(lo)[512:], piece(hi)[:512], piece(hi)[512:], halo(lo), halo(hi)]
        for s in range(7):
            p = s if s <= 6 - r else 0
            blocks += [piece(p)[:512], piece(p)[512:]]
        xin = np.ascontiguousarray(np.stack(blocks, axis=0))
        cv = np.stack([np.asarray(c_prompt[b], np.float32).reshape(KC, 128).T,
                       np.asarray(c_sample[c], np.float32).reshape(KC, 128).T], axis=2)
        flags = np.zeros((128, 16), np.float32)
        flags[:, 0] = -BIG if r == 0 else 0.0
        for i, s in enumerate([2, 1, 0]):
            flags[:, 1 + i] = 0.0 if s <= r - 1 else BIG
        for i, s in enumerate([6, 5, 4, 3, 2, 1, 0]):
            flags[:, 4 + i] = 0.0 if s <= 6 - r else BIG
        m = dict(shared)
        m.update({
            "xin": xin, "xs": np.ascontiguousarray(xs[c, ::-1]), "cvec": np.ascontiguousarray(cv), "flags": flags,
            "ca_k": np.ascontiguousarray(np.asarray(cache_a_k[0, c], np.float32)[::-1].reshape(512, 1024)),
            "ca_v": np.ascontiguousarray(np.asarray(cache_a_v[0, c], np.float32)[::-1].reshape(512, 1024)),
            "cb_k": np.ascontiguousarray(np.asarray(cache_b_k[0, c], np.float32)[::-1].reshape(2048, 1024)),
            "cb_v": np.ascontiguousarray(np.asarray(cache_b_v[0, c], np.float32)[::-1].reshape(2048, 1024)),
        })
        maps.append(m)
    return maps


def kernel(**inputs):
    maps = _prep_inputs(**inputs)
    if "nc" not in _NC_CACHE:
        _NC_CACHE["nc"] = build_program()
    nc = _NC_CACHE["nc"]
    res = run_bass_kernel_spmd(nc, maps, core_ids=list(range(8)))
    R = res.results
    y_p = np.zeros((2, 8192, D), np.float32)
    y_s = np.zeros((8, NS, D), np.float32)
    ak_p = np.zeros((1, 2, 512, NH, HD), np.float32)
    av_p = np.zeros((1, 2, 512, NH, HD), np.float32)
    bk_p = np.zeros((1, 2, 8192, NH, HD), np.float32)
    bv_p = np.zeros((1, 2, 8192, NH, HD), np.float32)
    ak_s = np.zeros((1, 8, NS, NH, HD), np.float32)
    av_s = np.zeros((1, 8, NS, NH, HD), np.float32)
    bk_s = np.zeros((1, 8, NS, NH, HD), np.float32)
    bv_s = np.zeros((1, 8, NS, NH, HD), np.float32)
    for c in range(8):
        b, r = c // 4, c % 4
        o = R[c]
        for pi, p in enumerate((r, 7 - r)):
            sl = slice(1024 * p, 1024 * (p + 1))
            rows = slice(1024 * pi, 1024 * (pi + 1))
            y_p[b, sl] = o["y"][rows][::-1]
            bk_p[0, b, sl] = o["kb_o"][rows][::-1].reshape(1024, NH, HD)
            bv_p[0, b, sl] = o["vb_o"][rows][::-1].reshape(1024, NH, HD)
        if r == 0:
            ak_p[0, b] = o["ka_o"][0:512][::-1].reshape(512, NH, HD)
            av_p[0, b] = o["va_o"][0:512][::-1].reshape(512, NH, HD)
        y_s[c] = o["y"][2048:2080][::-1]
        ak_s[0, c] = o["ka_o"][512:544][::-1].reshape(NS, NH, HD)
        av_s[0, c] = o["va_o"][512:544][::-1].reshape(NS, NH, HD)
        bk_s[0, c] = o["kb_o"][2048:2080][::-1].reshape(NS, NH, HD)
        bv_s[0, c] = o["vb_o"][2048:2080][::-1].reshape(NS, NH, HD)
    return (y_p, y_s, ak_p, av_p, bk_p, bv_p, ak_s, av_s, bk_s, bv_s)
```
